# Optimizing a Trainium2 kernel written in Bass

```python
import math
import jax, jax.numpy as jnp
from jax import lax
import numpy as np

D_MODEL = 1024
BATCH = 8
SEQ = 8192
DEPTH = 4
DEC_BATCH = 16
DEC_SEQ = 4096
PAST_LEN = 128

N_HEADS = 8
QK_NOPE_DIM = 128
QK_ROPE_DIM = 64
V_DIM = 128
Q_LORA = 384
KV_LORA = 256
MLA_WIDTH = N_HEADS * V_DIM
MLA_IN = Q_LORA + KV_LORA + QK_ROPE_DIM + MLA_WIDTH
ROPE_THETA = 10000.0
Q_BLOCK = 128
ATTN_SCALE = 1.0 / math.sqrt(QK_NOPE_DIM + QK_ROPE_DIM)
S5_WIDTH = D_MODEL
S5_GROUP = 16
S5_GROUPS = S5_WIDTH // S5_GROUP
S5_STATE = 64
DT_MIN = 0.001
DT_MAX = 0.1
N_MLA = (DEPTH + 1) // 2
N_S5 = DEPTH // 2
ALPHA = (2 * DEPTH) ** 0.25
BETA = (8 * DEPTH) ** -0.25
LN_EPS = 1e-5
RMS_EPS = 1e-6

kernel_name = 'hybrid_mla_s5_encoder'


def _layer_norm(x, g, b):
    xf = x.astype(jnp.float32)
    mu = xf.mean(-1, keepdims=True)
    var = jnp.square(xf - mu).mean(-1, keepdims=True)
    y = (xf - mu) * lax.rsqrt(var + LN_EPS) * g.astype(jnp.float32) + b.astype(jnp.float32)
    return y.astype(x.dtype)


def _rms_norm(x, g):
    xf = x.astype(jnp.float32)
    y = xf * lax.rsqrt(jnp.square(xf).mean(-1, keepdims=True) + RMS_EPS) * g.astype(jnp.float32)
    return y.astype(x.dtype)


def _rope_tables(length, dtype):
    inv = ROPE_THETA ** (-jnp.arange(0, QK_ROPE_DIM, 2, dtype=jnp.float32) / QK_ROPE_DIM)
    ang = jnp.arange(length, dtype=jnp.float32)[:, None] * inv[None, :]
    return jnp.cos(ang).astype(dtype), jnp.sin(ang).astype(dtype)


def _rope(t, cos, sin):
    half = QK_ROPE_DIM // 2
    t1, t2 = t[..., :half], t[..., half:]
    return jnp.concatenate([t1 * cos - t2 * sin, t2 * cos + t1 * sin], axis=-1)


def _attend(q_nope, q_rope, k_nope, k_rope, v):
    B, L, H, _ = q_nope.shape
    nb = L // Q_BLOCK

    def block(args):
        qn, qr = args
        s = (jnp.einsum('bqhd,bkhd->bhqk', qn, k_nope, preferred_element_type=jnp.float32)
             + jnp.einsum('bqhr,bkr->bhqk', qr, k_rope, preferred_element_type=jnp.float32)) * ATTN_SCALE
        p = jax.nn.softmax(s, axis=-1).astype(v.dtype)
        return jnp.einsum('bhqk,bkhd->bqhd', p, v)

    qn = q_nope.reshape(B, nb, Q_BLOCK, H, QK_NOPE_DIM).swapaxes(0, 1)
    qr = q_rope.reshape(B, nb, Q_BLOCK, H, QK_ROPE_DIM).swapaxes(0, 1)
    o = lax.map(block, (qn, qr))
    return o.swapaxes(0, 1).reshape(B, L, H * V_DIM)


def _mla(x, w_in, g_q, w_q_up, g_kv, w_kv_up, w_out):
    B, L, _ = x.shape
    h = x @ w_in
    c_q, c_kv, k_rope, gate = jnp.split(h, [Q_LORA, Q_LORA + KV_LORA, Q_LORA + KV_LORA + QK_ROPE_DIM], axis=-1)
    q = (_rms_norm(c_q, g_q) @ w_q_up).reshape(B, L, N_HEADS, QK_NOPE_DIM + QK_ROPE_DIM)
    q_nope, q_rope = q[..., :QK_NOPE_DIM], q[..., QK_NOPE_DIM:]
    kv = (_rms_norm(c_kv, g_kv) @ w_kv_up).reshape(B, L, N_HEADS, QK_NOPE_DIM + V_DIM)
    k_nope, v = kv[..., :QK_NOPE_DIM], kv[..., QK_NOPE_DIM:]
    cos, sin = _rope_tables(L, x.dtype)
    q_rope = _rope(q_rope, cos[:, None, :], sin[:, None, :])
    k_rope = _rope(k_rope, cos, sin)
    o = _attend(q_nope, q_rope, k_nope, k_rope, v)
    return (o * jax.nn.silu(gate)) @ w_out


def _ssm_combine(left, right):
    a1, b1 = left
    a2, b2 = right
    return a1 * a2, a2 * b1 + b2


def _s5_scan(us, lam_bar, b_bar, c):
    bu = lax.complex(jnp.einsum('lgp,gnp->lgn', us, b_bar.real),
                     jnp.einsum('lgp,gnp->lgn', us, b_bar.imag))
    a = jnp.broadcast_to(lam_bar, bu.shape)
    _, h = lax.associative_scan(_ssm_combine, (a, bu), axis=0)
    return jnp.einsum('gpn,lgn->lgp', c, h).real


def _s5(x, w_in, a_re, a_im, log_step, b_re, b_im, c_re, c_im, d, w_glu, b_glu, w_out):
    B, L, _ = x.shape
    f32 = jnp.float32
    u, gate = jnp.split(x @ w_in, 2, axis=-1)
    lam = lax.complex(a_re.astype(f32), a_im.astype(f32))
    step = jnp.exp(log_step.astype(f32))[..., None]
    lam_bar = jnp.exp(lam * step)
    b = lax.complex(b_re.astype(f32), b_im.astype(f32))
    b_bar = ((lam_bar - 1.0) / lam)[..., None] * b
    c = lax.complex(c_re.astype(f32), c_im.astype(f32))
    uf = u.astype(f32)
    ug = uf.reshape(B, L, S5_GROUPS, S5_GROUP)

    def per_seq(us):
        y_f = _s5_scan(us, lam_bar[0], b_bar[0], c[0])
        y_b = _s5_scan(us[::-1], lam_bar[1], b_bar[1], c[1])[::-1]
        return y_f + y_b

    y = lax.map(per_seq, ug).reshape(B, L, S5_WIDTH) + d.astype(f32) * uf
    y = jax.nn.gelu(y).astype(x.dtype)
    y = y * jax.nn.sigmoid(y @ w_glu + b_glu)
    return (y * jax.nn.silu(gate)) @ w_out


def _trunk(x, mla, s5, ln_g, ln_b):
    for i in range(DEPTH):
        j = i // 2
        if i % 2 == 0:
            y = _mla(x, *[w[j] for w in mla])
        else:
            y = _s5(x, *[w[j] for w in s5])
        x = _layer_norm(ALPHA * x + y, ln_g[i], ln_b[i])
    return x


def setup_inputs(seed: int = 0) -> dict:
    key = jax.random.key(seed)
    ks = jax.random.split(key, 24)
    f32 = jnp.float32
    nrm = lambda k, shape, s: jax.random.normal(k, shape, f32) * s
    G, N, P = S5_GROUPS, S5_STATE, S5_GROUP
    return {
        'x_prompt': nrm(ks[0], (BATCH, SEQ, D_MODEL), 1.0),
        'x_sample': nrm(ks[1], (DEC_BATCH, DEC_SEQ, D_MODEL), 1.0),
        'mla_w_in': nrm(ks[2], (N_MLA, D_MODEL, MLA_IN), D_MODEL ** -0.5),
        'mla_g_q': 1.0 + nrm(ks[3], (N_MLA, Q_LORA), 0.01),
        'mla_w_q_up': nrm(ks[4], (N_MLA, Q_LORA, N_HEADS * (QK_NOPE_DIM + QK_ROPE_DIM)), Q_LORA ** -0.5),
        'mla_g_kv': 1.0 + nrm(ks[5], (N_MLA, KV_LORA), 0.01),
        'mla_w_kv_up': nrm(ks[6], (N_MLA, KV_LORA, N_HEADS * (QK_NOPE_DIM + V_DIM)), KV_LORA ** -0.5),
        'mla_w_out': nrm(ks[7], (N_MLA, MLA_WIDTH, D_MODEL), BETA * MLA_WIDTH ** -0.5),
        's5_w_in': nrm(ks[8], (N_S5, D_MODEL, 2 * S5_WIDTH), D_MODEL ** -0.5),
        's5_a_re': -0.5 + nrm(ks[9], (N_S5, 2, G, N), 0.01),
        's5_a_im': jnp.pi * jnp.arange(N, dtype=f32) + nrm(ks[10], (N_S5, 2, G, N), 0.01),
        's5_log_step': jax.random.uniform(ks[11], (N_S5, 2, G), f32, math.log(DT_MIN), math.log(DT_MAX)),
        's5_b_re': nrm(ks[12], (N_S5, 2, G, N, P), (2 * P) ** -0.5),
        's5_b_im': nrm(ks[13], (N_S5, 2, G, N, P), (2 * P) ** -0.5),
        's5_c_re': nrm(ks[14], (N_S5, 2, G, P, N), (2 * N) ** -0.5),
        's5_c_im': nrm(ks[15], (N_S5, 2, G, P, N), (2 * N) ** -0.5),
        's5_d': nrm(ks[16], (N_S5, S5_WIDTH), 1.0),
        's5_w_glu': nrm(ks[17], (N_S5, S5_WIDTH, S5_WIDTH), S5_WIDTH ** -0.5),
        's5_b_glu': nrm(ks[18], (N_S5, S5_WIDTH), 0.01),
        's5_w_out': nrm(ks[19], (N_S5, S5_WIDTH, D_MODEL), BETA * S5_WIDTH ** -0.5),
        'ln_g': 1.0 + nrm(ks[20], (DEPTH, D_MODEL), 0.01),
        'ln_b': nrm(ks[21], (DEPTH, D_MODEL), 0.01),
    }


def reference(x_prompt, x_sample, mla_w_in, mla_g_q, mla_w_q_up, mla_g_kv, mla_w_kv_up, mla_w_out,
              s5_w_in, s5_a_re, s5_a_im, s5_log_step, s5_b_re, s5_b_im, s5_c_re, s5_c_im, s5_d,
              s5_w_glu, s5_b_glu, s5_w_out, ln_g, ln_b):
    mla = (mla_w_in, mla_g_q, mla_w_q_up, mla_g_kv, mla_w_kv_up, mla_w_out)
    s5 = (s5_w_in, s5_a_re, s5_a_im, s5_log_step, s5_b_re, s5_b_im, s5_c_re, s5_c_im, s5_d,
          s5_w_glu, s5_b_glu, s5_w_out)
    y_prompt = _trunk(x_prompt, mla, s5, ln_g, ln_b)
    y_sample = _trunk(x_sample, mla, s5, ln_g, ln_b)
    return (y_prompt, y_sample)
```

```python
import math
from contextlib import ExitStack

import numpy as np
import concourse.bass as bass
import concourse.mybir as mybir
from concourse.bass_utils import run_bass_kernel_spmd

F32 = mybir.dt.float32
BF16 = mybir.dt.bfloat16
AF = mybir.ActivationFunctionType
ALU = mybir.AluOpType

D = 1024
NH = 8
QLORA, KVLORA, ROPE = 384, 256, 64
MLA_IN = 1728
ATTN_SCALE = 1.0 / math.sqrt(192.0)
DEPTH = 4
ALPHA = (2 * DEPTH) ** 0.25
LN_EPS = 1e-5
RMS_EPS = 1e-6
NCORES = 8
import os as _os
DBG_STOP = int(_os.environ.get('DBG_STOP', '0'))
STQ = _os.environ.get('STQ', 'pool')
ATT_ONES = int(_os.environ.get('ATT_ONES', '0'))
POOL_EVERY = int(_os.environ.get('POOL_EVERY', '3'))
DBG_MASK = int(_os.environ.get('DBG_MASK', '15'))


class Res:
    __slots__ = ("name", "lw", "rd", "excl")

    def __init__(self, name="", excl=False):
        self.name = name
        self.lw = None
        self.rd = {}
        self.excl = excl


class Trk:
    def __init__(self, nc, stack):
        self.nc = nc
        self.stack = stack
        self.E = {"pe": nc.tensor, "act": nc.scalar, "dve": nc.vector, "pool": nc.gpsimd, "sp": nc.sync}
        self.sems, self.cnt, self.pend = {}, {}, {}
        for k in self.E:
            self.sems[k] = stack.enter_context(nc.semaphore("s_" + k))
            self.cnt[k] = 0
            self.pend[k] = False
        self.seen = {k: {} for k in self.E}

    def _need(self, eng, deps):
        for (sk, val) in deps:
            if self.seen[eng].get(sk, 0) >= val:
                continue
            if sk in self.E:
                assert self.cnt[sk] >= val, f"waiting on pending inc of {sk}"
            self.E[eng].wait_ge(self.sems[sk], val)
            self.seen[eng][sk] = val

    def _deps(self, eng, reads, writes):
        deps = []
        for r in reads:
            if r.lw is not None:
                deps.append(r.lw)
            if r.excl:
                for sk, v in r.rd.items():
                    if sk != eng:
                        deps.append((sk, v))
        for w in writes:
            if w.lw is not None and not (w.lw[0] == eng and eng == "pe"):
                deps.append(w.lw)
            for sk, v in w.rd.items():
                if not (sk == eng and eng == "pe"):
                    deps.append((sk, v))
        return deps

    def op(self, eng, fn, reads=(), writes=(), inc=True):
        self._need(eng, self._deps(eng, reads, writes))
        inst = fn(self.E[eng])
        if inc:
            inst.then_inc(self.sems[eng], 1)
            self.cnt[eng] += 1
            val = self.cnt[eng]
            self.pend[eng] = False
        else:
            val = self.cnt[eng] + 1
            self.pend[eng] = True
        for r in reads:
            r.rd[eng] = max(r.rd.get(eng, 0), val)
        for w in writes:
            w.lw = (eng, val)
            w.rd = {}
        return inst

    def dma(self, eng, out, in_, reads=(), writes=(), key=None, **kw):
        self._need(eng, self._deps("dma", reads, writes))
        if key is None:
            key = writes[0].name if writes else reads[0].name
        key = "d_" + key
        if key not in self.sems:
            self.sems[key] = self.stack.enter_context(self.nc.semaphore(key))
            self.cnt[key] = 0
        inst = self.E[eng].dma_start(out=out, in_=in_, **kw)
        inst.then_inc(self.sems[key], 16)
        self.cnt[key] += 16
        val = self.cnt[key]
        for r in reads:
            r.rd[key] = val
        for w in writes:
            w.lw = (key, val)
            w.rd = {}
        return inst

    def barrier(self, engines=("pe", "act", "dve", "pool", "sp")):
        for sk in self.E:
            assert not self.pend[sk], f"pending inc on {sk} at barrier"
        for e in engines:
            self._need(e, [(sk, self.cnt[sk]) for sk in self.sems if self.cnt[sk] > 0 and sk != e])


class Buf:
    def __init__(self, t, name, excl=False):
        self.t = t
        self.r = Res(name, excl)


class K:
    def __init__(self, nc, stack):
        self.nc = nc
        self.trk = Trk(nc, stack)
        self.uid = 0

    def sb(self, st, name, shape, dt):
        self.uid += 1
        nm = f"{name}_{self.uid}"
        if _os.environ.get("DBG_ALLOC"):
            n = 1
            for v in shape[1:]:
                n *= v
            print("ALLOC", nm, shape, n * (2 if dt == BF16 else 4))
        return Buf(st.enter_context(self.nc.sbuf_tensor(nm, list(shape), dt)), name)

    def ps(self, st, name, shape, dt):
        self.uid += 1
        nm = f"{name}_{self.uid}"
        return Buf(st.enter_context(self.nc.psum_tensor(nm, list(shape), dt)), name, excl=True)


def _R(bufs):
    return [b.r for b in bufs]


def load_w_bf16(k, st_bufs, w_dram, dst, nk, ncols, cast_eng=("act", "dve")):
    t = k.trk
    i = 0
    for kc in range(nk):
        for c0 in range(0, ncols, 2048):
            c1 = min(ncols, c0 + 2048)
            sbuf = st_bufs[i % len(st_bufs)]
            t.dma("sp", sbuf.t[:, 0:c1 - c0], w_dram[kc * 128:(kc + 1) * 128, c0:c1], writes=[sbuf.r])
            eng = cast_eng[i % len(cast_eng)]
            if eng == "act":
                t.op("act", lambda e, s=sbuf, kc=kc, c0=c0, c1=c1: e.copy(out=dst.t[:, kc, c0:c1], in_=s.t[:, 0:c1 - c0]),
                     reads=[sbuf.r], writes=[dst.r])
            else:
                t.op("dve", lambda e, s=sbuf, kc=kc, c0=c0, c1=c1: e.tensor_copy(out=dst.t[:, kc, c0:c1], in_=s.t[:, 0:c1 - c0]),
                     reads=[sbuf.r], writes=[dst.r])
            i += 1


def bcast_rows(ap_1d, n):
    return ap_1d.rearrange("(o n) -> o n", o=1).broadcast(0, 128) if hasattr(ap_1d, "broadcast") else None


def mla_layer(k, cst, x_in, x_out, W, seqs, scr):
    nc, t = k.nc, k.trk
    T_total = sum(L for _, L in seqs)
    Lmax = max(L for _, L in seqs)
    ident, ones = cst["ident"], cst["ones"]

    with ExitStack() as ls:
        cos2 = k.sb(ls, "cos2", [128, cst["ntile_rope"], 64], F32)
        sinpm = k.sb(ls, "sinpm", [128, cst["ntile_rope"], 64], F32)
        t.dma("sp", cos2.t[:], cst["c_cos"], writes=[cos2.r])
        t.dma("sp", sinpm.t[:], cst["c_sin"], writes=[sinpm.r])
        ckvT = k.sb(ls, "ckvT", [128, 2, Lmax], BF16)
        krT = k.sb(ls, "krT", [128, Lmax], BF16)
        t.op("pool", lambda e: e.memset(krT.t[64:128, :], 0.0), reads=[], writes=[krT.r])
        stage = [k.sb(ls, f"stage{i}", [128, 2048], F32) for i in range(2)]
        gq = k.sb(ls, "gq", [128, QLORA + KVLORA], F32)
        t.dma("sp", gq.t[:, 0:QLORA], W["g_q"].partition_broadcast(128), writes=[gq.r])
        t.dma("sp", gq.t[:, QLORA:QLORA + KVLORA], W["g_kv"].partition_broadcast(128), writes=[gq.r])

        for (row0, L) in seqs:
            nblk = L // 512
            with ExitStack() as p1:
                w_in = k.sb(p1, "w_in", [128, 8, MLA_IN], BF16)
                load_w_bf16(k, stage, W["w_in"], w_in, 8, MLA_IN)
                xt = k.sb(p1, "xt", [128, 4, D], F32)
                xb = k.sb(p1, "xb", [128, 4, D], BF16)
                xT = k.sb(p1, "xT", [128, 8, 512], BF16)
                lat = k.sb(p1, "lat", [128, 4, 704], BF16)
                sg = [k.sb(p1, f"sg{i}", [128, 8, 512], BF16) for i in range(2)]
                cqo = [k.sb(p1, f"cqo{i}", [128, 3, 512], BF16) for i in range(2)]
                junk = k.sb(p1, "junk", [128, 384], BF16)
                stat = k.sb(p1, "stat", [128, 8], F32)
                rtmp = k.sb(p1, "rtmp", [128, 4, 64], F32)
                rtmp2 = k.sb(p1, "rtmp2", [128, 4, 64], F32)
                psT = [k.ps(p1, f"psT{i}", [128, 2, 512], BF16) for i in range(2)]
                psL = k.ps(p1, "psL", [128, 1024], F32)
                psLT = k.ps(p1, "psLT", [128, 4, 512], BF16)
                psG = [k.ps(p1, f"psG{i}", [128, 512], F32) for i in range(2)]

                for b in range(nblk):
                    if DBG_STOP == 11:
                        break
                    r0 = row0 + b * 512
                    c0 = b * 512
                    g0 = r0
                    t.dma("sp", xt.t[:], x_in[r0:r0 + 512, :].rearrange("(j p) d -> p j d", p=128), writes=[xt.r])
                    t.op("act", lambda e: e.copy(out=xb.t[:, 0:2, :], in_=xt.t[:, 0:2, :]), reads=[xt.r], writes=[xb.r])
                    t.op("act", lambda e: e.copy(out=xb.t[:, 2:4, :], in_=xt.t[:, 2:4, :]), reads=[xt.r], writes=[xb.r])
                    for kp in range(4):
                        pT = psT[kp % 2]
                        for kk in range(2):
                            kc = kp * 2 + kk
                            for j in range(4):
                                last = (kk == 1 and j == 3)
                                t.op("pe", lambda e, kc=kc, kk=kk, j=j, pT=pT: e.transpose(
                                    out=pT.t[:, kk, j * 128:(j + 1) * 128], in_=xb.t[:, j, kc * 128:(kc + 1) * 128],
                                    identity=ident.t[:]), reads=[xb.r, ident.r], writes=[pT.r], inc=last)
                        t.op("dve", lambda e, kp=kp, pT=pT: e.tensor_copy(out=xT.t[:, kp * 2:kp * 2 + 2, :], in_=pT.t[:]),
                             reads=[pT.r], writes=[xT.r])
                    if DBG_STOP == 12:
                        continue
                    sgb = sg[b % 2]

                    def gate_head(h, sgb=sgb):
                        pG = psG[h % 2]
                        for kc in range(8):
                            t.op("pe", lambda e, kc=kc: e.matmul(
                                out=pG.t[:], lhsT=w_in.t[:, kc, 704 + h * 128:704 + (h + 1) * 128], rhs=xT.t[:, kc, :],
                                start=(kc == 0), stop=(kc == 7)), reads=[xT.r, w_in.r], writes=[pG.r], inc=(kc == 7))
                        t.op("act", lambda e: e.activation(out=sgb.t[:, h, :], in_=pG.t[:], func=AF.Silu),
                             reads=[pG.r], writes=[sgb.r])
                    for j in range(4):
                        for kc in range(8):
                            t.op("pe", lambda e, kc=kc, j=j: e.matmul(
                                out=psL.t[:, 0:512], lhsT=xT.t[:, kc, j * 128:(j + 1) * 128], rhs=w_in.t[:, kc, 0:512],
                                start=(kc == 0), stop=(kc == 7)), reads=[xT.r, w_in.r], writes=[psL.r], inc=False)
                            t.op("pe", lambda e, kc=kc, j=j: e.matmul(
                                out=psL.t[:, 512:704], lhsT=xT.t[:, kc, j * 128:(j + 1) * 128], rhs=w_in.t[:, kc, 512:704],
                                start=(kc == 0), stop=(kc == 7)), reads=[xT.r, w_in.r], writes=[psL.r], inc=(kc == 7))
                        t.op("act", lambda e: e.activation(out=junk.t[:, 0:QLORA], in_=psL.t[:, 0:QLORA], func=AF.Square,
                                                           accum_out=stat.t[:, 0:1]), reads=[psL.r], writes=[junk.r, stat.r])
                        t.op("act", lambda e: e.activation(out=junk.t[:, 0:KVLORA], in_=psL.t[:, QLORA:QLORA + KVLORA], func=AF.Square,
                                                           accum_out=stat.t[:, 1:2]), reads=[psL.r], writes=[junk.r, stat.r])
                        t.op("dve", lambda e: e.tensor_scalar(out=stat.t[:, 2:3], in0=stat.t[:, 0:1], scalar1=1.0 / QLORA, scalar2=RMS_EPS,
                                                              op0=ALU.mult, op1=ALU.add), reads=[stat.r], writes=[stat.r])
                        t.op("dve", lambda e: e.tensor_scalar(out=stat.t[:, 3:4], in0=stat.t[:, 1:2], scalar1=1.0 / KVLORA, scalar2=RMS_EPS,
                                                              op0=ALU.mult, op1=ALU.add), reads=[stat.r], writes=[stat.r])
                        t.op("act", lambda e: e.activation(out=stat.t[:, 4:6], in_=stat.t[:, 2:4], func=AF.Sqrt), reads=[stat.r], writes=[stat.r])
                        t.op("dve", lambda e: e.reciprocal(out=stat.t[:, 6:8], in_=stat.t[:, 4:6]), reads=[stat.r], writes=[stat.r])
                        t.op("dve", lambda e, j=j: e.scalar_tensor_tensor(
                            out=lat.t[:, j, 0:QLORA], in0=psL.t[:, 0:QLORA], scalar=stat.t[:, 6:7], in1=gq.t[:, 0:QLORA],
                            op0=ALU.mult, op1=ALU.mult), reads=[psL.r, stat.r, gq.r], writes=[lat.r])
                        t.op("dve", lambda e, j=j: e.scalar_tensor_tensor(
                            out=lat.t[:, j, QLORA:640], in0=psL.t[:, QLORA:640], scalar=stat.t[:, 7:8], in1=gq.t[:, QLORA:640],
                            op0=ALU.mult, op1=ALU.mult), reads=[psL.r, stat.r, gq.r], writes=[lat.r])
                        ti = b * 4 + j
                        t.op("dve", lambda e, j=j, ti=ti: e.tensor_tensor(out=rtmp.t[:, j, :], in0=psL.t[:, 640:704], in1=cos2.t[:, ti, :], op=ALU.mult),
                             reads=[psL.r, cos2.r], writes=[rtmp.r])
                        t.op("dve", lambda e, j=j, ti=ti: e.tensor_tensor(out=rtmp2.t[:, j, 0:32], in0=psL.t[:, 672:704], in1=sinpm.t[:, ti, 0:32], op=ALU.mult),
                             reads=[psL.r, sinpm.r], writes=[rtmp2.r])
                        t.op("dve", lambda e, j=j, ti=ti: e.tensor_tensor(out=rtmp2.t[:, j, 32:64], in0=psL.t[:, 640:672], in1=sinpm.t[:, ti, 32:64], op=ALU.mult),
                             reads=[psL.r, sinpm.r], writes=[rtmp2.r])
                        t.op("dve", lambda e, j=j: e.tensor_tensor(out=lat.t[:, j, 640:704], in0=rtmp.t[:, j, :], in1=rtmp2.t[:, j, :], op=ALU.add),
                             reads=[rtmp.r, rtmp2.r], writes=[lat.r])
                        if DBG_STOP not in (13, 14, 15):
                            gate_head(2 * j)
                            gate_head(2 * j + 1)
                    if DBG_STOP == 13:
                        continue
                    for j in range(4):
                        for c in range(6):
                            w = 128 if c < 5 else 64
                            last = (j == 3 and c == 5)
                            dstp = psLT if c < 4 else psT[0]
                            cc = c if c < 4 else c - 4
                            t.op("pe", lambda e, j=j, c=c, w=w, dstp=dstp, cc=cc: e.transpose(
                                out=dstp.t[0:w, cc, j * 128:(j + 1) * 128], in_=lat.t[:, j, c * 128:c * 128 + w], identity=ident.t[:]),
                                reads=[lat.r, ident.r], writes=[dstp.r], inc=(last or (j == 3 and c == 3)))
                    cq = cqo[b % 2]
                    if DBG_MASK & 1:
                        t.op("dve", lambda e, cq=cq: e.tensor_copy(out=cq.t[:], in_=psLT.t[:, 0:3, :]), reads=[psLT.r], writes=[cq.r])
                    if DBG_MASK & 2:
                        t.op("act", lambda e, c0=c0: e.copy(out=ckvT.t[:, 0, c0:c0 + 512], in_=psLT.t[:, 3, :]), reads=[psLT.r], writes=[])
                    if DBG_MASK & 4:
                        t.op("act", lambda e, c0=c0: e.copy(out=ckvT.t[:, 1, c0:c0 + 512], in_=psT[0].t[:, 0, :]), reads=[psT[0].r], writes=[])
                    if DBG_MASK & 8:
                        t.op("dve", lambda e, c0=c0: e.tensor_copy(out=krT.t[0:64, c0:c0 + 512], in_=psT[0].t[0:64, 1, :]), reads=[psT[0].r], writes=[])
                    if DBG_STOP != 15:
                        t.dma(STQ, scr["CQ"][:, g0:g0 + 512].rearrange("(c p) t -> p c t", p=128), cq.t[:], reads=[cq.r])
                    t.dma(STQ, scr["SG"][:, g0:g0 + 512].rearrange("(h p) t -> p h t", p=128), sgb.t[:], reads=[sgb.r])
            t.barrier()
            if DBG_STOP in (1, 11, 12, 13, 14, 15):
                continue
            with ExitStack() as p2:
                w_q = k.sb(p2, "w_q", [128, 3, 1536], BF16)
                w_kv = k.sb(p2, "w_kv", [128, 2, 2048], BF16)
                load_w_bf16(k, stage, W["w_q_up"], w_q, 3, 1536)
                load_w_bf16(k, stage, W["w_kv_up"], w_kv, 2, 2048)
                KT = k.sb(p2, "KT", [128, Lmax], BF16)
                V = k.sb(p2, "V", [128, Lmax // 128, 128], BF16)
                cqi = [k.sb(p2, f"cqi{i}", [128, 3, 512], BF16) for i in range(2)]
                sgi = [k.sb(p2, f"sgi{i}", [128, 512], BF16) for i in range(2)]
                ogo = [k.sb(p2, f"ogo{i}", [128, 512], BF16) for i in range(2)]
                qtok = k.sb(p2, "qtok", [128, 4, 192], BF16)
                qa = k.sb(p2, "qa", [128, 4, 64], F32)
                qb = k.sb(p2, "qb", [128, 4, 64], F32)
                QnT = [k.sb(p2, f"QnT{i}", [128, 512], BF16) for i in range(2)]
                QrT = [k.sb(p2, f"QrT{i}", [128, 512], BF16) for i in range(2)]
                for qq in QrT:
                    t.op("pool", lambda e, qq=qq: e.memset(qq.t[64:128, :], 0.0), reads=[], writes=[qq.r])
                NPT = 6
                pt = [k.sb(p2, f"pt{i}", [128, 512], BF16) for i in range(NPT)]
                rec = k.sb(p2, "rec", [128, 512], F32)
                otmp = k.sb(p2, "otmp", [128, 512], F32)
                psS = [k.ps(p2, f"psS{i}", [128, 512], F32) for i in range(3)]
                psO = [k.ps(p2, f"psO{i}", [128, 512], F32) for i in range(2)]
                psD = [k.ps(p2, f"psD{i}", [128, 512], F32) for i in range(1)]
                accsb = k.sb(p2, "accsb", [128, 512], F32)
                accP = k.sb(p2, "accP", [128, 512], F32)
                psQ = k.ps(p2, "psQ", [128, 2, 256], F32)
                psQT = k.ps(p2, "psQT", [128, 2, 512], BF16)

                nkb = L // 128
                qi0 = 0
                for _ in range(1):
                    pass

                items = [(h, qt) for h in range(NH) for qt in range(nblk)]

                def prologue(idx):
                    h, qt = items[idx]
                    g0 = row0 + qt * 512
                    bi = (qi0 + idx) % 2
                    cq, sgt, qn, qr = cqi[bi], sgi[bi], QnT[bi], QrT[bi]

                    def p0():
                        t.dma("sp", cq.t[:], scr["CQ"][:, g0:g0 + 512].rearrange("(c p) t -> p c t", p=128), writes=[cq.r])
                        t.dma("sp", sgt.t[:], scr["SG"][h * 128:(h + 1) * 128, g0:g0 + 512], writes=[sgt.r])

                    def phalf(half):
                        for jj in range(2):
                            j = half * 2 + jj
                            for kc in range(3):
                                t.op("pe", lambda e, kc=kc, j=j, jj=jj: e.matmul(
                                    out=psQ.t[:, jj, 0:192], lhsT=cq.t[:, kc, j * 128:(j + 1) * 128], rhs=w_q.t[:, kc, h * 192:(h + 1) * 192],
                                    start=(kc == 0), stop=(kc == 2)), reads=[cq.r, w_q.r], writes=[psQ.r], inc=(kc == 2 and jj == 1))
                        j0 = half * 2
                        ti0 = qt * 4 + j0
                        t.op("act", lambda e: e.copy(out=qtok.t[:, j0:j0 + 2, 0:128], in_=psQ.t[:, :, 0:128]), reads=[psQ.r], writes=[qtok.r])
                        t.op("dve", lambda e: e.tensor_tensor(out=qa.t[:, j0:j0 + 2, :], in0=psQ.t[:, :, 128:192], in1=cos2.t[:, ti0:ti0 + 2, :], op=ALU.mult),
                             reads=[psQ.r, cos2.r], writes=[qa.r])
                        t.op("dve", lambda e: e.tensor_tensor(out=qb.t[:, j0:j0 + 2, 0:32], in0=psQ.t[:, :, 160:192], in1=sinpm.t[:, ti0:ti0 + 2, 0:32], op=ALU.mult),
                             reads=[psQ.r, sinpm.r], writes=[qb.r])
                        t.op("dve", lambda e: e.tensor_tensor(out=qb.t[:, j0:j0 + 2, 32:64], in0=psQ.t[:, :, 128:160], in1=sinpm.t[:, ti0:ti0 + 2, 32:64], op=ALU.mult),
                             reads=[psQ.r, sinpm.r], writes=[qb.r])
                        t.op("dve", lambda e: e.tensor_tensor(out=qtok.t[:, j0:j0 + 2, 128:192], in0=qa.t[:, j0:j0 + 2, :], in1=qb.t[:, j0:j0 + 2, :], op=ALU.add),
                             reads=[qa.r, qb.r], writes=[qtok.r])

                    def p3():
                        for j in range(4):
                            t.op("pe", lambda e, j=j: e.transpose(out=psQT.t[:, 0, j * 128:(j + 1) * 128], in_=qtok.t[:, j, 0:128], identity=ident.t[:]),
                                 reads=[qtok.r, ident.r], writes=[psQT.r], inc=False)
                            t.op("pe", lambda e, j=j: e.transpose(out=psQT.t[0:64, 1, j * 128:(j + 1) * 128], in_=qtok.t[:, j, 128:192], identity=ident.t[:]),
                                 reads=[qtok.r, ident.r], writes=[psQT.r], inc=(j == 3))
                        t.op("act", lambda e: e.copy(out=qn.t[:], in_=psQT.t[:, 0, :]), reads=[psQT.r], writes=[qn.r])
                        t.op("act", lambda e: e.copy(out=qr.t[0:64, :], in_=psQT.t[0:64, 1, :]), reads=[psQT.r], writes=[qr.r])

                    return [p0, lambda: phalf(0), lambda: phalf(1), p3]

                def kv_head(h):
                    banks = [psQ.t[:].rearrange("p a b -> p (a b)"), psS[0].t[:], psS[1].t[:], psS[2].t[:]]
                    bres = [psQ.r, psS[0].r, psS[1].r, psS[2].r]
                    n = 0
                    for b in range(nblk):
                        bk, br = banks[n % 4], bres[n % 4]
                        n += 1
                        for kc in range(2):
                            t.op("pe", lambda e, kc=kc, b=b, bk=bk: e.matmul(
                                out=bk, lhsT=w_kv.t[:, kc, h * 256:h * 256 + 128],
                                rhs=ckvT.t[:, kc, b * 512:(b + 1) * 512], start=(kc == 0), stop=(kc == 1)),
                                reads=[w_kv.r], writes=[br], inc=(kc == 1))
                        t.op("dve", lambda e, b=b, bk=bk: e.tensor_copy(out=KT.t[:, b * 512:(b + 1) * 512], in_=bk),
                             reads=[br], writes=[KT.r])
                    for b in range(nblk):
                        bk, br = banks[n % 4], bres[n % 4]
                        n += 1
                        for j in range(4):
                            ti = b * 4 + j
                            for kc in range(2):
                                t.op("pe", lambda e, kc=kc, ti=ti, j=j, bk=bk: e.matmul(
                                    out=bk[:, j * 128:(j + 1) * 128], lhsT=ckvT.t[:, kc, ti * 128:(ti + 1) * 128],
                                    rhs=w_kv.t[:, kc, h * 256 + 128:h * 256 + 256], start=(kc == 0), stop=(kc == 1)),
                                    reads=[w_kv.r], writes=[br], inc=(kc == 1 and j == 3))
                        t.op("act", lambda e, b=b, bk=bk: e.copy(out=V.t[:, b * 4:(b + 1) * 4, :].rearrange("p a b -> p (a b)"), in_=bk),
                             reads=[br], writes=[V.r])

                for pc in prologue(0):
                    pc()
                for idx, (h, qt) in enumerate(items):
                    if qt == 0:
                        kv_head(h)
                    g0 = row0 + qt * 512
                    bi = (qi0 + idx) % 2
                    sgt, og, qn, qr = sgi[bi], ogo[bi], QnT[bi], QrT[bi]
                    pO, pD = psO[bi], psD[0]
                    nxt = prologue(idx + 1) if idx + 1 < len(items) else []
                    when = {0: 0, max(1, nkb // 4): 1, max(2, nkb // 2): 2, max(3, (3 * nkb) // 4): 3}

                    def issue_S(kb, qn=qn, qr=qr):
                        pS = psS[kb % 3]
                        t.op("pe", lambda e: e.matmul(out=pS.t[:], lhsT=KT.t[:, kb * 128:(kb + 1) * 128], rhs=qn.t[:], start=True, stop=False),
                             reads=[KT.r, qn.r], writes=[pS.r], inc=False)
                        t.op("pe", lambda e: e.matmul(out=pS.t[:], lhsT=krT.t[:, kb * 128:(kb + 1) * 128], rhs=qr.t[:], start=False, stop=True),
                             reads=[qr.r], writes=[pS.r], inc=True)
                        p = pt[kb % NPT]
                        t.op("act", lambda e: e.activation(out=p.t[:], in_=pS.t[:], func=AF.Exp, scale=ATTN_SCALE), reads=[pS.r], writes=[p.r])

                    def issue_O(kb, pO=pO, pD=pD):
                        p = pt[kb % NPT]
                        t.op("pe", lambda e: e.matmul(out=pO.t[:], lhsT=V.t[:, kb, :], rhs=p.t[:], start=(kb == 0), stop=(kb == nkb - 1)),
                             reads=[V.r, p.r], writes=[pO.r], inc=(not ATT_ONES))
                        if ATT_ONES:
                            t.op("pe", lambda e: e.matmul(out=pD.t[:], lhsT=ones.t[:], rhs=p.t[:], start=(kb == 0), stop=(kb == nkb - 1)),
                                 reads=[ones.r, p.r], writes=[pD.r], inc=True)
                        elif kb % POOL_EVERY == POOL_EVERY - 1:
                            if kb == POOL_EVERY - 1:
                                t.op("pool", lambda e: e.tensor_copy(out=accP.t[:], in_=p.t[:]), reads=[p.r], writes=[accP.r])
                            else:
                                t.op("pool", lambda e: e.tensor_tensor(out=accP.t[:], in0=accP.t[:], in1=p.t[:], op=ALU.add), reads=[p.r, accP.r], writes=[accP.r])
                        elif kb == 0:
                            t.op("dve", lambda e: e.tensor_copy(out=pD.t[:], in_=p.t[:]), reads=[p.r], writes=[pD.r])
                        else:
                            t.op("dve", lambda e: e.tensor_tensor(out=pD.t[:], in0=pD.t[:], in1=p.t[:], op=ALU.add), reads=[p.r, pD.r], writes=[pD.r])

                    issue_S(0)
                    if nkb > 1:
                        issue_S(1)
                    for kb in range(nkb):
                        if kb + 2 < nkb:
                            issue_S(kb + 2)
                        issue_O(kb)
                        if nxt and kb in when:
                            nxt[when[kb]]()
                    if not ATT_ONES:
                        t.op("dve", lambda e, pD=pD: e.tensor_copy(out=accsb.t[:], in_=pD.t[:]), reads=[pD.r], writes=[accsb.r])
                        t.op("pe", lambda e, pD=pD: e.matmul(out=pD.t[:], lhsT=cst["ones32"].t[:], rhs=accsb.t[:], start=True, stop=False),
                             reads=[cst["ones32"].r, accsb.r], writes=[pD.r], inc=False)
                        t.op("pe", lambda e, pD=pD: e.matmul(out=pD.t[:], lhsT=cst["ones32"].t[:], rhs=accP.t[:], start=False, stop=True),
                             reads=[cst["ones32"].r, accP.r], writes=[pD.r], inc=True)
                    t.op("dve", lambda e, pD=pD: e.reciprocal(out=rec.t[:], in_=pD.t[:]), reads=[pD.r], writes=[rec.r])
                    t.op("dve", lambda e, pO=pO: e.tensor_tensor(out=otmp.t[:], in0=pO.t[:], in1=rec.t[:], op=ALU.mult), reads=[pO.r, rec.r], writes=[otmp.r])
                    t.op("dve", lambda e, og=og, sgt=sgt: e.tensor_tensor(out=og.t[:], in0=otmp.t[:], in1=sgt.t[:], op=ALU.mult), reads=[otmp.r, sgt.r], writes=[og.r])
                    t.dma(STQ, scr["OG"][h * 128:(h + 1) * 128, g0:g0 + 512], og.t[:], reads=[og.r])
                qi0 += len(items)
            t.barrier()

    if DBG_STOP in (1, 2, 11, 12, 13, 14, 15):
        return
    t.barrier()
    with ExitStack() as p3:
        w_o = k.sb(p3, "w_o", [128, 8, D], BF16)
        with ExitStack() as sst:
            stage3 = [k.sb(sst, f"stage{i}", [128, 2048], F32) for i in range(2)]
            load_w_bf16(k, stage3, W["w_out"], w_o, 8, D)
            t.barrier()
        out_proj_ln(k, p3, x_in, x_out, W, w_o, scr["OG"], T_total, tok_perm=None)
    t.barrier()


def out_proj_ln(k, st, x_in, x_out, W, w_o, OGT, T_total, tok_perm=None):
    nc, t = k.nc, k.trk
    lng = k.sb(st, "lng", [128, D], F32)
    lnb = k.sb(st, "lnb", [128, D], F32)
    t.dma("sp", lng.t[:], W["ln_g"].partition_broadcast(128), writes=[lng.r])
    t.dma("sp", lnb.t[:], W["ln_b"].partition_broadcast(128), writes=[lnb.r])
    ogi = [k.sb(st, f"ogi{i}", [128, 8, 512], BF16) for i in range(2)]
    xi = [k.sb(st, f"xi{i}", [128, 4, D], F32) for i in range(2)]
    yo = [k.sb(st, f"yo{i}", [128, 4, D], F32) for i in range(2)]
    zs, bsts, mvs = ln_scratch(k, st)
    psY = [k.ps(st, f"psY{i}", [128, 1024], F32) for i in range(2)]
    for b in range(T_total // 512):
        r0 = b * 512
        og, x, y = ogi[b % 2], xi[b % 2], yo[b % 2]
        t.dma("sp", og.t[:], OGT[:, r0:r0 + 512].rearrange("(h p) t -> p h t", p=128), writes=[og.r])
        t.dma("sp", x.t[:], x_in[r0:r0 + 512, :].rearrange("(j p) d -> p j d", p=128), writes=[x.r])
        for j in range(4):
            pY = psY[j % 2]
            for half in range(2):
                for h in range(8):
                    t.op("pe", lambda e, h=h, j=j, half=half, pY=pY, og=og: e.matmul(
                        out=pY.t[:, half * 512:(half + 1) * 512], lhsT=og.t[:, h, j * 128:(j + 1) * 128], rhs=w_o.t[:, h, half * 512:(half + 1) * 512],
                        start=(h == 0), stop=(h == 7)), reads=[og.r, w_o.r], writes=[pY.r], inc=(h == 7 and half == 1))
            ln_tail(k, x.t[:, j, :], x.r, pY, zs, bsts, mvs, lng, lnb, y.t[:, j, :], y.r, j)
        t.dma(STQ, x_out[r0:r0 + 512, :].rearrange("(j p) d -> p j d", p=128), y.t[:], reads=[y.r])


def ln_tail(k, x_ap, x_r, pY, zs, bsts, mvs, lng, lnb, y_ap, y_r, i):
    t = k.trk
    z, z2, bst, mv = zs[0][i % 2], zs[1][i % 2], bsts[i % 2], mvs[i % 2]
    t.op("dve", lambda e: e.scalar_tensor_tensor(out=z.t[:], in0=x_ap, scalar=ALPHA, in1=pY.t[:], op0=ALU.mult, op1=ALU.add),
         reads=[x_r, pY.r], writes=[z.r])
    t.op("dve", lambda e: e.bn_stats(out=bst.t[:, 0, :], in_=z.t[:, 0:512]), reads=[z.r], writes=[bst.r])
    t.op("dve", lambda e: e.bn_stats(out=bst.t[:, 1, :], in_=z.t[:, 512:1024]), reads=[z.r], writes=[bst.r])
    t.op("dve", lambda e: e.bn_aggr(out=mv.t[:, 0:2], in_=bst.t[:].rearrange("p a b -> p (a b)")), reads=[bst.r], writes=[mv.r])
    t.op("act", lambda e: e.activation(out=mv.t[:, 2:3], in_=mv.t[:, 1:2], func=AF.Sqrt, bias=k.eps_ln.t[:, 0:1]), reads=[mv.r], writes=[mv.r])
    t.op("dve", lambda e: e.reciprocal(out=mv.t[:, 3:4], in_=mv.t[:, 2:3]), reads=[mv.r], writes=[mv.r])
    t.op("dve", lambda e: e.tensor_scalar(out=z.t[:], in0=z.t[:], scalar1=mv.t[:, 0:1], scalar2=mv.t[:, 3:4], op0=ALU.subtract, op1=ALU.mult),
         reads=[z.r, mv.r], writes=[z.r])
    t.op("dve", lambda e: e.tensor_tensor(out=z2.t[:], in0=z.t[:], in1=lng.t[:], op=ALU.mult), reads=[z.r, lng.r], writes=[z2.r])
    t.op("pool", lambda e: e.tensor_tensor(out=y_ap, in0=z2.t[:], in1=lnb.t[:], op=ALU.add), reads=[z2.r, lnb.r], writes=[y_r])


def ln_scratch(k, st):
    zs = ([k.sb(st, f"z{i}", [128, D], F32) for i in range(2)], [k.sb(st, f"zz{i}", [128, D], F32) for i in range(2)])
    bsts = [k.sb(st, f"bst{i}", [128, 2, 6], F32) for i in range(2)]
    mvs = [k.sb(st, f"mv{i}", [128, 4], F32) for i in range(2)]
    return zs, bsts, mvs


TWO_PI = 2.0 * math.pi
GELU_C = 2.0 * math.sqrt(2.0 / math.pi)


def s5_setup(k, cst, W, SW, lam):
    nc, t = k.nc, k.trk
    id32, maskf, maskb = cst["id32"], cst["maskf"], cst["maskb"]
    dve = lambda fn, rd, wr: t.op("dve", fn, reads=rd, writes=wr)
    with ExitStack() as su:
        rs = Res("s5small")
        SM = k.sb(su, "SM", [128, 40, 64], F32)
        PW = k.sb(su, "PW", [128, 16, 2, 64], F32)
        BT = k.sb(su, "BT", [128, 2, 2, 16, 32], F32)
        CT = k.sb(su, "CT", [128, 2, 2, 16, 32], F32)
        BB = k.sb(su, "BB", [128, 2, 2, 16, 32], F32)
        W3 = k.sb(su, "W3", [128, 2, 32, 128], BF16)
        W1T = k.sb(su, "W1T", [128, 2, 32, 128], BF16)
        TOE = k.sb(su, "TOE", [128, 64, 128], BF16)
        TAC = k.sb(su, "TAC", [128, 64, 128], F32)
        dcol = k.sb(su, "dcol", [128, 64], F32)
        psA = k.ps(su, "psA", [128, 512], F32)
        psB = k.ps(su, "psB", [128, 512], F32)
        sm = lambda i: SM.t[:, i, :]
        AR, AI, LS, STEP, LR, LI, M_, R_, TMP, TH, T2, SN, CS, T1, T2b, T3, LBr, LBi, DEN, MUr, MUi, NR, Qr, Qi = range(24)

        with ExitStack() as s1:
            praw = k.sb(s1, "praw", [32, 3, 2, 128], F32)
            ls = k.sb(s1, "ls", [32, 2, 2], F32)
            zer = k.sb(s1, "zer", [32, 64], F32)
            for d in range(2):
                t.dma("sp", praw.t[:, 0, d, :], W["a_re"][d].rearrange("(gb g2) n -> gb (g2 n)", g2=2), writes=[praw.r])
                t.dma("sp", praw.t[:, 1, d, :], W["a_im"][d].rearrange("(gb g2) n -> gb (g2 n)", g2=2), writes=[praw.r])
                t.dma("sp", ls.t[:, d, :], W["log_step"][d].rearrange("(gb g2) -> gb g2", g2=2), writes=[ls.r])
            dve(lambda e: e.memset(zer.t[:], 0.0), [], [zer.r])
            for d in range(2):
                for g2 in range(2):
                    dve(lambda e, d=d, g2=g2: e.tensor_scalar(out=praw.t[:, 2, d, g2 * 64:(g2 + 1) * 64], in0=zer.t[:], scalar1=ls.t[:, d, g2:g2 + 1],
                                                              scalar2=None, op0=ALU.add), [zer.r, ls.r, praw.r], [praw.r])
            for j in range(3):
                for d in range(2):
                    i = j * 2 + d
                    t.op("pe", lambda e, j=j, d=d, i=i: e.transpose(out=psA.t[:, i * 32:(i + 1) * 32], in_=praw.t[:, j, d, :], identity=id32.t[0:32, 0:32]),
                         reads=[praw.r, id32.r], writes=[psA.r], inc=(i == 5))
            dve(lambda e: e.tensor_copy(out=SM.t[:, 0:3, :].rearrange("q a b -> q (a b)"), in_=psA.t[:, 0:192]), [psA.r], [rs])

        t.barrier()

        if DBG_STOP == 31:
            return
        def tt(o, a, b, op):
            dve(lambda e: e.tensor_tensor(out=sm(o), in0=sm(a), in1=sm(b), op=op), [rs], [rs])

        def ts(o, a, s1_, s2_, op0, op1=None):
            if op1 is None:
                dve(lambda e: e.tensor_scalar(out=sm(o), in0=sm(a), scalar1=s1_, scalar2=None, op0=op0), [rs], [rs])
            else:
                dve(lambda e: e.tensor_scalar(out=sm(o), in0=sm(a), scalar1=s1_, scalar2=s2_, op0=op0, op1=op1), [rs], [rs])

        t.op("act", lambda e: e.activation(out=sm(STEP), in_=sm(LS), func=AF.Exp), reads=[rs], writes=[rs])
        tt(LR, AR, STEP, ALU.mult)
        tt(LI, AI, STEP, ALU.mult)
        ts(M_, LR, 1.0 / 120, 1.0 / 24, ALU.mult, ALU.add)
        for cco in (1.0 / 6, 0.5, 1.0, 1.0):
            tt(M_, M_, LR, ALU.mult)
            ts(M_, M_, cco, None, ALU.add)
        ts(R_, LI, 1.0, None, ALU.mult)
        for m in range(1, 6):
            ts(TMP, LI, TWO_PI * m, -TWO_PI, ALU.is_ge, ALU.mult)
            tt(R_, R_, TMP, ALU.add)
        ts(TH, R_, -math.pi, 0.125, ALU.add, ALU.mult)
        tt(T2, TH, TH, ALU.mult)
        ts(SN, T2, -1.0 / 5040, 1.0 / 120, ALU.mult, ALU.add)
        for cco in (-1.0 / 6, 1.0):
            tt(SN, SN, T2, ALU.mult)
            ts(SN, SN, cco, None, ALU.add)
        tt(SN, SN, TH, ALU.mult)
        ts(CS, T2, 1.0 / 40320, -1.0 / 720, ALU.mult, ALU.add)
        for cco in (1.0 / 24, -0.5, 1.0):
            tt(CS, CS, T2, ALU.mult)
            ts(CS, CS, cco, None, ALU.add)
        for _ in range(3):
            tt(T3, CS, SN, ALU.mult)
            tt(T1, CS, CS, ALU.mult)
            tt(T2b, SN, SN, ALU.mult)
            tt(CS, T1, T2b, ALU.subtract)
            ts(SN, T3, 2.0, None, ALU.mult)
        tt(LBr, M_, CS, ALU.mult)
        ts(LBr, LBr, -1.0, None, ALU.mult)
        tt(LBi, M_, SN, ALU.mult)
        ts(LBi, LBi, -1.0, None, ALU.mult)
        tt(T1, LBr, LBr, ALU.mult)
        tt(T2b, LBi, LBi, ALU.mult)
        tt(DEN, T1, T2b, ALU.add)
        dve(lambda e: e.reciprocal(out=sm(DEN), in_=sm(DEN)), [rs], [rs])
        tt(MUr, LBr, DEN, ALU.mult)
        tt(MUi, LBi, DEN, ALU.mult)
        ts(MUi, MUi, -1.0, None, ALU.mult)
        ts(NR, LBr, -1.0, None, ALU.add)
        tt(T1, AR, AR, ALU.mult)
        tt(T2b, AI, AI, ALU.mult)
        tt(DEN, T1, T2b, ALU.add)
        dve(lambda e: e.reciprocal(out=sm(DEN), in_=sm(DEN)), [rs], [rs])
        tt(T1, NR, AR, ALU.mult)
        tt(T2b, LBi, AI, ALU.mult)
        tt(Qr, T1, T2b, ALU.add)
        tt(Qr, Qr, DEN, ALU.mult)
        tt(T1, LBi, AR, ALU.mult)
        tt(T2b, NR, AI, ALU.mult)
        tt(Qi, T1, T2b, ALU.subtract)
        tt(Qi, Qi, DEN, ALU.mult)
        pw = lambda kk, ri: PW.t[:, kk, ri, :]
        dve(lambda e: e.memset(pw(7, 0), 1.0), [rs], [rs])
        dve(lambda e: e.memset(pw(7, 1), 0.0), [rs], [rs])

        def cmul_pw(ko, ki, br, bi):
            dve(lambda e: e.tensor_tensor(out=sm(T1), in0=pw(ki, 0), in1=sm(br), op=ALU.mult), [rs], [rs])
            dve(lambda e: e.tensor_tensor(out=sm(T2b), in0=pw(ki, 1), in1=sm(bi), op=ALU.mult), [rs], [rs])
            dve(lambda e: e.tensor_tensor(out=pw(ko, 0), in0=sm(T1), in1=sm(T2b), op=ALU.subtract), [rs], [rs])
            dve(lambda e: e.tensor_tensor(out=sm(T1), in0=pw(ki, 0), in1=sm(bi), op=ALU.mult), [rs], [rs])
            dve(lambda e: e.tensor_tensor(out=sm(T2b), in0=pw(ki, 1), in1=sm(br), op=ALU.mult), [rs], [rs])
            dve(lambda e: e.tensor_tensor(out=pw(ko, 1), in0=sm(T1), in1=sm(T2b), op=ALU.add), [rs], [rs])

        for kk in range(7, 15):
            cmul_pw(kk + 1, kk, LBr, LBi)
        for kk in range(7, 0, -1):
            cmul_pw(kk - 1, kk, MUr, MUi)
        for d in range(2):
            for r in range(2):
                dve(lambda e, d=d, r=r: e.tensor_copy(out=lam["A"].t[:, d, r, :], in_=PW.t[:, 15, 0, d * 32:(d + 1) * 32]), [rs], [lam["A"].r])
            dve(lambda e, d=d: e.tensor_copy(out=lam["I"].t[:, d, :], in_=PW.t[:, 15, 1, d * 32:(d + 1) * 32]), [rs], [lam["I"].r])
            for a_ in range(2):
                dve(lambda e, d=d, a_=a_: e.tensor_copy(out=lam["L4"].t[:, d, a_, 0, :], in_=PW.t[:, 15, 0, d * 32:(d + 1) * 32]), [rs, lam["L4"].r], [lam["L4"].r])
            dve(lambda e, d=d: e.tensor_copy(out=lam["L4"].t[:, d, 0, 1, :], in_=PW.t[:, 15, 1, d * 32:(d + 1) * 32]), [rs, lam["L4"].r], [lam["L4"].r])
            dve(lambda e, d=d: e.tensor_scalar(out=lam["L4"].t[:, d, 1, 1, :], in0=PW.t[:, 15, 1, d * 32:(d + 1) * 32], scalar1=-1.0, scalar2=None, op0=ALU.mult),
                [rs, lam["L4"].r], [lam["L4"].r])
            dve(lambda e, d=d: e.tensor_copy(out=lam["B"].t[:, d, 1, :], in_=PW.t[:, 15, 1, d * 32:(d + 1) * 32]), [rs], [lam["B"].r])
            dve(lambda e, d=d: e.tensor_scalar(out=lam["B"].t[:, d, 0, :], in0=PW.t[:, 15, 1, d * 32:(d + 1) * 32], scalar1=-1.0, scalar2=None, op0=ALU.mult),
                [rs, lam["B"].r], [lam["B"].r])
            dve(lambda e, d=d: e.tensor_scalar(out=lam["NI"].t[:, d, :], in0=PW.t[:, 15, 1, d * 32:(d + 1) * 32], scalar1=-1.0, scalar2=None, op0=ALU.mult),
                [rs], [lam["NI"].r])

        if DBG_STOP == 32:
            return
        with ExitStack() as s2:
            raw = k.sb(s2, "raw", [32, 2, 2048], F32)
            raw2 = k.sb(s2, "raw2", [32, 2, 2048], F32)
            for (src_r, src_i, dst, isC) in ((W["b_re"], W["b_im"], BT, False), (W["c_re"], W["c_im"], CT, True)):
                for d in range(2):
                    pat = "(gb g2) p n -> gb (g2 p n)" if isC else "(gb g2) n p -> gb (g2 n p)"
                    t.dma("sp", raw.t[:, 0, :], src_r[d].rearrange(pat, g2=2), writes=[raw.r])
                    t.dma("sp", raw.t[:, 1, :], src_i[d].rearrange(pat, g2=2), writes=[raw.r])
                    for ri in range(2):
                        ps = psA if ri == 0 else psB
                        if isC:
                            dve(lambda e, ri=ri: e.tensor_copy(out=raw2.t[:, ri, :].rearrange("q (p g n) -> q p g n", p=16, g=2),
                                                               in_=raw.t[:, ri, :].rearrange("q (g p n) -> q p g n", g=2, p=16)), [raw.r], [raw2.r])
                        for pp in range(16):
                            if isC:
                                src = raw2.t[:, ri, pp * 128:(pp + 1) * 128]
                            else:
                                src = raw.t[:, ri, :].rearrange("q (m p) -> q m p", p=16)[:, :, pp]
                            t.op("pe", lambda e, src=src, pp=pp, ps=ps: e.transpose(out=ps.t[:, pp * 32:(pp + 1) * 32], in_=src, identity=id32.t[0:32, 0:32]),
                                 reads=[raw.r, raw2.r, id32.r], writes=[ps.r], inc=(pp == 15))
                        dve(lambda e, d=d, ri=ri, ps=ps, dst=dst: e.tensor_copy(out=dst.t[:, d, ri, :, :].rearrange("q a b -> q (a b)"), in_=ps.t[:]), [ps.r], [dst.r])
        t.barrier()
        if DBG_STOP == 33:
            return
        with ExitStack() as s3:
            u1 = k.sb(s3, "u1", [128, 16, 32], F32)
            u2 = k.sb(s3, "u2", [128, 16, 32], F32)
            for d in range(2):
                qr = SM.t[:, Qr, d * 32:(d + 1) * 32].unsqueeze(1).broadcast_to([128, 16, 32])
                qi = SM.t[:, Qi, d * 32:(d + 1) * 32].unsqueeze(1).broadcast_to([128, 16, 32])
                br, bi = BT.t[:, d, 0, :, :], BT.t[:, d, 1, :, :]
                dve(lambda e: e.tensor_tensor(out=u1.t[:], in0=br, in1=qr, op=ALU.mult), [rs, BT.r], [u1.r])
                dve(lambda e: e.tensor_tensor(out=u2.t[:], in0=bi, in1=qi, op=ALU.mult), [rs, BT.r], [u2.r])
                dve(lambda e, d=d: e.tensor_tensor(out=BB.t[:, d, 0, :, :], in0=u1.t[:], in1=u2.t[:], op=ALU.subtract), [u1.r, u2.r], [BB.r])
                dve(lambda e: e.tensor_tensor(out=u1.t[:], in0=bi, in1=qr, op=ALU.mult), [rs, BT.r, BB.r], [u1.r])
                dve(lambda e: e.tensor_tensor(out=u2.t[:], in0=br, in1=qi, op=ALU.mult), [rs, BT.r, BB.r], [u2.r])
                dve(lambda e, d=d: e.tensor_tensor(out=BB.t[:, d, 1, :, :], in0=u1.t[:], in1=u2.t[:], op=ALU.add), [u1.r, u2.r], [BB.r])
        t.barrier()
        with nc.allow_non_contiguous_dma(reason="tiny param relayout"):
            for s in range(8):
                t.dma("sp", dcol.t[16 * s:16 * s + 16, :], W["d"].rearrange("(g p) -> p g", p=16), writes=[dcol.r])
        if DBG_STOP == 34:
            return
        with ExitStack() as s4:
            V1 = k.sb(s4, "V1", [128, 2, 16, 8, 16], F32)
            Z = k.sb(s4, "Z", [128, 2, 16, 8, 16], F32)
            v1 = k.sb(s4, "v1", [128, 16, 16], F32)
            v2 = k.sb(s4, "v2", [128, 16, 16], F32)
            tq = k.sb(s4, "tq", [128, 4, 128], F32)
            for d in range(2):
                for h2 in range(2):
                    g0 = h2 * 16
                    pwv = lambda kk, ri: PW.t[:, kk, ri, d * 32 + g0:d * 32 + g0 + 16].unsqueeze(2).broadcast_to([128, 16, 16])
                    bbv = lambda ri: BB.t[:, d, ri, :, g0:g0 + 16].rearrange("q p g -> q g p")
                    ctv = lambda ri: CT.t[:, d, ri, :, g0:g0 + 16].rearrange("q p g -> q g p")

                    def cprod(out_r, out_i, a_r, a_i, kk, neg_i, rd):
                        wr = rd[-1:]
                        dve(lambda e: e.tensor_tensor(out=v1.t[:], in0=a_r, in1=pwv(kk, 0), op=ALU.mult), [rs] + rd[:-1], [v1.r])
                        dve(lambda e: e.tensor_tensor(out=v2.t[:], in0=a_i, in1=pwv(kk, 1), op=ALU.mult), [rs] + rd[:-1], [v2.r])
                        dve(lambda e: e.tensor_tensor(out=out_r, in0=v1.t[:], in1=v2.t[:], op=ALU.subtract), [v1.r, v2.r], wr)
                        dve(lambda e: e.tensor_tensor(out=v1.t[:], in0=a_r, in1=pwv(kk, 1), op=ALU.mult), [rs] + rd[:-1] + wr, [v1.r])
                        dve(lambda e: e.tensor_tensor(out=v2.t[:], in0=a_i, in1=pwv(kk, 0), op=ALU.mult), [rs] + rd[:-1] + wr, [v2.r])
                        if neg_i:
                            dve(lambda e: e.scalar_tensor_tensor(out=out_i, in0=v1.t[:], scalar=-1.0, in1=v2.t[:], op0=ALU.mult, op1=ALU.subtract), [v1.r, v2.r], wr)
                        else:
                            dve(lambda e: e.tensor_tensor(out=out_i, in0=v1.t[:], in1=v2.t[:], op=ALU.add), [v1.r, v2.r], wr)

                    for s in range(8):
                        kk = (14 - s) if d == 0 else (s + 7)
                        cprod(V1.t[:, 0, :, s, :], V1.t[:, 1, :, s, :], bbv(0), bbv(1), kk, False, [BB.r, V1.r])
                        kz = s if d == 0 else (7 - s)
                        cprod(Z.t[:, 0, :, s, :], Z.t[:, 1, :, s, :], ctv(0), ctv(1), kz, True, [CT.r, Z.r])
                        kw = (s + 8) if d == 0 else (15 - s)
                        w3v = lambda ri: W3.t[:, ri, g0:g0 + 16, s * 16:(s + 1) * 16]
                        cprod(w3v(0), w3v(1), ctv(0), ctv(1), kw, True, [CT.r, W3.r])
                    for ri in range(2):
                        for gq in range(4):
                            ps = psA if (gq % 2 == 0) else psB
                            for gl in range(4):
                                gbl = gq * 4 + gl
                                t.op("pe", lambda e, ri=ri, gbl=gbl, gl=gl, ps=ps: e.transpose(
                                    out=ps.t[:, gl * 128:(gl + 1) * 128], in_=V1.t[:, ri, gbl, :, :].rearrange("q s p -> q (s p)"), identity=id32.t[:]),
                                    reads=[V1.r, id32.r], writes=[ps.r], inc=(gl == 3))
                            gb0 = g0 + gq * 4
                            dve(lambda e, ri=ri, gb0=gb0, ps=ps: e.tensor_copy(out=W1T.t[:, ri, gb0:gb0 + 4, :].rearrange("q a b -> q (a b)"), in_=ps.t[:]),
                                [ps.r], [W1T.r])
                    for gq in range(4):
                        for g2 in range(2):
                            ps = psA if g2 == 0 else psB
                            pr = slice(g2 * 64, (g2 + 1) * 64)
                            for gl in range(4):
                                gbl = gq * 4 + gl
                                for ri in range(2):
                                    t.op("pe", lambda e, ri=ri, gbl=gbl, pr=pr, gl=gl, ps=ps: e.matmul(
                                        out=ps.t[:, gl * 128:(gl + 1) * 128], lhsT=V1.t[pr, ri, gbl, :, :].rearrange("q s p -> q (s p)"),
                                        rhs=Z.t[pr, ri, gbl, :, :].rearrange("q s p -> q (s p)"), start=(ri == 0), stop=(ri == 1)),
                                        reads=[V1.r, Z.r], writes=[ps.r], inc=(ri == 1 and gl == 3))
                        for g2 in range(2):
                            ps = psA if g2 == 0 else psB
                            gg0 = 2 * (g0 + gq * 4) + g2
                            msk = (maskf if d == 0 else maskb).t[:].unsqueeze(1).broadcast_to([128, 4, 128])
                            psv = ps.t[:].rearrange("q (a b) -> q a b", a=4)
                            tav = TAC.t[:, 2 * (g0 + gq * 4):2 * (g0 + gq * 4) + 8, :].rearrange("q (a two) b -> q a two b", two=2)[:, :, g2, :]
                            if d == 0:
                                dve(lambda e, tav=tav, psv=psv, msk=msk: e.tensor_tensor(out=tav, in0=psv, in1=msk, op=ALU.mult),
                                    [ps.r, maskf.r], [TAC.r])
                            else:
                                dve(lambda e, psv=psv, msk=msk: e.tensor_tensor(out=tq.t[:], in0=psv, in1=msk, op=ALU.mult), [ps.r, maskb.r], [tq.r])
                                dve(lambda e, tav=tav: e.tensor_tensor(out=tav, in0=tav, in1=tq.t[:], op=ALU.add),
                                    [tq.r, TAC.r], [TAC.r])
                t.dma(STQ, SW["W1T"][:, d], W1T.t[:], reads=[W1T.r])
                t.dma(STQ, SW["W3"][:, d], W3.t[:], reads=[W3.r])
        t.barrier()
        for g in range(64):
            dve(lambda e, g=g: e.scalar_tensor_tensor(out=TOE.t[:, g, :], in0=id32.t[:], scalar=dcol.t[:, g:g + 1], in1=TAC.t[:, g, :], op0=ALU.mult, op1=ALU.add),
                [id32.r, dcol.r, TAC.r], [TOE.r])
        t.dma(STQ, SW["TOEP"], TOE.t[:], reads=[TOE.r])
    t.barrier()


def s5_scan_sweep(k, cst, x_in, W, seqs, SW, lam, YS, YG, d, stage):
    nc, t = k.nc, k.trk
    ident = cst["ident"]
    with ExitStack() as sw:
        w_u = k.sb(sw, "w_u", [128, 8, D], BF16)
        with ExitStack() as sst:
            stage = [k.sb(sst, f"stage{i}", [128, 2048], F32) for i in range(2)]
            load_w_bf16(k, stage, W["w_in"][:, 0:D], w_u, 8, D)
            t.barrier()
        W1T = k.sb(sw, "W1Ts", [128, 2, 32, 128], BF16)
        W3 = k.sb(sw, "W3s", [128, 2, 32, 128], BF16)
        t.dma("sp", W1T.t[:], SW["W1T"][:, d], writes=[W1T.r])
        t.dma("sp", W3.t[:], SW["W3"][:, d], writes=[W3.r])
        if d == 0:
            TOE = k.sb(sw, "TOEs", [128, 64, 128], BF16)
            t.dma("sp", TOE.t[:], SW["TOEP"], writes=[TOE.r])
        xq = k.sb(sw, "xq", [128, D], F32)
        xcb = k.sb(sw, "xcb", [128, D], BF16)
        xTp = k.sb(sw, "xTp", [128, 8, 128], BF16)
        ucq = k.sb(sw, "ucq", [128, 64, 4, 16], BF16)
        Us = [k.sb(sw, f"U{i}", [128, 64, 128], BF16) for i in range(2)]
        XHs = [k.sb(sw, f"XH{i}", [128, 130, 2, 32], F32) for i in range(2)]
        Hb = k.sb(sw, "Hb", [128, 32, 2, 130], BF16)
        P4s = [k.sb(sw, f"P4{i}", [128, 2, 2, 32], F32) for i in range(2)]
        t2ra = [Res(f"t2ra{i}") for i in range(2)]
        t2rb = [Res(f"t2rb{i}") for i in range(2)]
        carry = k.sb(sw, "carry", [128, 2, 32], F32)
        ycm = k.sb(sw, "ycm", [128, 8, 256], F32)
        if d == 1:
            yfh = k.sb(sw, "yfh", [128, 8, 256], F32)
            gt = yfh
            ygb = k.sb(sw, "ygb", [128, 8, 256], BF16)
        psT = [k.ps(sw, f"psT{i}", [128, 8, 128], BF16) for i in range(2)]
        psU = [k.ps(sw, f"psU{i}", [128, 512], F32) for i in range(2)]
        psUT = k.ps(sw, "psUT", [128, 8, 128], BF16)
        psX = k.ps(sw, "psX", [128, 4, 128], F32)
        psY = [k.ps(sw, f"psY{i}", [128, 4, 128], F32) for i in range(2)]
        LA, LI_, LNI = lam["A"], lam["I"], lam["NI"]
        for XH in XHs:
            t.op("dve", lambda e, XH=XH: e.memset(XH.t[:], 0.0), reads=[], writes=[XH.r])
        xoff = 1 if d == 0 else 0
        cin = 0 if d == 0 else 128
        cout = 128 if d == 0 else 0
        hoff = 0 if d == 0 else 1

        segs = []
        for (row0, L) in seqs:
            nseg = L // 1024
            order = range(nseg) if d == 0 else range(nseg - 1, -1, -1)
            for si, s in enumerate(order):
                segs.append((row0 + s * 1024, si == 0))

        def stage_a(i):
            r0, _first = segs[i]
            U, XH = Us[i % 2], XHs[i % 2]
            xv = x_in[r0:r0 + 1024, :].rearrange("(c s) d -> c s d", s=8)
            for q in range(8):
                t.dma("sp", xq.t[:], xv[:, q, :], writes=[xq.r])
                t.op("act", lambda e: e.copy(out=xcb.t[:], in_=xq.t[:]), reads=[xq.r], writes=[xcb.r])
                pT = psT[q % 2]
                for kc in range(8):
                    t.op("pe", lambda e, kc=kc, pT=pT: e.transpose(out=pT.t[:, kc, :], in_=xcb.t[:, kc * 128:(kc + 1) * 128], identity=ident.t[:]),
                         reads=[xcb.r, ident.r], writes=[pT.r], inc=(kc == 7))
                t.op("act", lambda e, pT=pT: e.copy(out=xTp.t[:], in_=pT.t[:]), reads=[pT.r], writes=[xTp.r])
                for half in range(2):
                    pU = psU[half]
                    for kc in range(8):
                        t.op("pe", lambda e, kc=kc, half=half, pU=pU: e.matmul(out=pU.t[:], lhsT=xTp.t[:, kc, :], rhs=w_u.t[:, kc, half * 512:(half + 1) * 512],
                                                                               start=(kc == 0), stop=(kc == 7)), reads=[xTp.r, w_u.r], writes=[pU.r], inc=(kc == 7))
                    t.op("act", lambda e, half=half, pU=pU, q=q: e.copy(out=ucq.t[:, half * 32:(half + 1) * 32, q % 4, :],
                                                                         in_=pU.t[:].rearrange("q (g p) -> q g p", p=16)), reads=[pU.r], writes=[ucq.r])
                if q % 4 != 3:
                    continue
                hq = q // 4
                for gq in range(8):
                    for gl in range(8):
                        g = gq * 8 + gl
                        t.op("pe", lambda e, g=g, gl=gl, hq=hq: e.transpose(out=psUT.t[64 * hq:64 * hq + 64, gl, :], in_=ucq.t[:, g, :, :].rearrange("q s p -> q (s p)"), identity=ident.t[:]),
                             reads=[ucq.r, ident.r], writes=[psUT.r], inc=(gl == 7))
                    t.op("act", lambda e, gq=gq, hq=hq, U=U: e.copy(out=U.t[64 * hq:64 * hq + 64, gq * 8:gq * 8 + 8, :], in_=psUT.t[64 * hq:64 * hq + 64, :, :]),
                         reads=[psUT.r], writes=[U.r])
            for gp in range(16):
                for gl in range(2):
                    gb = gp * 2 + gl
                    for ri in range(2):
                        for g2 in range(2):
                            t.op("pe", lambda e, gb=gb, gl=gl, ri=ri, g2=g2, U=U: e.matmul(
                                out=psX.t[g2 * 64:(g2 + 1) * 64, gl * 2 + ri, :], lhsT=W1T.t[:, ri, gb, g2 * 64:(g2 + 1) * 64], rhs=U.t[:, 2 * gb + g2, :],
                                start=True, stop=True), reads=[W1T.r, U.r], writes=[psX.r], inc=(gl == 1 and ri == 1 and g2 == 1))
                t.op("act", lambda e, gp=gp, XH=XH: e.copy(
                    out=XH.t[:, xoff:xoff + 128, :, gp * 2:gp * 2 + 2].rearrange("q c r g -> q g r c"),
                    in_=psX.t[:].rearrange("q (g r) c -> q g r c", g=2)), reads=[psX.r], writes=[XH.r])

        def stage_b(i):
            r0, first = segs[i]
            U, XH = Us[i % 2], XHs[i % 2]
            XHp = XHs[(i + 1) % 2]
            if first:
                t.op("dve", lambda e: e.memset(XH.t[:, cin, :, :], 0.0), reads=[], writes=[XH.r])
            else:
                t.op("dve", lambda e: e.tensor_copy(out=XH.t[:, cin, :, :], in_=carry.t[:]), reads=[carry.r, XH.r], writes=[XH.r])
            steps = range(128) if d == 0 else range(127, -1, -1)
            for c in steps:
                cur = c + xoff
                prv = cur - 1 if d == 0 else cur + 1
                P = P4s[c % 2]
                t.op("dve", lambda e, prv=prv, P=P: e.tensor_tensor(out=P.t[:], in0=XH.t[:, prv, :, :].unsqueeze(2).broadcast_to([128, 2, 2, 32]),
                                                                     in1=lam["L4"].t[:, d, :, :, :], op=ALU.mult), reads=[XH.r, lam["L4"].r], writes=[P.r])
                t.op("dve", lambda e, cur=cur, P=P: e.tensor_tensor(out=XH.t[:, cur, :, :], in0=XH.t[:, cur, :, :], in1=P.t[:, 0, :, :], op=ALU.add), reads=[XH.r, P.r], writes=[XH.r])
                t.op("dve", lambda e, cur=cur, P=P: e.tensor_tensor(out=XH.t[:, cur, :, :], in0=XH.t[:, cur, :, :], in1=P.t[:, 1, ::-1, :], op=ALU.add), reads=[XH.r, P.r], writes=[XH.r])
            t.op("dve", lambda e: e.tensor_copy(out=carry.t[:], in_=XH.t[:, cout, :, :]), reads=[XH.r], writes=[carry.r])
            t.op("act", lambda e: e.copy(out=Hb.t[:], in_=XH.t[:].rearrange("q c r g -> q g r c")), reads=[XH.r], writes=[Hb.r])
            for qtr in range(4):
                ysl = YS[r0:r0 + 1024, :].rearrange("(c s) d -> c s d", s=8)[:, :, qtr * 256:(qtr + 1) * 256]
                if d == 1:
                    t.dma("sp", yfh.t[:], ysl, writes=[yfh.r])
                for gq in range(2):
                    for g2 in range(2):
                        pY = psY[g2]
                        pr = slice(g2 * 64, (g2 + 1) * 64)
                        for gl in range(4):
                            g = qtr * 16 + gq * 8 + 2 * gl + g2
                            gb = g // 2
                            if d == 0:
                                t.op("pe", lambda e, g=g, gl=gl, pY=pY: e.matmul(out=pY.t[:, gl, :], lhsT=U.t[:, g, :], rhs=TOE.t[:, g, :], start=True, stop=False),
                                     reads=[U.r, TOE.r], writes=[pY.r], inc=False)
                            for ri in range(2):
                                t.op("pe", lambda e, gb=gb, pr=pr, ri=ri, gl=gl, pY=pY: e.matmul(
                                    out=pY.t[:, gl, :], lhsT=Hb.t[pr, gb, ri, hoff:hoff + 128], rhs=W3.t[pr, ri, gb, :],
                                    start=(d == 1 and ri == 0), stop=(ri == 1)), reads=[Hb.r, W3.r], writes=[pY.r], inc=(ri == 1 and gl == 3))
                    for g2 in range(2):
                        pY = psY[g2]
                        oview = ycm.t[:, :, gq * 128:(gq + 1) * 128].rearrange("q t (g two p) -> q g two t p", g=4, two=2)[:, :, g2, :, :]
                        iview = pY.t[:].rearrange("q g (t p) -> q g t p", t=8)
                        if d == 0:
                            t.op("act", lambda e, oview=oview, iview=iview: e.copy(out=oview, in_=iview), reads=[pY.r], writes=[ycm.r])
                        else:
                            fview = yfh.t[:, :, gq * 128:(gq + 1) * 128].rearrange("q t (g two p) -> q g two t p", g=4, two=2)[:, :, g2, :, :]
                            t.op("dve", lambda e, oview=oview, iview=iview, fview=fview: e.tensor_tensor(out=oview, in0=iview, in1=fview, op=ALU.add),
                                 reads=[pY.r, yfh.r], writes=[ycm.r])
                if d == 0:
                    t.dma(STQ, ysl, ycm.t[:], reads=[ycm.r])
                else:
                    t.op("act", lambda e: e.activation(out=gt.t[:], in_=ycm.t[:], func=AF.Square), reads=[ycm.r], writes=[gt.r])
                    t.op("dve", lambda e: e.tensor_scalar(out=gt.t[:], in0=gt.t[:], scalar1=0.044715, scalar2=1.0, op0=ALU.mult, op1=ALU.add), reads=[gt.r], writes=[gt.r])
                    t.op("dve", lambda e: e.tensor_tensor(out=gt.t[:], in0=gt.t[:], in1=ycm.t[:], op=ALU.mult), reads=[gt.r, ycm.r], writes=[gt.r])
                    t.op("act", lambda e: e.activation(out=gt.t[:], in_=gt.t[:], func=AF.Sigmoid, scale=GELU_C), reads=[gt.r], writes=[gt.r])
                    t.op("dve", lambda e: e.tensor_tensor(out=ygb.t[:], in0=gt.t[:], in1=ycm.t[:], op=ALU.mult), reads=[gt.r, ycm.r], writes=[ygb.r])
                    t.dma(STQ, YG[r0:r0 + 1024, :].rearrange("(c s) d -> c s d", s=8)[:, :, qtr * 256:(qtr + 1) * 256], ygb.t[:], reads=[ygb.r])

        stage_a(0)
        for i in range(len(segs)):
            if i + 1 < len(segs):
                stage_a(i + 1)
            stage_b(i)
    t.barrier()


def s5_tail_sweep(k, cst, x_in, x_out, W, YG, T_total, stage):
    nc, t = k.nc, k.trk
    ident = cst["ident"]
    with ExitStack() as sw:
        w_g = k.sb(sw, "w_g", [128, 8, D], BF16)
        w_glu = k.sb(sw, "w_glu", [128, 8, D], BF16)
        w_o = k.sb(sw, "w_o", [128, 8, D], BF16)
        with ExitStack() as sst:
            stage = [k.sb(sst, f"stage{i}", [128, 2048], F32) for i in range(2)]
            load_w_bf16(k, stage, W["w_in"][:, D:2 * D], w_g, 8, D)
            load_w_bf16(k, stage, W["w_glu"], w_glu, 8, D)
            load_w_bf16(k, stage, W["w_out"], w_o, 8, D)
            t.barrier()
        bglu = k.sb(sw, "bglu", [128, 8], F32)
        with nc.allow_non_contiguous_dma(reason="tiny bias relayout"):
            t.dma("sp", bglu.t[:], W["b_glu"].rearrange("(c p) -> p c", p=128), writes=[bglu.r])
        lng = k.sb(sw, "lng", [128, D], F32)
        lnb = k.sb(sw, "lnb", [128, D], F32)
        t.dma("sp", lng.t[:], W["ln_g"].partition_broadcast(128), writes=[lng.r])
        t.dma("sp", lnb.t[:], W["ln_b"].partition_broadcast(128), writes=[lnb.r])
        xi = k.sb(sw, "xi", [128, 4, D], F32)
        xb = k.sb(sw, "xb", [128, 4, D], BF16)
        yi = k.sb(sw, "yi", [128, 4, D], BF16)
        xT = k.sb(sw, "xT", [128, 8, 512], BF16)
        yT = k.sb(sw, "yT", [128, 8, 512], BF16)
        sgms = [k.sb(sw, f"sgm{i}", [128, 512], F32) for i in range(2)]
        sils = [k.sb(sw, f"sil{i}", [128, 512], F32) for i in range(2)]
        y3T = k.sb(sw, "y3T", [128, 8, 512], BF16)
        yos = [k.sb(sw, f"yo{i}", [128, 4, D], F32) for i in range(2)]
        zs, bsts, mvs = ln_scratch(k, sw)
        psT = [k.ps(sw, f"psT{i}", [128, 2, 512], BF16) for i in range(2)]
        psZs = [k.ps(sw, f"psZ{i}", [128, 512], F32) for i in range(2)]
        psGs = [k.ps(sw, f"psG{i}", [128, 512], F32) for i in range(2)]
        psY = [k.ps(sw, "psY0", [128, 1024], F32)] * 2
        for b in range(T_total // 512):
            r0 = b * 512
            yo = yos[b % 2]
            t.dma("sp", xi.t[:], x_in[r0:r0 + 512, :].rearrange("(j p) d -> p j d", p=128), writes=[xi.r])
            t.dma("sp", yi.t[:], YG[r0:r0 + 512, :].rearrange("(j p) d -> p j d", p=128), writes=[yi.r])
            t.op("act", lambda e: e.copy(out=xb.t[:], in_=xi.t[:]), reads=[xi.r], writes=[xb.r])
            for (src, dstT) in ((xb, xT), (yi, yT)):
                for kp in range(4):
                    pT = psT[kp % 2]
                    for kk in range(2):
                        kc = kp * 2 + kk
                        for j in range(4):
                            t.op("pe", lambda e, kc=kc, kk=kk, j=j, pT=pT, src=src: e.transpose(
                                out=pT.t[:, kk, j * 128:(j + 1) * 128], in_=src.t[:, j, kc * 128:(kc + 1) * 128], identity=ident.t[:]),
                                reads=[src.r, ident.r], writes=[pT.r], inc=(kk == 1 and j == 3))
                    t.op("dve", lambda e, kp=kp, pT=pT, dstT=dstT: e.tensor_copy(out=dstT.t[:, kp * 2:kp * 2 + 2, :], in_=pT.t[:]), reads=[pT.r], writes=[dstT.r])
            for co in range(8):
                psZ, psG, sgm, sil = psZs[co % 2], psGs[co % 2], sgms[co % 2], sils[co % 2]
                for kc in range(8):
                    t.op("pe", lambda e, kc=kc, co=co: e.matmul(out=psZ.t[:], lhsT=w_glu.t[:, kc, co * 128:(co + 1) * 128], rhs=yT.t[:, kc, :], start=(kc == 0), stop=(kc == 7)),
                         reads=[w_glu.r, yT.r], writes=[psZ.r], inc=(kc == 7))
                for kc in range(8):
                    t.op("pe", lambda e, kc=kc, co=co: e.matmul(out=psG.t[:], lhsT=w_g.t[:, kc, co * 128:(co + 1) * 128], rhs=xT.t[:, kc, :], start=(kc == 0), stop=(kc == 7)),
                         reads=[w_g.r, xT.r], writes=[psG.r], inc=(kc == 7))
                t.op("act", lambda e, co=co: e.activation(out=sgm.t[:], in_=psZ.t[:], func=AF.Sigmoid, bias=bglu.t[:, co:co + 1]), reads=[psZ.r, bglu.r], writes=[sgm.r])
                t.op("act", lambda e: e.activation(out=sil.t[:], in_=psG.t[:], func=AF.Silu), reads=[psG.r], writes=[sil.r])
                t.op("dve", lambda e: e.tensor_tensor(out=sgm.t[:], in0=sgm.t[:], in1=sil.t[:], op=ALU.mult), reads=[sgm.r, sil.r], writes=[sgm.r])
                t.op("dve", lambda e, co=co: e.tensor_tensor(out=y3T.t[:, co, :], in0=sgm.t[:], in1=yT.t[:, co, :], op=ALU.mult), reads=[sgm.r, yT.r], writes=[y3T.r])
            for j in range(4):
                pY = psY[j % 2]
                for half in range(2):
                    for kc in range(8):
                        t.op("pe", lambda e, kc=kc, j=j, half=half, pY=pY: e.matmul(
                            out=pY.t[:, half * 512:(half + 1) * 512], lhsT=y3T.t[:, kc, j * 128:(j + 1) * 128], rhs=w_o.t[:, kc, half * 512:(half + 1) * 512],
                            start=(kc == 0), stop=(kc == 7)), reads=[y3T.r, w_o.r], writes=[pY.r], inc=(kc == 7 and half == 1))
                ln_tail(k, xi.t[:, j, :], xi.r, pY, zs, bsts, mvs, lng, lnb, yo.t[:, j, :], yo.r, j)
            t.dma(STQ, x_out[r0:r0 + 512, :].rearrange("(j p) d -> p j d", p=128), yo.t[:], reads=[yo.r])
    t.barrier()


def s5_layer(k, cst, x_in, x_out, W, seqs, S5S):
    nc, t = k.nc, k.trk
    T_total = sum(L for _, L in seqs)
    with ExitStack() as ls:
        lam = {"A": k.sb(ls, "lamA", [128, 2, 2, 32], F32), "I": k.sb(ls, "lamI", [128, 2, 32], F32), "NI": k.sb(ls, "lamNI", [128, 2, 32], F32),
               "B": k.sb(ls, "lamB", [128, 2, 2, 32], F32), "L4": k.sb(ls, "lamL4", [128, 2, 2, 2, 32], F32)}
        stage = None
        s5_setup(k, cst, W, S5S, lam)
        if DBG_STOP in (21, 31, 32, 33, 34):
            return
        s5_scan_sweep(k, cst, x_in, W, seqs, S5S, lam, S5S["YS"], S5S["YG"], 0, stage)
        if DBG_STOP == 22:
            return
        s5_scan_sweep(k, cst, x_in, W, seqs, S5S, lam, S5S["YS"], S5S["YG"], 1, stage)
        if DBG_STOP == 23:
            return
        s5_tail_sweep(k, cst, x_in, x_out, W, S5S["YG"], T_total, stage)


def build_program(seqs, layers, ntile_rope):
    T_total = sum(L for _, L in seqs)
    nc = bass.Bass("TRN2", target_bir_lowering=False)
    dt = nc.dram_tensor
    x = dt("x", [T_total, D], F32, kind="ExternalInput").ap()
    y = dt("y", [T_total, D], F32, kind="ExternalOutput").ap()
    n_mla = sum(1 for l in layers if l == "mla")
    n_s5 = sum(1 for l in layers if l == "s5")
    Wd = {}
    if n_mla:
        Wd["mla_w_in"] = dt("mla_w_in", [n_mla, D, MLA_IN], F32, kind="ExternalInput").ap()
        Wd["mla_g_q"] = dt("mla_g_q", [n_mla, QLORA], F32, kind="ExternalInput").ap()
        Wd["mla_w_q_up"] = dt("mla_w_q_up", [n_mla, QLORA, 1536], F32, kind="ExternalInput").ap()
        Wd["mla_g_kv"] = dt("mla_g_kv", [n_mla, KVLORA], F32, kind="ExternalInput").ap()
        Wd["mla_w_kv_up"] = dt("mla_w_kv_up", [n_mla, KVLORA, 2048], F32, kind="ExternalInput").ap()
        Wd["mla_w_out"] = dt("mla_w_out", [n_mla, D, D], F32, kind="ExternalInput").ap()
    if n_s5:
        for nm, shp in (("s5_w_in", [D, 2 * D]), ("s5_a_re", [2, 64, 64]), ("s5_a_im", [2, 64, 64]), ("s5_log_step", [2, 64]),
                        ("s5_b_re", [2, 64, 64, 16]), ("s5_b_im", [2, 64, 64, 16]), ("s5_c_re", [2, 64, 16, 64]), ("s5_c_im", [2, 64, 16, 64]),
                        ("s5_d", [D]), ("s5_w_glu", [D, D]), ("s5_b_glu", [D]), ("s5_w_out", [D, D])):
            Wd[nm] = dt(nm, [n_s5] + shp, F32, kind="ExternalInput").ap()
    Wd["ln_g"] = dt("ln_g", [len(layers), D], F32, kind="ExternalInput").ap()
    Wd["ln_b"] = dt("ln_b", [len(layers), D], F32, kind="ExternalInput").ap()
    c_ident = dt("c_ident", [128, 128], BF16, kind="ExternalInput").ap()
    c_cos = dt("c_cos", [128, ntile_rope, 64], F32, kind="ExternalInput").ap()
    c_sin = dt("c_sin", [128, ntile_rope, 64], F32, kind="ExternalInput").ap()
    c_id32 = dt("c_id32", [128, 128], F32, kind="ExternalInput").ap()
    c_maskf = dt("c_maskf", [128, 128], F32, kind="ExternalInput").ap()
    c_maskb = dt("c_maskb", [128, 128], F32, kind="ExternalInput").ap()
    S5S = {
        "W1T": dt("s5s_w1t", [128, 2, 2, 32, 128], BF16, kind="Internal").ap(),
        "W3": dt("s5s_w3", [128, 2, 2, 32, 128], BF16, kind="Internal").ap(),
        "TOEP": dt("s5s_toep", [128, 64, 128], BF16, kind="Internal").ap(),
        "YS": dt("s5s_ys", [T_total, D], F32, kind="Internal").ap(),
        "YG": dt("s5s_yg", [T_total, D], BF16, kind="Internal").ap(),
    }
    xa = dt("xa", [T_total, D], F32, kind="Internal").ap()
    xb_ = dt("xb", [T_total, D], F32, kind="Internal").ap()
    scr = {
        "SG": dt("scr_sg", [D, T_total], BF16, kind="Internal").ap(),
        "OG": dt("scr_og", [D, T_total], BF16, kind="Internal").ap(),
        "CQ": dt("scr_cq", [QLORA, T_total], BF16, kind="Internal").ap(),
    }

    with ExitStack() as st:
        k = K(nc, st)
        t = k.trk
        cst = {}
        cst["ident"] = k.sb(st, "ident", [128, 128], BF16)
        cst["ones"] = k.sb(st, "ones", [128, 128], BF16)
        cst["c_cos"], cst["c_sin"], cst["ntile_rope"] = c_cos, c_sin, ntile_rope
        for nm, src in (("id32", c_id32), ("maskf", c_maskf), ("maskb", c_maskb)):
            cst[nm] = k.sb(st, nm, [128, 128], F32)
            t.dma("sp", cst[nm].t[:], src, writes=[cst[nm].r])
        k.eps_ln = k.sb(st, "eps_ln", [128, 1], F32)
        t.dma("sp", cst["ident"].t[:], c_ident, writes=[cst["ident"].r])
        t.op("dve", lambda e: e.memset(cst["ones"].t[:], 1.0), writes=[cst["ones"].r])
        cst["ones32"] = k.sb(st, "ones32", [128, 128], F32)
        t.op("dve", lambda e: e.memset(cst["ones32"].t[:], 1.0), writes=[cst["ones32"].r])
        t.op("dve", lambda e: e.memset(k.eps_ln.t[:], LN_EPS), writes=[k.eps_ln.r])

        cur = x
        i_mla = i_s5 = 0
        for li, lt in enumerate(layers):
            dst = y if li == len(layers) - 1 else (xa if li % 2 == 0 else xb_)
            if lt == "mla":
                W = {"w_in": Wd["mla_w_in"][i_mla], "g_q": Wd["mla_g_q"][i_mla], "w_q_up": Wd["mla_w_q_up"][i_mla],
                     "g_kv": Wd["mla_g_kv"][i_mla], "w_kv_up": Wd["mla_w_kv_up"][i_mla], "w_out": Wd["mla_w_out"][i_mla],
                     "ln_g": Wd["ln_g"][li], "ln_b": Wd["ln_b"][li]}
                mla_layer(k, cst, cur, dst, W, seqs, scr)
                i_mla += 1
            else:
                W = {n[3:]: Wd[n][i_s5] for n in Wd if n.startswith("s5_")}
                W["ln_g"], W["ln_b"] = Wd["ln_g"][li], Wd["ln_b"][li]
                s5_layer(k, cst, cur, dst, W, seqs, S5S)
                i_s5 += 1
            cur = dst
        t.barrier(("sp",))
    return nc


def s5_consts():
    sp = np.arange(128) // 16
    maskf = (sp[None, :] >= sp[:, None]).astype(np.float32)
    maskb = (sp[:, None] >= sp[None, :]).astype(np.float32)
    return np.eye(128, dtype=np.float32), maskf, maskb


def rope_consts(ntile):
    inv = (10000.0 ** (-np.arange(0, 64, 2, dtype=np.float32) / np.float32(64))).astype(np.float32)
    pos = np.arange(ntile * 128, dtype=np.float32)
    ang = (pos[:, None] * inv[None, :]).astype(np.float32)
    cos, sin = np.cos(ang).astype(np.float32), np.sin(ang).astype(np.float32)
    cos2 = np.concatenate([cos, cos], axis=1)
    sinpm = np.concatenate([-sin, sin], axis=1)
    f = lambda a: np.ascontiguousarray(a.reshape(ntile, 128, 64).transpose(1, 0, 2))
    return f(cos2), f(sinpm)


SEQ_P, SEQ_S = 8192, 4096
LAYERS = ["mla", "s5", "mla", "s5"]
_PROG = {}


def kernel(x_prompt, x_sample, mla_w_in, mla_g_q, mla_w_q_up, mla_g_kv, mla_w_kv_up, mla_w_out,
           s5_w_in, s5_a_re, s5_a_im, s5_log_step, s5_b_re, s5_b_im, s5_c_re, s5_c_im, s5_d,
           s5_w_glu, s5_b_glu, s5_w_out, ln_g, ln_b):
    import ml_dtypes
    f32 = lambda a: np.ascontiguousarray(np.asarray(a, dtype=np.float32))
    x_prompt, x_sample = f32(x_prompt), f32(x_sample)
    seqs = [(0, SEQ_P), (SEQ_P, SEQ_S), (SEQ_P + SEQ_S, SEQ_S)]
    ntile = SEQ_P // 128
    if "nc" not in _PROG:
        _PROG["nc"] = build_program(seqs, LAYERS, ntile)
    nc = _PROG["nc"]
    cos2, sinpm = rope_consts(ntile)
    i32, mf, mb = s5_consts()
    shared = {
        "mla_w_in": f32(mla_w_in), "mla_g_q": f32(mla_g_q), "mla_w_q_up": f32(mla_w_q_up), "mla_g_kv": f32(mla_g_kv),
        "mla_w_kv_up": f32(mla_w_kv_up), "mla_w_out": f32(mla_w_out),
        "s5_w_in": f32(s5_w_in), "s5_a_re": f32(s5_a_re), "s5_a_im": f32(s5_a_im), "s5_log_step": f32(s5_log_step),
        "s5_b_re": f32(s5_b_re), "s5_b_im": f32(s5_b_im), "s5_c_re": f32(s5_c_re), "s5_c_im": f32(s5_c_im), "s5_d": f32(s5_d),
        "s5_w_glu": f32(s5_w_glu), "s5_b_glu": f32(s5_b_glu), "s5_w_out": f32(s5_w_out),
        "ln_g": f32(ln_g), "ln_b": f32(ln_b),
        "c_ident": np.eye(128, dtype=ml_dtypes.bfloat16), "c_cos": cos2, "c_sin": sinpm,
        "c_id32": i32, "c_maskf": mf, "c_maskb": mb,
    }
    in_maps = []
    for c in range(NCORES):
        xc = np.concatenate([x_prompt[c], x_sample[2 * c], x_sample[2 * c + 1]], axis=0)
        m = dict(shared)
        m["x"] = np.ascontiguousarray(xc)
        in_maps.append(m)
    res = run_bass_kernel_spmd(nc, in_maps, core_ids=list(range(NCORES)))
    y_p = np.empty_like(x_prompt)
    y_s = np.empty_like(x_sample)
    for c in range(NCORES):
        yc = np.asarray(res.results[c]["y"], dtype=np.float32)
        y_p[c] = yc[0:SEQ_P]
        y_s[2 * c] = yc[SEQ_P:SEQ_P + SEQ_S]
        y_s[2 * c + 1] = yc[SEQ_P + SEQ_S:]
    return (y_p, y_s)
```

```python
import math
from contextlib import ExitStack

import numpy as np
import concourse.bass as bass
import concourse.mybir as mybir
from concourse.bass_utils import run_bass_kernel_spmd

F32 = mybir.dt.float32
BF16 = mybir.dt.bfloat16
AF = mybir.ActivationFunctionType
ALU = mybir.AluOpType

D = 1024
NH = 8
QLORA, KVLORA, ROPE = 384, 256, 64
MLA_IN = 1728
ATTN_SCALE = 1.0 / math.sqrt(192.0)
DEPTH = 4
ALPHA = (2 * DEPTH) ** 0.25
LN_EPS = 1e-5
RMS_EPS = 1e-6
NCORES = 8
import os as _os
DBG_STOP = int(_os.environ.get('DBG_STOP', '0'))
STQ = _os.environ.get('STQ', 'pool')
ATT_ONES = int(_os.environ.get('ATT_ONES', '0'))
POOL_EVERY = int(_os.environ.get('POOL_EVERY', '3'))
DBG_MASK = int(_os.environ.get('DBG_MASK', '15'))


class Res:
    __slots__ = ("name", "lw", "rd", "excl")

    def __init__(self, name="", excl=False):
        self.name = name
        self.lw = None
        self.rd = {}
        self.excl = excl


class Trk:
    def __init__(self, nc, stack):
        self.nc = nc
        self.stack = stack
        self.E = {"pe": nc.tensor, "act": nc.scalar, "dve": nc.vector, "pool": nc.gpsimd, "sp": nc.sync}
        self.sems, self.cnt, self.pend = {}, {}, {}
        for k in self.E:
            self.sems[k] = stack.enter_context(nc.semaphore("s_" + k))
            self.cnt[k] = 0
            self.pend[k] = False
        self.seen = {k: {} for k in self.E}

    def _need(self, eng, deps):
        for (sk, val) in deps:
            if self.seen[eng].get(sk, 0) >= val:
                continue
            if sk in self.E:
                assert self.cnt[sk] >= val, f"waiting on pending inc of {sk}"
            self.E[eng].wait_ge(self.sems[sk], val)
            self.seen[eng][sk] = val

    def _deps(self, eng, reads, writes):
        deps = []
        for r in reads:
            if r.lw is not None:
                deps.append(r.lw)
            if r.excl:
                for sk, v in r.rd.items():
                    if sk != eng:
                        deps.append((sk, v))
        for w in writes:
            if w.lw is not None and not (w.lw[0] == eng and eng == "pe"):
                deps.append(w.lw)
            for sk, v in w.rd.items():
                if not (sk == eng and eng == "pe"):
                    deps.append((sk, v))
        return deps

    def op(self, eng, fn, reads=(), writes=(), inc=True):
        self._need(eng, self._deps(eng, reads, writes))
        inst = fn(self.E[eng])
        if inc:
            inst.then_inc(self.sems[eng], 1)
            self.cnt[eng] += 1
            val = self.cnt[eng]
            self.pend[eng] = False
        else:
            val = self.cnt[eng] + 1
            self.pend[eng] = True
        for r in reads:
            r.rd[eng] = max(r.rd.get(eng, 0), val)
        for w in writes:
            w.lw = (eng, val)
            w.rd = {}
        return inst

    def dma(self, eng, out, in_, reads=(), writes=(), key=None, **kw):
        self._need(eng, self._deps("dma", reads, writes))
        if key is None:
            key = writes[0].name if writes else reads[0].name
        key = "d_" + key
        if key not in self.sems:
            self.sems[key] = self.stack.enter_context(self.nc.semaphore(key))
            self.cnt[key] = 0
        inst = self.E[eng].dma_start(out=out, in_=in_, **kw)
        inst.then_inc(self.sems[key], 16)
        self.cnt[key] += 16
        val = self.cnt[key]
        for r in reads:
            r.rd[key] = val
        for w in writes:
            w.lw = (key, val)
            w.rd = {}
        return inst

    def barrier(self, engines=("pe", "act", "dve", "pool", "sp")):
        for sk in self.E:
            assert not self.pend[sk], f"pending inc on {sk} at barrier"
        for e in engines:
            self._need(e, [(sk, self.cnt[sk]) for sk in self.sems if self.cnt[sk] > 0 and sk != e])


class Buf:
    def __init__(self, t, name, excl=False):
        self.t = t
        self.r = Res(name, excl)


class K:
    def __init__(self, nc, stack):
        self.nc = nc
        self.trk = Trk(nc, stack)
        self.uid = 0

    def sb(self, st, name, shape, dt):
        self.uid += 1
        nm = f"{name}_{self.uid}"
        if _os.environ.get("DBG_ALLOC"):
            n = 1
            for v in shape[1:]:
                n *= v
            print("ALLOC", nm, shape, n * (2 if dt == BF16 else 4))
        return Buf(st.enter_context(self.nc.sbuf_tensor(nm, list(shape), dt)), name)

    def ps(self, st, name, shape, dt):
        self.uid += 1
        nm = f"{name}_{self.uid}"
        return Buf(st.enter_context(self.nc.psum_tensor(nm, list(shape), dt)), name, excl=True)


def _R(bufs):
    return [b.r for b in bufs]


def load_w_bf16(k, st_bufs, w_dram, dst, nk, ncols, cast_eng=("act", "dve")):
    t = k.trk
    i = 0
    for kc in range(nk):
        for c0 in range(0, ncols, 2048):
            c1 = min(ncols, c0 + 2048)
            sbuf = st_bufs[i % len(st_bufs)]
            t.dma("sp", sbuf.t[:, 0:c1 - c0], w_dram[kc * 128:(kc + 1) * 128, c0:c1], writes=[sbuf.r])
            eng = cast_eng[i % len(cast_eng)]
            if eng == "act":
                t.op("act", lambda e, s=sbuf, kc=kc, c0=c0, c1=c1: e.copy(out=dst.t[:, kc, c0:c1], in_=s.t[:, 0:c1 - c0]),
                     reads=[sbuf.r], writes=[dst.r])
            else:
                t.op("dve", lambda e, s=sbuf, kc=kc, c0=c0, c1=c1: e.tensor_copy(out=dst.t[:, kc, c0:c1], in_=s.t[:, 0:c1 - c0]),
                     reads=[sbuf.r], writes=[dst.r])
            i += 1


def bcast_rows(ap_1d, n):
    return ap_1d.rearrange("(o n) -> o n", o=1).broadcast(0, 128) if hasattr(ap_1d, "broadcast") else None


def mla_layer(k, cst, x_in, x_out, W, seqs, scr):
    nc, t = k.nc, k.trk
    T_total = sum(L for _, L in seqs)
    Lmax = max(L for _, L in seqs)
    ident, ones = cst["ident"], cst["ones"]

    with ExitStack() as ls:
        cos2 = k.sb(ls, "cos2", [128, cst["ntile_rope"], 64], F32)
        sinpm = k.sb(ls, "sinpm", [128, cst["ntile_rope"], 64], F32)
        t.dma("sp", cos2.t[:], cst["c_cos"], writes=[cos2.r])
        t.dma("sp", sinpm.t[:], cst["c_sin"], writes=[sinpm.r])
        ckvT = k.sb(ls, "ckvT", [128, 2, Lmax], BF16)
        krT = k.sb(ls, "krT", [128, Lmax], BF16)
        t.op("pool", lambda e: e.memset(krT.t[64:128, :], 0.0), reads=[], writes=[krT.r])
        stage = [k.sb(ls, f"stage{i}", [128, 2048], F32) for i in range(2)]
        gq = k.sb(ls, "gq", [128, QLORA + KVLORA], F32)
        t.dma("sp", gq.t[:, 0:QLORA], W["g_q"].partition_broadcast(128), writes=[gq.r])
        t.dma("sp", gq.t[:, QLORA:QLORA + KVLORA], W["g_kv"].partition_broadcast(128), writes=[gq.r])

        for (row0, L) in seqs:
            nblk = L // 512
            with ExitStack() as p1:
                w_in = k.sb(p1, "w_in", [128, 8, MLA_IN], BF16)
                load_w_bf16(k, stage, W["w_in"], w_in, 8, MLA_IN)
                xt = k.sb(p1, "xt", [128, 4, D], F32)
                xb = k.sb(p1, "xb", [128, 4, D], BF16)
                xT = k.sb(p1, "xT", [128, 8, 512], BF16)
                lat = k.sb(p1, "lat", [128, 4, 704], BF16)
                sg = [k.sb(p1, f"sg{i}", [128, 8, 512], BF16) for i in range(2)]
                cqo = [k.sb(p1, f"cqo{i}", [128, 3, 512], BF16) for i in range(2)]
                junk = k.sb(p1, "junk", [128, 384], BF16)
                stat = k.sb(p1, "stat", [128, 8], F32)
                rtmp = k.sb(p1, "rtmp", [128, 4, 64], F32)
                rtmp2 = k.sb(p1, "rtmp2", [128, 4, 64], F32)
                psT = [k.ps(p1, f"psT{i}", [128, 2, 512], BF16) for i in range(2)]
                psL = k.ps(p1, "psL", [128, 1024], F32)
                psLT = k.ps(p1, "psLT", [128, 4, 512], BF16)
                psG = [k.ps(p1, f"psG{i}", [128, 512], F32) for i in range(2)]

                for b in range(nblk):
                    if DBG_STOP == 11:
                        break
                    r0 = row0 + b * 512
                    c0 = b * 512
                    g0 = r0
                    t.dma("sp", xt.t[:], x_in[r0:r0 + 512, :].rearrange("(j p) d -> p j d", p=128), writes=[xt.r])
                    t.op("act", lambda e: e.copy(out=xb.t[:, 0:2, :], in_=xt.t[:, 0:2, :]), reads=[xt.r], writes=[xb.r])
                    t.op("act", lambda e: e.copy(out=xb.t[:, 2:4, :], in_=xt.t[:, 2:4, :]), reads=[xt.r], writes=[xb.r])
                    for kp in range(4):
                        pT = psT[kp % 2]
                        for kk in range(2):
                            kc = kp * 2 + kk
                            for j in range(4):
                                last = (kk == 1 and j == 3)
                                t.op("pe", lambda e, kc=kc, kk=kk, j=j, pT=pT: e.transpose(
                                    out=pT.t[:, kk, j * 128:(j + 1) * 128], in_=xb.t[:, j, kc * 128:(kc + 1) * 128],
                                    identity=ident.t[:]), reads=[xb.r, ident.r], writes=[pT.r], inc=last)
                        t.op("dve", lambda e, kp=kp, pT=pT: e.tensor_copy(out=xT.t[:, kp * 2:kp * 2 + 2, :], in_=pT.t[:]),
                             reads=[pT.r], writes=[xT.r])
                    if DBG_STOP == 12:
                        continue
                    sgb = sg[b % 2]

                    def gate_head(h, sgb=sgb):
                        pG = psG[h % 2]
                        for kc in range(8):
                            t.op("pe", lambda e, kc=kc: e.matmul(
                                out=pG.t[:], lhsT=w_in.t[:, kc, 704 + h * 128:704 + (h + 1) * 128], rhs=xT.t[:, kc, :],
                                start=(kc == 0), stop=(kc == 7)), reads=[xT.r, w_in.r], writes=[pG.r], inc=(kc == 7))
                        t.op("act", lambda e: e.activation(out=sgb.t[:, h, :], in_=pG.t[:], func=AF.Silu),
                             reads=[pG.r], writes=[sgb.r])
                    for j in range(4):
                        for kc in range(8):
                            t.op("pe", lambda e, kc=kc, j=j: e.matmul(
                                out=psL.t[:, 0:512], lhsT=xT.t[:, kc, j * 128:(j + 1) * 128], rhs=w_in.t[:, kc, 0:512],
                                start=(kc == 0), stop=(kc == 7)), reads=[xT.r, w_in.r], writes=[psL.r], inc=False)
                            t.op("pe", lambda e, kc=kc, j=j: e.matmul(
                                out=psL.t[:, 512:704], lhsT=xT.t[:, kc, j * 128:(j + 1) * 128], rhs=w_in.t[:, kc, 512:704],
                                start=(kc == 0), stop=(kc == 7)), reads=[xT.r, w_in.r], writes=[psL.r], inc=(kc == 7))
                        t.op("act", lambda e: e.activation(out=junk.t[:, 0:QLORA], in_=psL.t[:, 0:QLORA], func=AF.Square,
                                                           accum_out=stat.t[:, 0:1]), reads=[psL.r], writes=[junk.r, stat.r])
                        t.op("act", lambda e: e.activation(out=junk.t[:, 0:KVLORA], in_=psL.t[:, QLORA:QLORA + KVLORA], func=AF.Square,
                                                           accum_out=stat.t[:, 1:2]), reads=[psL.r], writes=[junk.r, stat.r])
                        t.op("dve", lambda e: e.tensor_scalar(out=stat.t[:, 2:3], in0=stat.t[:, 0:1], scalar1=1.0 / QLORA, scalar2=RMS_EPS,
                                                              op0=ALU.mult, op1=ALU.add), reads=[stat.r], writes=[stat.r])
                        t.op("dve", lambda e: e.tensor_scalar(out=stat.t[:, 3:4], in0=stat.t[:, 1:2], scalar1=1.0 / KVLORA, scalar2=RMS_EPS,
                                                              op0=ALU.mult, op1=ALU.add), reads=[stat.r], writes=[stat.r])
                        t.op("act", lambda e: e.activation(out=stat.t[:, 4:6], in_=stat.t[:, 2:4], func=AF.Sqrt), reads=[stat.r], writes=[stat.r])
                        t.op("dve", lambda e: e.reciprocal(out=stat.t[:, 6:8], in_=stat.t[:, 4:6]), reads=[stat.r], writes=[stat.r])
                        t.op("dve", lambda e, j=j: e.scalar_tensor_tensor(
                            out=lat.t[:, j, 0:QLORA], in0=psL.t[:, 0:QLORA], scalar=stat.t[:, 6:7], in1=gq.t[:, 0:QLORA],
                            op0=ALU.mult, op1=ALU.mult), reads=[psL.r, stat.r, gq.r], writes=[lat.r])
                        t.op("dve", lambda e, j=j: e.scalar_tensor_tensor(
                            out=lat.t[:, j, QLORA:640], in0=psL.t[:, QLORA:640], scalar=stat.t[:, 7:8], in1=gq.t[:, QLORA:640],
                            op0=ALU.mult, op1=ALU.mult), reads=[psL.r, stat.r, gq.r], writes=[lat.r])
                        ti = b * 4 + j
                        t.op("dve", lambda e, j=j, ti=ti: e.tensor_tensor(out=rtmp.t[:, j, :], in0=psL.t[:, 640:704], in1=cos2.t[:, ti, :], op=ALU.mult),
                             reads=[psL.r, cos2.r], writes=[rtmp.r])
                        t.op("dve", lambda e, j=j, ti=ti: e.tensor_tensor(out=rtmp2.t[:, j, 0:32], in0=psL.t[:, 672:704], in1=sinpm.t[:, ti, 0:32], op=ALU.mult),
                             reads=[psL.r, sinpm.r], writes=[rtmp2.r])
                        t.op("dve", lambda e, j=j, ti=ti: e.tensor_tensor(out=rtmp2.t[:, j, 32:64], in0=psL.t[:, 640:672], in1=sinpm.t[:, ti, 32:64], op=ALU.mult),
                             reads=[psL.r, sinpm.r], writes=[rtmp2.r])
                        t.op("dve", lambda e, j=j: e.tensor_tensor(out=lat.t[:, j, 640:704], in0=rtmp.t[:, j, :], in1=rtmp2.t[:, j, :], op=ALU.add),
                             reads=[rtmp.r, rtmp2.r], writes=[lat.r])
                        if DBG_STOP not in (13, 14, 15):
                            gate_head(2 * j)
                            gate_head(2 * j + 1)
                    if DBG_STOP == 13:
                        continue
                    for j in range(4):
                        for c in range(6):
                            w = 128 if c < 5 else 64
                            last = (j == 3 and c == 5)
                            dstp = psLT if c < 4 else psT[0]
                            cc = c if c < 4 else c - 4
                            t.op("pe", lambda e, j=j, c=c, w=w, dstp=dstp, cc=cc: e.transpose(
                                out=dstp.t[0:w, cc, j * 128:(j + 1) * 128], in_=lat.t[:, j, c * 128:c * 128 + w], identity=ident.t[:]),
                                reads=[lat.r, ident.r], writes=[dstp.r], inc=(last or (j == 3 and c == 3)))
                    cq = cqo[b % 2]
                    if DBG_MASK & 1:
                        t.op("dve", lambda e, cq=cq: e.tensor_copy(out=cq.t[:], in_=psLT.t[:, 0:3, :]), reads=[psLT.r], writes=[cq.r])
                    if DBG_MASK & 2:
                        t.op("act", lambda e, c0=c0: e.copy(out=ckvT.t[:, 0, c0:c0 + 512], in_=psLT.t[:, 3, :]), reads=[psLT.r], writes=[])
                    if DBG_MASK & 4:
                        t.op("act", lambda e, c0=c0: e.copy(out=ckvT.t[:, 1, c0:c0 + 512], in_=psT[0].t[:, 0, :]), reads=[psT[0].r], writes=[])
                    if DBG_MASK & 8:
                        t.op("dve", lambda e, c0=c0: e.tensor_copy(out=krT.t[0:64, c0:c0 + 512], in_=psT[0].t[0:64, 1, :]), reads=[psT[0].r], writes=[])
                    if DBG_STOP != 15:
                        t.dma(STQ, scr["CQ"][:, g0:g0 + 512].rearrange("(c p) t -> p c t", p=128), cq.t[:], reads=[cq.r])
                    t.dma(STQ, scr["SG"][:, g0:g0 + 512].rearrange("(h p) t -> p h t", p=128), sgb.t[:], reads=[sgb.r])
            t.barrier()
            if DBG_STOP in (1, 11, 12, 13, 14, 15):
                continue
            with ExitStack() as p2:
                w_q = k.sb(p2, "w_q", [128, 3, 1536], BF16)
                w_kv = k.sb(p2, "w_kv", [128, 2, 2048], BF16)
                load_w_bf16(k, stage, W["w_q_up"], w_q, 3, 1536)
                load_w_bf16(k, stage, W["w_kv_up"], w_kv, 2, 2048)
                KT = k.sb(p2, "KT", [128, Lmax], BF16)
                V = k.sb(p2, "V", [128, Lmax // 128, 128], BF16)
                cqi = [k.sb(p2, f"cqi{i}", [128, 3, 512], BF16) for i in range(2)]
                sgi = [k.sb(p2, f"sgi{i}", [128, 512], BF16) for i in range(2)]
                ogo = [k.sb(p2, f"ogo{i}", [128, 512], BF16) for i in range(2)]
                qtok = k.sb(p2, "qtok", [128, 4, 192], BF16)
                qa = k.sb(p2, "qa", [128, 4, 64], F32)
                qb = k.sb(p2, "qb", [128, 4, 64], F32)
                QnT = [k.sb(p2, f"QnT{i}", [128, 512], BF16) for i in range(2)]
                QrT = [k.sb(p2, f"QrT{i}", [128, 512], BF16) for i in range(2)]
                for qq in QrT:
                    t.op("pool", lambda e, qq=qq: e.memset(qq.t[64:128, :], 0.0), reads=[], writes=[qq.r])
                NPT = 6
                pt = [k.sb(p2, f"pt{i}", [128, 512], BF16) for i in range(NPT)]
                rec = k.sb(p2, "rec", [128, 512], F32)
                otmp = k.sb(p2, "otmp", [128, 512], F32)
                psS = [k.ps(p2, f"psS{i}", [128, 512], F32) for i in range(3)]
                psO = [k.ps(p2, f"psO{i}", [128, 512], F32) for i in range(2)]
                psD = [k.ps(p2, f"psD{i}", [128, 512], F32) for i in range(1)]
                accsb = k.sb(p2, "accsb", [128, 512], F32)
                accP = k.sb(p2, "accP", [128, 512], F32)
                psQ = k.ps(p2, "psQ", [128, 2, 256], F32)
                psQT = k.ps(p2, "psQT", [128, 2, 512], BF16)

                nkb = L // 128
                qi0 = 0
                for _ in range(1):
                    pass

                items = [(h, qt) for h in range(NH) for qt in range(nblk)]

                def prologue(idx):
                    h, qt = items[idx]
                    g0 = row0 + qt * 512
                    bi = (qi0 + idx) % 2
                    cq, sgt, qn, qr = cqi[bi], sgi[bi], QnT[bi], QrT[bi]

                    def p0():
                        t.dma("sp", cq.t[:], scr["CQ"][:, g0:g0 + 512].rearrange("(c p) t -> p c t", p=128), writes=[cq.r])
                        t.dma("sp", sgt.t[:], scr["SG"][h * 128:(h + 1) * 128, g0:g0 + 512], writes=[sgt.r])

                    def phalf(half):
                        for jj in range(2):
                            j = half * 2 + jj
                            for kc in range(3):
                                t.op("pe", lambda e, kc=kc, j=j, jj=jj: e.matmul(
                                    out=psQ.t[:, jj, 0:192], lhsT=cq.t[:, kc, j * 128:(j + 1) * 128], rhs=w_q.t[:, kc, h * 192:(h + 1) * 192],
                                    start=(kc == 0), stop=(kc == 2)), reads=[cq.r, w_q.r], writes=[psQ.r], inc=(kc == 2 and jj == 1))
                        j0 = half * 2
                        ti0 = qt * 4 + j0
                        t.op("act", lambda e: e.copy(out=qtok.t[:, j0:j0 + 2, 0:128], in_=psQ.t[:, :, 0:128]), reads=[psQ.r], writes=[qtok.r])
                        t.op("dve", lambda e: e.tensor_tensor(out=qa.t[:, j0:j0 + 2, :], in0=psQ.t[:, :, 128:192], in1=cos2.t[:, ti0:ti0 + 2, :], op=ALU.mult),
                             reads=[psQ.r, cos2.r], writes=[qa.r])
                        t.op("dve", lambda e: e.tensor_tensor(out=qb.t[:, j0:j0 + 2, 0:32], in0=psQ.t[:, :, 160:192], in1=sinpm.t[:, ti0:ti0 + 2, 0:32], op=ALU.mult),
                             reads=[psQ.r, sinpm.r], writes=[qb.r])
                        t.op("dve", lambda e: e.tensor_tensor(out=qb.t[:, j0:j0 + 2, 32:64], in0=psQ.t[:, :, 128:160], in1=sinpm.t[:, ti0:ti0 + 2, 32:64], op=ALU.mult),
                             reads=[psQ.r, sinpm.r], writes=[qb.r])
                        t.op("dve", lambda e: e.tensor_tensor(out=qtok.t[:, j0:j0 + 2, 128:192], in0=qa.t[:, j0:j0 + 2, :], in1=qb.t[:, j0:j0 + 2, :], op=ALU.add),
                             reads=[qa.r, qb.r], writes=[qtok.r])

                    def p3():
                        for j in range(4):
                            t.op("pe", lambda e, j=j: e.transpose(out=psQT.t[:, 0, j * 128:(j + 1) * 128], in_=qtok.t[:, j, 0:128], identity=ident.t[:]),
                                 reads=[qtok.r, ident.r], writes=[psQT.r], inc=False)
                            t.op("pe", lambda e, j=j: e.transpose(out=psQT.t[0:64, 1, j * 128:(j + 1) * 128], in_=qtok.t[:, j, 128:192], identity=ident.t[:]),
                                 reads=[qtok.r, ident.r], writes=[psQT.r], inc=(j == 3))
                        t.op("act", lambda e: e.copy(out=qn.t[:], in_=psQT.t[:, 0, :]), reads=[psQT.r], writes=[qn.r])
                        t.op("act", lambda e: e.copy(out=qr.t[0:64, :], in_=psQT.t[0:64, 1, :]), reads=[psQT.r], writes=[qr.r])

                    return [p0, lambda: phalf(0), lambda: phalf(1), p3]

                def kv_head(h):
                    banks = [psQ.t[:].rearrange("p a b -> p (a b)"), psS[0].t[:], psS[1].t[:], psS[2].t[:]]
                    bres = [psQ.r, psS[0].r, psS[1].r, psS[2].r]
                    n = 0
                    for b in range(nblk):
                        bk, br = banks[n % 4], bres[n % 4]
                        n += 1
                        for kc in range(2):
                            t.op("pe", lambda e, kc=kc, b=b, bk=bk: e.matmul(
                                out=bk, lhsT=w_kv.t[:, kc, h * 256:h * 256 + 128],
                                rhs=ckvT.t[:, kc, b * 512:(b + 1) * 512], start=(kc == 0), stop=(kc == 1)),
                                reads=[w_kv.r], writes=[br], inc=(kc == 1))
                        t.op("dve", lambda e, b=b, bk=bk: e.tensor_copy(out=KT.t[:, b * 512:(b + 1) * 512], in_=bk),
                             reads=[br], writes=[KT.r])
                    for b in range(nblk):
                        bk, br = banks[n % 4], bres[n % 4]
                        n += 1
                        for j in range(4):
                            ti = b * 4 + j
                            for kc in range(2):
                                t.op("pe", lambda e, kc=kc, ti=ti, j=j, bk=bk: e.matmul(
                                    out=bk[:, j * 128:(j + 1) * 128], lhsT=ckvT.t[:, kc, ti * 128:(ti + 1) * 128],
                                    rhs=w_kv.t[:, kc, h * 256 + 128:h * 256 + 256], start=(kc == 0), stop=(kc == 1)),
                                    reads=[w_kv.r], writes=[br], inc=(kc == 1 and j == 3))
                        t.op("act", lambda e, b=b, bk=bk: e.copy(out=V.t[:, b * 4:(b + 1) * 4, :].rearrange("p a b -> p (a b)"), in_=bk),
                             reads=[br], writes=[V.r])

                for pc in prologue(0):
                    pc()
                for idx, (h, qt) in enumerate(items):
                    if qt == 0:
                        kv_head(h)
                    g0 = row0 + qt * 512
                    bi = (qi0 + idx) % 2
                    sgt, og, qn, qr = sgi[bi], ogo[bi], QnT[bi], QrT[bi]
                    pO, pD = psO[bi], psD[0]
                    nxt = prologue(idx + 1) if idx + 1 < len(items) else []
                    when = {0: 0, max(1, nkb // 4): 1, max(2, nkb // 2): 2, max(3, (3 * nkb) // 4): 3}

                    def issue_S(kb, qn=qn, qr=qr):
                        pS = psS[kb % 3]
                        t.op("pe", lambda e: e.matmul(out=pS.t[:], lhsT=KT.t[:, kb * 128:(kb + 1) * 128], rhs=qn.t[:], start=True, stop=False),
                             reads=[KT.r, qn.r], writes=[pS.r], inc=False)
                        t.op("pe", lambda e: e.matmul(out=pS.t[:], lhsT=krT.t[:, kb * 128:(kb + 1) * 128], rhs=qr.t[:], start=False, stop=True),
                             reads=[qr.r], writes=[pS.r], inc=True)
                        p = pt[kb % NPT]
                        t.op("act", lambda e: e.activation(out=p.t[:], in_=pS.t[:], func=AF.Exp, scale=ATTN_SCALE), reads=[pS.r], writes=[p.r])

                    def issue_O(kb, pO=pO, pD=pD):
                        p = pt[kb % NPT]
                        t.op("pe", lambda e: e.matmul(out=pO.t[:], lhsT=V.t[:, kb, :], rhs=p.t[:], start=(kb == 0), stop=(kb == nkb - 1)),
                             reads=[V.r, p.r], writes=[pO.r], inc=(not ATT_ONES))
                        if ATT_ONES:
                            t.op("pe", lambda e: e.matmul(out=pD.t[:], lhsT=ones.t[:], rhs=p.t[:], start=(kb == 0), stop=(kb == nkb - 1)),
                                 reads=[ones.r, p.r], writes=[pD.r], inc=True)
                        elif kb % POOL_EVERY == POOL_EVERY - 1:
                            if kb == POOL_EVERY - 1:
                                t.op("pool", lambda e: e.tensor_copy(out=accP.t[:], in_=p.t[:]), reads=[p.r], writes=[accP.r])
                            else:
                                t.op("pool", lambda e: e.tensor_tensor(out=accP.t[:], in0=accP.t[:], in1=p.t[:], op=ALU.add), reads=[p.r, accP.r], writes=[accP.r])
                        elif kb == 0:
                            t.op("dve", lambda e: e.tensor_copy(out=pD.t[:], in_=p.t[:]), reads=[p.r], writes=[pD.r])
                        else:
                            t.op("dve", lambda e: e.tensor_tensor(out=pD.t[:], in0=pD.t[:], in1=p.t[:], op=ALU.add), reads=[p.r, pD.r], writes=[pD.r])

                    issue_S(0)
                    if nkb > 1:
                        issue_S(1)
                    for kb in range(nkb):
                        if kb + 2 < nkb:
                            issue_S(kb + 2)
                        issue_O(kb)
                        if nxt and kb in when:
                            nxt[when[kb]]()
                    if not ATT_ONES:
                        t.op("dve", lambda e, pD=pD: e.tensor_copy(out=accsb.t[:], in_=pD.t[:]), reads=[pD.r], writes=[accsb.r])
                        t.op("pe", lambda e, pD=pD: e.matmul(out=pD.t[:], lhsT=cst["ones32"].t[:], rhs=accsb.t[:], start=True, stop=False),
                             reads=[cst["ones32"].r, accsb.r], writes=[pD.r], inc=False)
                        t.op("pe", lambda e, pD=pD: e.matmul(out=pD.t[:], lhsT=cst["ones32"].t[:], rhs=accP.t[:], start=False, stop=True),
                             reads=[cst["ones32"].r, accP.r], writes=[pD.r], inc=True)
                    t.op("dve", lambda e, pD=pD: e.reciprocal(out=rec.t[:], in_=pD.t[:]), reads=[pD.r], writes=[rec.r])
                    t.op("dve", lambda e, pO=pO: e.tensor_tensor(out=otmp.t[:], in0=pO.t[:], in1=rec.t[:], op=ALU.mult), reads=[pO.r, rec.r], writes=[otmp.r])
                    t.op("dve", lambda e, og=og, sgt=sgt: e.tensor_tensor(out=og.t[:], in0=otmp.t[:], in1=sgt.t[:], op=ALU.mult), reads=[otmp.r, sgt.r], writes=[og.r])
                    t.dma(STQ, scr["OG"][h * 128:(h + 1) * 128, g0:g0 + 512], og.t[:], reads=[og.r])
                qi0 += len(items)
            t.barrier()

    if DBG_STOP in (1, 2, 11, 12, 13, 14, 15):
        return
    t.barrier()
    with ExitStack() as p3:
        w_o = k.sb(p3, "w_o", [128, 8, D], BF16)
        with ExitStack() as sst:
            stage3 = [k.sb(sst, f"stage{i}", [128, 2048], F32) for i in range(2)]
            load_w_bf16(k, stage3, W["w_out"], w_o, 8, D)
            t.barrier()
        out_proj_ln(k, p3, x_in, x_out, W, w_o, scr["OG"], T_total, tok_perm=None)
    t.barrier()


def out_proj_ln(k, st, x_in, x_out, W, w_o, OGT, T_total, tok_perm=None):
    nc, t = k.nc, k.trk
    lng = k.sb(st, "lng", [128, D], F32)
    lnb = k.sb(st, "lnb", [128, D], F32)
    t.dma("sp", lng.t[:], W["ln_g"].partition_broadcast(128), writes=[lng.r])
    t.dma("sp", lnb.t[:], W["ln_b"].partition_broadcast(128), writes=[lnb.r])
    ogi = [k.sb(st, f"ogi{i}", [128, 8, 512], BF16) for i in range(2)]
    xi = [k.sb(st, f"xi{i}", [128, 4, D], F32) for i in range(2)]
    yo = [k.sb(st, f"yo{i}", [128, 4, D], F32) for i in range(2)]
    zs, bsts, mvs = ln_scratch(k, st)
    psY = [k.ps(st, f"psY{i}", [128, 1024], F32) for i in range(2)]
    for b in range(T_total // 512):
        r0 = b * 512
        og, x, y = ogi[b % 2], xi[b % 2], yo[b % 2]
        t.dma("sp", og.t[:], OGT[:, r0:r0 + 512].rearrange("(h p) t -> p h t", p=128), writes=[og.r])
        t.dma("sp", x.t[:], x_in[r0:r0 + 512, :].rearrange("(j p) d -> p j d", p=128), writes=[x.r])
        for j in range(4):
            pY = psY[j % 2]
            for half in range(2):
                for h in range(8):
                    t.op("pe", lambda e, h=h, j=j, half=half, pY=pY, og=og: e.matmul(
                        out=pY.t[:, half * 512:(half + 1) * 512], lhsT=og.t[:, h, j * 128:(j + 1) * 128], rhs=w_o.t[:, h, half * 512:(half + 1) * 512],
                        start=(h == 0), stop=(h == 7)), reads=[og.r, w_o.r], writes=[pY.r], inc=(h == 7 and half == 1))
            ln_tail(k, x.t[:, j, :], x.r, pY, zs, bsts, mvs, lng, lnb, y.t[:, j, :], y.r, j)
        t.dma(STQ, x_out[r0:r0 + 512, :].rearrange("(j p) d -> p j d", p=128), y.t[:], reads=[y.r])


def ln_tail(k, x_ap, x_r, pY, zs, bsts, mvs, lng, lnb, y_ap, y_r, i):
    t = k.trk
    z, z2, bst, mv = zs[0][i % 2], zs[1][i % 2], bsts[i % 2], mvs[i % 2]
    t.op("dve", lambda e: e.scalar_tensor_tensor(out=z.t[:], in0=x_ap, scalar=ALPHA, in1=pY.t[:], op0=ALU.mult, op1=ALU.add),
         reads=[x_r, pY.r], writes=[z.r])
    t.op("dve", lambda e: e.bn_stats(out=bst.t[:, 0, :], in_=z.t[:, 0:512]), reads=[z.r], writes=[bst.r])
    t.op("dve", lambda e: e.bn_stats(out=bst.t[:, 1, :], in_=z.t[:, 512:1024]), reads=[z.r], writes=[bst.r])
    t.op("dve", lambda e: e.bn_aggr(out=mv.t[:, 0:2], in_=bst.t[:].rearrange("p a b -> p (a b)")), reads=[bst.r], writes=[mv.r])
    t.op("act", lambda e: e.activation(out=mv.t[:, 2:3], in_=mv.t[:, 1:2], func=AF.Sqrt, bias=k.eps_ln.t[:, 0:1]), reads=[mv.r], writes=[mv.r])
    t.op("dve", lambda e: e.reciprocal(out=mv.t[:, 3:4], in_=mv.t[:, 2:3]), reads=[mv.r], writes=[mv.r])
    t.op("dve", lambda e: e.tensor_scalar(out=z.t[:], in0=z.t[:], scalar1=mv.t[:, 0:1], scalar2=mv.t[:, 3:4], op0=ALU.subtract, op1=ALU.mult),
         reads=[z.r, mv.r], writes=[z.r])
    t.op("dve", lambda e: e.tensor_tensor(out=z2.t[:], in0=z.t[:], in1=lng.t[:], op=ALU.mult), reads=[z.r, lng.r], writes=[z2.r])
    t.op("pool", lambda e: e.tensor_tensor(out=y_ap, in0=z2.t[:], in1=lnb.t[:], op=ALU.add), reads=[z2.r, lnb.r], writes=[y_r])


def ln_scratch(k, st):
    zs = ([k.sb(st, f"z{i}", [128, D], F32) for i in range(2)], [k.sb(st, f"zz{i}", [128, D], F32) for i in range(2)])
    bsts = [k.sb(st, f"bst{i}", [128, 2, 6], F32) for i in range(2)]
    mvs = [k.sb(st, f"mv{i}", [128, 4], F32) for i in range(2)]
    return zs, bsts, mvs


TWO_PI = 2.0 * math.pi
GELU_C = 2.0 * math.sqrt(2.0 / math.pi)


def s5_setup(k, cst, W, SW, lam):
    nc, t = k.nc, k.trk
    id32, maskf, maskb = cst["id32"], cst["maskf"], cst["maskb"]
    dve = lambda fn, rd, wr: t.op("dve", fn, reads=rd, writes=wr)
    with ExitStack() as su:
        rs = Res("s5small")
        SM = k.sb(su, "SM", [128, 40, 64], F32)
        PW = k.sb(su, "PW", [128, 16, 2, 64], F32)
        BT = k.sb(su, "BT", [128, 2, 2, 16, 32], F32)
        CT = k.sb(su, "CT", [128, 2, 2, 16, 32], F32)
        BB = k.sb(su, "BB", [128, 2, 2, 16, 32], F32)
        W3 = k.sb(su, "W3", [128, 2, 32, 128], BF16)
        W1T = k.sb(su, "W1T", [128, 2, 32, 128], BF16)
        TOE = k.sb(su, "TOE", [128, 64, 128], BF16)
        TAC = k.sb(su, "TAC", [128, 64, 128], F32)
        dcol = k.sb(su, "dcol", [128, 64], F32)
        psA = k.ps(su, "psA", [128, 512], F32)
        psB = k.ps(su, "psB", [128, 512], F32)
        sm = lambda i: SM.t[:, i, :]
        AR, AI, LS, STEP, LR, LI, M_, R_, TMP, TH, T2, SN, CS, T1, T2b, T3, LBr, LBi, DEN, MUr, MUi, NR, Qr, Qi = range(24)

        with ExitStack() as s1:
            praw = k.sb(s1, "praw", [32, 3, 2, 128], F32)
            ls = k.sb(s1, "ls", [32, 2, 2], F32)
            zer = k.sb(s1, "zer", [32, 64], F32)
            for d in range(2):
                t.dma("sp", praw.t[:, 0, d, :], W["a_re"][d].rearrange("(gb g2) n -> gb (g2 n)", g2=2), writes=[praw.r])
                t.dma("sp", praw.t[:, 1, d, :], W["a_im"][d].rearrange("(gb g2) n -> gb (g2 n)", g2=2), writes=[praw.r])
                t.dma("sp", ls.t[:, d, :], W["log_step"][d].rearrange("(gb g2) -> gb g2", g2=2), writes=[ls.r])
            dve(lambda e: e.memset(zer.t[:], 0.0), [], [zer.r])
            for d in range(2):
                for g2 in range(2):
                    dve(lambda e, d=d, g2=g2: e.tensor_scalar(out=praw.t[:, 2, d, g2 * 64:(g2 + 1) * 64], in0=zer.t[:], scalar1=ls.t[:, d, g2:g2 + 1],
                                                              scalar2=None, op0=ALU.add), [zer.r, ls.r, praw.r], [praw.r])
            for j in range(3):
                for d in range(2):
                    i = j * 2 + d
                    t.op("pe", lambda e, j=j, d=d, i=i: e.transpose(out=psA.t[:, i * 32:(i + 1) * 32], in_=praw.t[:, j, d, :], identity=id32.t[0:32, 0:32]),
                         reads=[praw.r, id32.r], writes=[psA.r], inc=(i == 5))
            dve(lambda e: e.tensor_copy(out=SM.t[:, 0:3, :].rearrange("q a b -> q (a b)"), in_=psA.t[:, 0:192]), [psA.r], [rs])

        t.barrier()

        if DBG_STOP == 31:
            return
        def tt(o, a, b, op):
            dve(lambda e: e.tensor_tensor(out=sm(o), in0=sm(a), in1=sm(b), op=op), [rs], [rs])

        def ts(o, a, s1_, s2_, op0, op1=None):
            if op1 is None:
                dve(lambda e: e.tensor_scalar(out=sm(o), in0=sm(a), scalar1=s1_, scalar2=None, op0=op0), [rs], [rs])
            else:
                dve(lambda e: e.tensor_scalar(out=sm(o), in0=sm(a), scalar1=s1_, scalar2=s2_, op0=op0, op1=op1), [rs], [rs])

        t.op("act", lambda e: e.activation(out=sm(STEP), in_=sm(LS), func=AF.Exp), reads=[rs], writes=[rs])
        tt(LR, AR, STEP, ALU.mult)
        tt(LI, AI, STEP, ALU.mult)
        ts(M_, LR, 1.0 / 120, 1.0 / 24, ALU.mult, ALU.add)
        for cco in (1.0 / 6, 0.5, 1.0, 1.0):
            tt(M_, M_, LR, ALU.mult)
            ts(M_, M_, cco, None, ALU.add)
        ts(R_, LI, 1.0, None, ALU.mult)
        for m in range(1, 6):
            ts(TMP, LI, TWO_PI * m, -TWO_PI, ALU.is_ge, ALU.mult)
            tt(R_, R_, TMP, ALU.add)
        ts(TH, R_, -math.pi, 0.125, ALU.add, ALU.mult)
        tt(T2, TH, TH, ALU.mult)
        ts(SN, T2, -1.0 / 5040, 1.0 / 120, ALU.mult, ALU.add)
        for cco in (-1.0 / 6, 1.0):
            tt(SN, SN, T2, ALU.mult)
            ts(SN, SN, cco, None, ALU.add)
        tt(SN, SN, TH, ALU.mult)
        ts(CS, T2, 1.0 / 40320, -1.0 / 720, ALU.mult, ALU.add)
        for cco in (1.0 / 24, -0.5, 1.0):
            tt(CS, CS, T2, ALU.mult)
            ts(CS, CS, cco, None, ALU.add)
        for _ in range(3):
            tt(T3, CS, SN, ALU.mult)
            tt(T1, CS, CS, ALU.mult)
            tt(T2b, SN, SN, ALU.mult)
            tt(CS, T1, T2b, ALU.subtract)
            ts(SN, T3, 2.0, None, ALU.mult)
        tt(LBr, M_, CS, ALU.mult)
        ts(LBr, LBr, -1.0, None, ALU.mult)
        tt(LBi, M_, SN, ALU.mult)
        ts(LBi, LBi, -1.0, None, ALU.mult)
        tt(T1, LBr, LBr, ALU.mult)
        tt(T2b, LBi, LBi, ALU.mult)
        tt(DEN, T1, T2b, ALU.add)
        dve(lambda e: e.reciprocal(out=sm(DEN), in_=sm(DEN)), [rs], [rs])
        tt(MUr, LBr, DEN, ALU.mult)
        tt(MUi, LBi, DEN, ALU.mult)
        ts(MUi, MUi, -1.0, None, ALU.mult)
        ts(NR, LBr, -1.0, None, ALU.add)
        tt(T1, AR, AR, ALU.mult)
        tt(T2b, AI, AI, ALU.mult)
        tt(DEN, T1, T2b, ALU.add)
        dve(lambda e: e.reciprocal(out=sm(DEN), in_=sm(DEN)), [rs], [rs])
        tt(T1, NR, AR, ALU.mult)
        tt(T2b, LBi, AI, ALU.mult)
        tt(Qr, T1, T2b, ALU.add)
        tt(Qr, Qr, DEN, ALU.mult)
        tt(T1, LBi, AR, ALU.mult)
        tt(T2b, NR, AI, ALU.mult)
        tt(Qi, T1, T2b, ALU.subtract)
        tt(Qi, Qi, DEN, ALU.mult)
        pw = lambda kk, ri: PW.t[:, kk, ri, :]
        dve(lambda e: e.memset(pw(7, 0), 1.0), [rs], [rs])
        dve(lambda e: e.memset(pw(7, 1), 0.0), [rs], [rs])

        def cmul_pw(ko, ki, br, bi):
            dve(lambda e: e.tensor_tensor(out=sm(T1), in0=pw(ki, 0), in1=sm(br), op=ALU.mult), [rs], [rs])
            dve(lambda e: e.tensor_tensor(out=sm(T2b), in0=pw(ki, 1), in1=sm(bi), op=ALU.mult), [rs], [rs])
            dve(lambda e: e.tensor_tensor(out=pw(ko, 0), in0=sm(T1), in1=sm(T2b), op=ALU.subtract), [rs], [rs])
            dve(lambda e: e.tensor_tensor(out=sm(T1), in0=pw(ki, 0), in1=sm(bi), op=ALU.mult), [rs], [rs])
            dve(lambda e: e.tensor_tensor(out=sm(T2b), in0=pw(ki, 1), in1=sm(br), op=ALU.mult), [rs], [rs])
            dve(lambda e: e.tensor_tensor(out=pw(ko, 1), in0=sm(T1), in1=sm(T2b), op=ALU.add), [rs], [rs])

        for kk in range(7, 15):
            cmul_pw(kk + 1, kk, LBr, LBi)
        for kk in range(7, 0, -1):
            cmul_pw(kk - 1, kk, MUr, MUi)
        for d in range(2):
            for r in range(2):
                dve(lambda e, d=d, r=r: e.tensor_copy(out=lam["A"].t[:, d, r, :], in_=PW.t[:, 15, 0, d * 32:(d + 1) * 32]), [rs], [lam["A"].r])
            dve(lambda e, d=d: e.tensor_copy(out=lam["I"].t[:, d, :], in_=PW.t[:, 15, 1, d * 32:(d + 1) * 32]), [rs], [lam["I"].r])
            dve(lambda e, d=d: e.tensor_copy(out=lam["B"].t[:, d, 1, :], in_=PW.t[:, 15, 1, d * 32:(d + 1) * 32]), [rs], [lam["B"].r])
            dve(lambda e, d=d: e.tensor_scalar(out=lam["B"].t[:, d, 0, :], in0=PW.t[:, 15, 1, d * 32:(d + 1) * 32], scalar1=-1.0, scalar2=None, op0=ALU.mult),
                [rs, lam["B"].r], [lam["B"].r])
            dve(lambda e, d=d: e.tensor_scalar(out=lam["NI"].t[:, d, :], in0=PW.t[:, 15, 1, d * 32:(d + 1) * 32], scalar1=-1.0, scalar2=None, op0=ALU.mult),
                [rs], [lam["NI"].r])

        if DBG_STOP == 32:
            return
        with ExitStack() as s2:
            raw = k.sb(s2, "raw", [32, 2, 2048], F32)
            raw2 = k.sb(s2, "raw2", [32, 2, 2048], F32)
            for (src_r, src_i, dst, isC) in ((W["b_re"], W["b_im"], BT, False), (W["c_re"], W["c_im"], CT, True)):
                for d in range(2):
                    pat = "(gb g2) p n -> gb (g2 p n)" if isC else "(gb g2) n p -> gb (g2 n p)"
                    t.dma("sp", raw.t[:, 0, :], src_r[d].rearrange(pat, g2=2), writes=[raw.r])
                    t.dma("sp", raw.t[:, 1, :], src_i[d].rearrange(pat, g2=2), writes=[raw.r])
                    for ri in range(2):
                        ps = psA if ri == 0 else psB
                        if isC:
                            dve(lambda e, ri=ri: e.tensor_copy(out=raw2.t[:, ri, :].rearrange("q (p g n) -> q p g n", p=16, g=2),
                                                               in_=raw.t[:, ri, :].rearrange("q (g p n) -> q p g n", g=2, p=16)), [raw.r], [raw2.r])
                        for pp in range(16):
                            if isC:
                                src = raw2.t[:, ri, pp * 128:(pp + 1) * 128]
                            else:
                                src = raw.t[:, ri, :].rearrange("q (m p) -> q m p", p=16)[:, :, pp]
                            t.op("pe", lambda e, src=src, pp=pp, ps=ps: e.transpose(out=ps.t[:, pp * 32:(pp + 1) * 32], in_=src, identity=id32.t[0:32, 0:32]),
                                 reads=[raw.r, raw2.r, id32.r], writes=[ps.r], inc=(pp == 15))
                        dve(lambda e, d=d, ri=ri, ps=ps, dst=dst: e.tensor_copy(out=dst.t[:, d, ri, :, :].rearrange("q a b -> q (a b)"), in_=ps.t[:]), [ps.r], [dst.r])
        t.barrier()
        if DBG_STOP == 33:
            return
        with ExitStack() as s3:
            u1 = k.sb(s3, "u1", [128, 16, 32], F32)
            u2 = k.sb(s3, "u2", [128, 16, 32], F32)
            for d in range(2):
                qr = SM.t[:, Qr, d * 32:(d + 1) * 32].unsqueeze(1).broadcast_to([128, 16, 32])
                qi = SM.t[:, Qi, d * 32:(d + 1) * 32].unsqueeze(1).broadcast_to([128, 16, 32])
                br, bi = BT.t[:, d, 0, :, :], BT.t[:, d, 1, :, :]
                dve(lambda e: e.tensor_tensor(out=u1.t[:], in0=br, in1=qr, op=ALU.mult), [rs, BT.r], [u1.r])
                dve(lambda e: e.tensor_tensor(out=u2.t[:], in0=bi, in1=qi, op=ALU.mult), [rs, BT.r], [u2.r])
                dve(lambda e, d=d: e.tensor_tensor(out=BB.t[:, d, 0, :, :], in0=u1.t[:], in1=u2.t[:], op=ALU.subtract), [u1.r, u2.r], [BB.r])
                dve(lambda e: e.tensor_tensor(out=u1.t[:], in0=bi, in1=qr, op=ALU.mult), [rs, BT.r, BB.r], [u1.r])
                dve(lambda e: e.tensor_tensor(out=u2.t[:], in0=br, in1=qi, op=ALU.mult), [rs, BT.r, BB.r], [u2.r])
                dve(lambda e, d=d: e.tensor_tensor(out=BB.t[:, d, 1, :, :], in0=u1.t[:], in1=u2.t[:], op=ALU.add), [u1.r, u2.r], [BB.r])
        t.barrier()
        with nc.allow_non_contiguous_dma(reason="tiny param relayout"):
            for s in range(8):
                t.dma("sp", dcol.t[16 * s:16 * s + 16, :], W["d"].rearrange("(g p) -> p g", p=16), writes=[dcol.r])
        if DBG_STOP == 34:
            return
        with ExitStack() as s4:
            V1 = k.sb(s4, "V1", [128, 2, 16, 8, 16], F32)
            Z = k.sb(s4, "Z", [128, 2, 16, 8, 16], F32)
            v1 = k.sb(s4, "v1", [128, 16, 16], F32)
            v2 = k.sb(s4, "v2", [128, 16, 16], F32)
            tq = k.sb(s4, "tq", [128, 4, 128], F32)
            for d in range(2):
                for h2 in range(2):
                    g0 = h2 * 16
                    pwv = lambda kk, ri: PW.t[:, kk, ri, d * 32 + g0:d * 32 + g0 + 16].unsqueeze(2).broadcast_to([128, 16, 16])
                    bbv = lambda ri: BB.t[:, d, ri, :, g0:g0 + 16].rearrange("q p g -> q g p")
                    ctv = lambda ri: CT.t[:, d, ri, :, g0:g0 + 16].rearrange("q p g -> q g p")

                    def cprod(out_r, out_i, a_r, a_i, kk, neg_i, rd):
                        wr = rd[-1:]
                        dve(lambda e: e.tensor_tensor(out=v1.t[:], in0=a_r, in1=pwv(kk, 0), op=ALU.mult), [rs] + rd[:-1], [v1.r])
                        dve(lambda e: e.tensor_tensor(out=v2.t[:], in0=a_i, in1=pwv(kk, 1), op=ALU.mult), [rs] + rd[:-1], [v2.r])
                        dve(lambda e: e.tensor_tensor(out=out_r, in0=v1.t[:], in1=v2.t[:], op=ALU.subtract), [v1.r, v2.r], wr)
                        dve(lambda e: e.tensor_tensor(out=v1.t[:], in0=a_r, in1=pwv(kk, 1), op=ALU.mult), [rs] + rd[:-1] + wr, [v1.r])
                        dve(lambda e: e.tensor_tensor(out=v2.t[:], in0=a_i, in1=pwv(kk, 0), op=ALU.mult), [rs] + rd[:-1] + wr, [v2.r])
                        if neg_i:
                            dve(lambda e: e.scalar_tensor_tensor(out=out_i, in0=v1.t[:], scalar=-1.0, in1=v2.t[:], op0=ALU.mult, op1=ALU.subtract), [v1.r, v2.r], wr)
                        else:
                            dve(lambda e: e.tensor_tensor(out=out_i, in0=v1.t[:], in1=v2.t[:], op=ALU.add), [v1.r, v2.r], wr)

                    for s in range(8):
                        kk = (14 - s) if d == 0 else (s + 7)
                        cprod(V1.t[:, 0, :, s, :], V1.t[:, 1, :, s, :], bbv(0), bbv(1), kk, False, [BB.r, V1.r])
                        kz = s if d == 0 else (7 - s)
                        cprod(Z.t[:, 0, :, s, :], Z.t[:, 1, :, s, :], ctv(0), ctv(1), kz, True, [CT.r, Z.r])
                        kw = (s + 8) if d == 0 else (15 - s)
                        w3v = lambda ri: W3.t[:, ri, g0:g0 + 16, s * 16:(s + 1) * 16]
                        cprod(w3v(0), w3v(1), ctv(0), ctv(1), kw, True, [CT.r, W3.r])
                    for ri in range(2):
                        for gq in range(4):
                            ps = psA if (gq % 2 == 0) else psB
                            for gl in range(4):
                                gbl = gq * 4 + gl
                                t.op("pe", lambda e, ri=ri, gbl=gbl, gl=gl, ps=ps: e.transpose(
                                    out=ps.t[:, gl * 128:(gl + 1) * 128], in_=V1.t[:, ri, gbl, :, :].rearrange("q s p -> q (s p)"), identity=id32.t[:]),
                                    reads=[V1.r, id32.r], writes=[ps.r], inc=(gl == 3))
                            gb0 = g0 + gq * 4
                            dve(lambda e, ri=ri, gb0=gb0, ps=ps: e.tensor_copy(out=W1T.t[:, ri, gb0:gb0 + 4, :].rearrange("q a b -> q (a b)"), in_=ps.t[:]),
                                [ps.r], [W1T.r])
                    for gq in range(4):
                        for g2 in range(2):
                            ps = psA if g2 == 0 else psB
                            pr = slice(g2 * 64, (g2 + 1) * 64)
                            for gl in range(4):
                                gbl = gq * 4 + gl
                                for ri in range(2):
                                    t.op("pe", lambda e, ri=ri, gbl=gbl, pr=pr, gl=gl, ps=ps: e.matmul(
                                        out=ps.t[:, gl * 128:(gl + 1) * 128], lhsT=V1.t[pr, ri, gbl, :, :].rearrange("q s p -> q (s p)"),
                                        rhs=Z.t[pr, ri, gbl, :, :].rearrange("q s p -> q (s p)"), start=(ri == 0), stop=(ri == 1)),
                                        reads=[V1.r, Z.r], writes=[ps.r], inc=(ri == 1 and gl == 3))
                        for g2 in range(2):
                            ps = psA if g2 == 0 else psB
                            gg0 = 2 * (g0 + gq * 4) + g2
                            msk = (maskf if d == 0 else maskb).t[:].unsqueeze(1).broadcast_to([128, 4, 128])
                            psv = ps.t[:].rearrange("q (a b) -> q a b", a=4)
                            tav = TAC.t[:, 2 * (g0 + gq * 4):2 * (g0 + gq * 4) + 8, :].rearrange("q (a two) b -> q a two b", two=2)[:, :, g2, :]
                            if d == 0:
                                dve(lambda e, tav=tav, psv=psv, msk=msk: e.tensor_tensor(out=tav, in0=psv, in1=msk, op=ALU.mult),
                                    [ps.r, maskf.r], [TAC.r])
                            else:
                                dve(lambda e, psv=psv, msk=msk: e.tensor_tensor(out=tq.t[:], in0=psv, in1=msk, op=ALU.mult), [ps.r, maskb.r], [tq.r])
                                dve(lambda e, tav=tav: e.tensor_tensor(out=tav, in0=tav, in1=tq.t[:], op=ALU.add),
                                    [tq.r, TAC.r], [TAC.r])
                t.dma(STQ, SW["W1T"][:, d], W1T.t[:], reads=[W1T.r])
                t.dma(STQ, SW["W3"][:, d], W3.t[:], reads=[W3.r])
        t.barrier()
        for g in range(64):
            dve(lambda e, g=g: e.scalar_tensor_tensor(out=TOE.t[:, g, :], in0=id32.t[:], scalar=dcol.t[:, g:g + 1], in1=TAC.t[:, g, :], op0=ALU.mult, op1=ALU.add),
                [id32.r, dcol.r, TAC.r], [TOE.r])
        t.dma(STQ, SW["TOEP"], TOE.t[:], reads=[TOE.r])
    t.barrier()


def s5_scan_sweep(k, cst, x_in, W, seqs, SW, lam, YS, YG, d, stage):
    nc, t = k.nc, k.trk
    ident = cst["ident"]
    with ExitStack() as sw:
        w_u = k.sb(sw, "w_u", [128, 8, D], BF16)
        with ExitStack() as sst:
            stage = [k.sb(sst, f"stage{i}", [128, 2048], F32) for i in range(2)]
            load_w_bf16(k, stage, W["w_in"][:, 0:D], w_u, 8, D)
            t.barrier()
        W1T = k.sb(sw, "W1Ts", [128, 2, 32, 128], BF16)
        W3 = k.sb(sw, "W3s", [128, 2, 32, 128], BF16)
        t.dma("sp", W1T.t[:], SW["W1T"][:, d], writes=[W1T.r])
        t.dma("sp", W3.t[:], SW["W3"][:, d], writes=[W3.r])
        if d == 0:
            TOE = k.sb(sw, "TOEs", [128, 64, 128], BF16)
            t.dma("sp", TOE.t[:], SW["TOEP"], writes=[TOE.r])
        xq = k.sb(sw, "xq", [128, D], F32)
        xcb = k.sb(sw, "xcb", [128, D], BF16)
        xTp = k.sb(sw, "xTp", [128, 8, 128], BF16)
        ucq = k.sb(sw, "ucq", [128, 64, 4, 16], BF16)
        Us = [k.sb(sw, f"U{i}", [128, 64, 128], BF16) for i in range(2)]
        XHs = [k.sb(sw, f"XH{i}", [128, 130, 2, 32], F32) for i in range(2)]
        Hb = k.sb(sw, "Hb", [128, 32, 2, 130], BF16)
        t1s = [k.sb(sw, f"t1{i}", [128, 2, 32], F32) for i in range(2)]
        t2s = [k.sb(sw, f"t2{i}", [128, 2, 32], F32) for i in range(2)]
        t2ra = [Res(f"t2ra{i}") for i in range(2)]
        t2rb = [Res(f"t2rb{i}") for i in range(2)]
        carry = k.sb(sw, "carry", [128, 2, 32], F32)
        ycm = k.sb(sw, "ycm", [128, 8, 256], F32)
        if d == 1:
            yfh = k.sb(sw, "yfh", [128, 8, 256], F32)
            gt = yfh
            ygb = k.sb(sw, "ygb", [128, 8, 256], BF16)
        psT = [k.ps(sw, "psT0", [128, 8, 128], BF16)] * 2
        psU = [k.ps(sw, f"psU{i}", [128, 512], F32) for i in range(2)]
        psUTs = [k.ps(sw, f"psUT{i}", [128, 8, 128], BF16) for i in range(2)]
        psX = k.ps(sw, "psX", [128, 4, 128], F32)
        psY = [k.ps(sw, f"psY{i}", [128, 4, 128], F32) for i in range(2)]
        LA, LI_, LNI = lam["A"], lam["I"], lam["NI"]
        for XH in XHs:
            t.op("dve", lambda e, XH=XH: e.memset(XH.t[:], 0.0), reads=[], writes=[XH.r])
        xoff = 1 if d == 0 else 0
        cin = 0 if d == 0 else 128
        cout = 128 if d == 0 else 0
        hoff = 0 if d == 0 else 1

        segs = []
        for (row0, L) in seqs:
            nseg = L // 1024
            order = range(nseg) if d == 0 else range(nseg - 1, -1, -1)
            for si, s in enumerate(order):
                segs.append((row0 + s * 1024, si == 0))

        def stage_a(i):
            r0, _first = segs[i]
            U, XH = Us[i % 2], XHs[i % 2]
            xv = x_in[r0:r0 + 1024, :].rearrange("(c s) d -> c s d", s=8)
            for q in range(8):
                t.dma("sp", xq.t[:], xv[:, q, :], writes=[xq.r])
                t.op("act", lambda e: e.copy(out=xcb.t[:], in_=xq.t[:]), reads=[xq.r], writes=[xcb.r])
                pT = psT[q % 2]
                for kc in range(8):
                    t.op("pe", lambda e, kc=kc, pT=pT: e.transpose(out=pT.t[:, kc, :], in_=xcb.t[:, kc * 128:(kc + 1) * 128], identity=ident.t[:]),
                         reads=[xcb.r, ident.r], writes=[pT.r], inc=(kc == 7))
                t.op("act", lambda e, pT=pT: e.copy(out=xTp.t[:], in_=pT.t[:]), reads=[pT.r], writes=[xTp.r])
                for half in range(2):
                    pU = psU[half]
                    for kc in range(8):
                        t.op("pe", lambda e, kc=kc, half=half, pU=pU: e.matmul(out=pU.t[:], lhsT=xTp.t[:, kc, :], rhs=w_u.t[:, kc, half * 512:(half + 1) * 512],
                                                                               start=(kc == 0), stop=(kc == 7)), reads=[xTp.r, w_u.r], writes=[pU.r], inc=(kc == 7))
                    t.op("act", lambda e, half=half, pU=pU, q=q: e.copy(out=ucq.t[:, half * 32:(half + 1) * 32, q % 4, :],
                                                                         in_=pU.t[:].rearrange("q (g p) -> q g p", p=16)), reads=[pU.r], writes=[ucq.r])
                if q % 4 != 3:
                    continue
                hq = q // 4
                for gq in range(8):
                    psUT = psUTs[gq % 2]
                    for gl in range(8):
                        g = gq * 8 + gl
                        t.op("pe", lambda e, g=g, gl=gl, hq=hq, psUT=psUT: e.transpose(out=psUT.t[64 * hq:64 * hq + 64, gl, :], in_=ucq.t[:, g, :, :].rearrange("q s p -> q (s p)"), identity=ident.t[:]),
                             reads=[ucq.r, ident.r], writes=[psUT.r], inc=(gl == 7))
                    t.op("act", lambda e, gq=gq, hq=hq, U=U, psUT=psUT: e.copy(out=U.t[64 * hq:64 * hq + 64, gq * 8:gq * 8 + 8, :], in_=psUT.t[64 * hq:64 * hq + 64, :, :]),
                         reads=[psUT.r], writes=[U.r])
            for gp in range(16):
                for gl in range(2):
                    gb = gp * 2 + gl
                    for ri in range(2):
                        for g2 in range(2):
                            t.op("pe", lambda e, gb=gb, gl=gl, ri=ri, g2=g2, U=U: e.matmul(
                                out=psX.t[g2 * 64:(g2 + 1) * 64, gl * 2 + ri, :], lhsT=W1T.t[:, ri, gb, g2 * 64:(g2 + 1) * 64], rhs=U.t[:, 2 * gb + g2, :],
                                start=True, stop=True), reads=[W1T.r, U.r], writes=[psX.r], inc=(gl == 1 and ri == 1 and g2 == 1))
                t.op("act", lambda e, gp=gp, XH=XH: e.copy(
                    out=XH.t[:, xoff:xoff + 128, :, gp * 2:gp * 2 + 2].rearrange("q c r g -> q g r c"),
                    in_=psX.t[:].rearrange("q (g r) c -> q g r c", g=2)), reads=[psX.r], writes=[XH.r])

        def stage_b(i):
            r0, first = segs[i]
            U, XH = Us[i % 2], XHs[i % 2]
            XHp = XHs[(i + 1) % 2]
            if first:
                t.op("dve", lambda e: e.memset(XH.t[:, cin, :, :], 0.0), reads=[], writes=[XH.r])
            else:
                t.op("dve", lambda e: e.tensor_copy(out=XH.t[:, cin, :, :], in_=carry.t[:]), reads=[carry.r, XH.r], writes=[XH.r])
            steps = range(128) if d == 0 else range(127, -1, -1)
            for c in steps:
                cur = c + xoff
                prv = cur - 1 if d == 0 else cur + 1
                ta, tb = t1s[c % 2], t2s[c % 2]
                ra, rb = t2ra[c % 2], t2rb[c % 2]
                t.op("dve", lambda e, prv=prv, ta=ta: e.tensor_tensor(out=ta.t[:], in0=XH.t[:, prv, :, :], in1=LA.t[:, d, :, :], op=ALU.mult), reads=[XH.r, LA.r], writes=[ta.r])
                t.op("dve", lambda e, prv=prv, tb=tb: e.tensor_tensor(out=tb.t[:], in0=XH.t[:, prv, ::-1, :], in1=lam["B"].t[:, d, :, :], op=ALU.mult),
                     reads=[XH.r, lam["B"].r], writes=[ra, rb])
                t.op("dve", lambda e, cur=cur, ta=ta: e.tensor_tensor(out=XH.t[:, cur, :, :], in0=XH.t[:, cur, :, :], in1=ta.t[:], op=ALU.add), reads=[XH.r, ta.r], writes=[XH.r])
                t.op("dve", lambda e, cur=cur, tb=tb: e.tensor_tensor(out=XH.t[:, cur, :, :], in0=XH.t[:, cur, :, :], in1=tb.t[:], op=ALU.add), reads=[XH.r, ra, rb], writes=[XH.r])
            t.op("dve", lambda e: e.tensor_copy(out=carry.t[:], in_=XH.t[:, cout, :, :]), reads=[XH.r], writes=[carry.r])
            t.op("act", lambda e: e.copy(out=Hb.t[:], in_=XH.t[:].rearrange("q c r g -> q g r c")), reads=[XH.r], writes=[Hb.r])
            for qtr in range(4):
                ysl = YS[r0:r0 + 1024, :].rearrange("(c s) d -> c s d", s=8)[:, :, qtr * 256:(qtr + 1) * 256]
                if d == 1:
                    t.dma("sp", yfh.t[:], ysl, writes=[yfh.r])
                for gq in range(2):
                    for g2 in range(2):
                        pY = psY[g2]
                        pr = slice(g2 * 64, (g2 + 1) * 64)
                        for gl in range(4):
                            g = qtr * 16 + gq * 8 + 2 * gl + g2
                            gb = g // 2
                            if d == 0:
                                t.op("pe", lambda e, g=g, gl=gl, pY=pY: e.matmul(out=pY.t[:, gl, :], lhsT=U.t[:, g, :], rhs=TOE.t[:, g, :], start=True, stop=False),
                                     reads=[U.r, TOE.r], writes=[pY.r], inc=False)
                            for ri in range(2):
                                t.op("pe", lambda e, gb=gb, pr=pr, ri=ri, gl=gl, pY=pY: e.matmul(
                                    out=pY.t[:, gl, :], lhsT=Hb.t[pr, gb, ri, hoff:hoff + 128], rhs=W3.t[pr, ri, gb, :],
                                    start=(d == 1 and ri == 0), stop=(ri == 1)), reads=[Hb.r, W3.r], writes=[pY.r], inc=(ri == 1 and gl == 3))
                    for g2 in range(2):
                        pY = psY[g2]
                        oview = ycm.t[:, :, gq * 128:(gq + 1) * 128].rearrange("q t (g two p) -> q g two t p", g=4, two=2)[:, :, g2, :, :]
                        iview = pY.t[:].rearrange("q g (t p) -> q g t p", t=8)
                        if d == 0:
                            t.op("act", lambda e, oview=oview, iview=iview: e.copy(out=oview, in_=iview), reads=[pY.r], writes=[ycm.r])
                        else:
                            fview = yfh.t[:, :, gq * 128:(gq + 1) * 128].rearrange("q t (g two p) -> q g two t p", g=4, two=2)[:, :, g2, :, :]
                            t.op("dve", lambda e, oview=oview, iview=iview, fview=fview: e.tensor_tensor(out=oview, in0=iview, in1=fview, op=ALU.add),
                                 reads=[pY.r, yfh.r], writes=[ycm.r])
                if d == 0:
                    t.dma(STQ, ysl, ycm.t[:], reads=[ycm.r])
                else:
                    t.op("act", lambda e: e.activation(out=gt.t[:], in_=ycm.t[:], func=AF.Square), reads=[ycm.r], writes=[gt.r])
                    t.op("dve", lambda e: e.tensor_scalar(out=gt.t[:], in0=gt.t[:], scalar1=0.044715, scalar2=1.0, op0=ALU.mult, op1=ALU.add), reads=[gt.r], writes=[gt.r])
                    t.op("dve", lambda e: e.tensor_tensor(out=gt.t[:], in0=gt.t[:], in1=ycm.t[:], op=ALU.mult), reads=[gt.r, ycm.r], writes=[gt.r])
                    t.op("act", lambda e: e.activation(out=gt.t[:], in_=gt.t[:], func=AF.Sigmoid, scale=GELU_C), reads=[gt.r], writes=[gt.r])
                    t.op("dve", lambda e: e.tensor_tensor(out=ygb.t[:], in0=gt.t[:], in1=ycm.t[:], op=ALU.mult), reads=[gt.r, ycm.r], writes=[ygb.r])
                    t.dma(STQ, YG[r0:r0 + 1024, :].rearrange("(c s) d -> c s d", s=8)[:, :, qtr * 256:(qtr + 1) * 256], ygb.t[:], reads=[ygb.r])

        stage_a(0)
        for i in range(len(segs)):
            if i + 1 < len(segs):
                stage_a(i + 1)
            stage_b(i)
    t.barrier()


def s5_tail_sweep(k, cst, x_in, x_out, W, YG, T_total, stage):
    nc, t = k.nc, k.trk
    ident = cst["ident"]
    with ExitStack() as sw:
        w_g = k.sb(sw, "w_g", [128, 8, D], BF16)
        w_glu = k.sb(sw, "w_glu", [128, 8, D], BF16)
        w_o = k.sb(sw, "w_o", [128, 8, D], BF16)
        with ExitStack() as sst:
            stage = [k.sb(sst, f"stage{i}", [128, 2048], F32) for i in range(2)]
            load_w_bf16(k, stage, W["w_in"][:, D:2 * D], w_g, 8, D)
            load_w_bf16(k, stage, W["w_glu"], w_glu, 8, D)
            load_w_bf16(k, stage, W["w_out"], w_o, 8, D)
            t.barrier()
        bglu = k.sb(sw, "bglu", [128, 8], F32)
        with nc.allow_non_contiguous_dma(reason="tiny bias relayout"):
            t.dma("sp", bglu.t[:], W["b_glu"].rearrange("(c p) -> p c", p=128), writes=[bglu.r])
        lng = k.sb(sw, "lng", [128, D], F32)
        lnb = k.sb(sw, "lnb", [128, D], F32)
        t.dma("sp", lng.t[:], W["ln_g"].partition_broadcast(128), writes=[lng.r])
        t.dma("sp", lnb.t[:], W["ln_b"].partition_broadcast(128), writes=[lnb.r])
        xi = k.sb(sw, "xi", [128, 4, D], F32)
        xb = k.sb(sw, "xb", [128, 4, D], BF16)
        yi = k.sb(sw, "yi", [128, 4, D], BF16)
        xT = k.sb(sw, "xT", [128, 8, 512], BF16)
        yT = k.sb(sw, "yT", [128, 8, 512], BF16)
        sgms = [k.sb(sw, f"sgm{i}", [128, 512], F32) for i in range(2)]
        sils = [k.sb(sw, f"sil{i}", [128, 512], F32) for i in range(2)]
        y3T = k.sb(sw, "y3T", [128, 8, 512], BF16)
        yos = [k.sb(sw, f"yo{i}", [128, 4, D], F32) for i in range(2)]
        zs, bsts, mvs = ln_scratch(k, sw)
        psT = [k.ps(sw, f"psT{i}", [128, 2, 512], BF16) for i in range(2)]
        psZs = [k.ps(sw, f"psZ{i}", [128, 512], F32) for i in range(2)]
        psGs = [k.ps(sw, f"psG{i}", [128, 512], F32) for i in range(2)]
        psY = [k.ps(sw, "psY0", [128, 1024], F32)] * 2
        for b in range(T_total // 512):
            r0 = b * 512
            yo = yos[b % 2]
            t.dma("sp", xi.t[:], x_in[r0:r0 + 512, :].rearrange("(j p) d -> p j d", p=128), writes=[xi.r])
            t.dma("sp", yi.t[:], YG[r0:r0 + 512, :].rearrange("(j p) d -> p j d", p=128), writes=[yi.r])
            t.op("act", lambda e: e.copy(out=xb.t[:], in_=xi.t[:]), reads=[xi.r], writes=[xb.r])
            for (src, dstT) in ((xb, xT), (yi, yT)):
                for kp in range(4):
                    pT = psT[kp % 2]
                    for kk in range(2):
                        kc = kp * 2 + kk
                        for j in range(4):
                            t.op("pe", lambda e, kc=kc, kk=kk, j=j, pT=pT, src=src: e.transpose(
                                out=pT.t[:, kk, j * 128:(j + 1) * 128], in_=src.t[:, j, kc * 128:(kc + 1) * 128], identity=ident.t[:]),
                                reads=[src.r, ident.r], writes=[pT.r], inc=(kk == 1 and j == 3))
                    t.op("dve", lambda e, kp=kp, pT=pT, dstT=dstT: e.tensor_copy(out=dstT.t[:, kp * 2:kp * 2 + 2, :], in_=pT.t[:]), reads=[pT.r], writes=[dstT.r])
            for co in range(8):
                psZ, psG, sgm, sil = psZs[co % 2], psGs[co % 2], sgms[co % 2], sils[co % 2]
                for kc in range(8):
                    t.op("pe", lambda e, kc=kc, co=co: e.matmul(out=psZ.t[:], lhsT=w_glu.t[:, kc, co * 128:(co + 1) * 128], rhs=yT.t[:, kc, :], start=(kc == 0), stop=(kc == 7)),
                         reads=[w_glu.r, yT.r], writes=[psZ.r], inc=(kc == 7))
                for kc in range(8):
                    t.op("pe", lambda e, kc=kc, co=co: e.matmul(out=psG.t[:], lhsT=w_g.t[:, kc, co * 128:(co + 1) * 128], rhs=xT.t[:, kc, :], start=(kc == 0), stop=(kc == 7)),
                         reads=[w_g.r, xT.r], writes=[psG.r], inc=(kc == 7))
                t.op("act", lambda e, co=co: e.activation(out=sgm.t[:], in_=psZ.t[:], func=AF.Sigmoid, bias=bglu.t[:, co:co + 1]), reads=[psZ.r, bglu.r], writes=[sgm.r])
                t.op("act", lambda e: e.activation(out=sil.t[:], in_=psG.t[:], func=AF.Silu), reads=[psG.r], writes=[sil.r])
                t.op("dve", lambda e: e.tensor_tensor(out=sgm.t[:], in0=sgm.t[:], in1=sil.t[:], op=ALU.mult), reads=[sgm.r, sil.r], writes=[sgm.r])
                t.op("dve", lambda e, co=co: e.tensor_tensor(out=y3T.t[:, co, :], in0=sgm.t[:], in1=yT.t[:, co, :], op=ALU.mult), reads=[sgm.r, yT.r], writes=[y3T.r])
            for j in range(4):
                pY = psY[j % 2]
                for half in range(2):
                    for kc in range(8):
                        t.op("pe", lambda e, kc=kc, j=j, half=half, pY=pY: e.matmul(
                            out=pY.t[:, half * 512:(half + 1) * 512], lhsT=y3T.t[:, kc, j * 128:(j + 1) * 128], rhs=w_o.t[:, kc, half * 512:(half + 1) * 512],
                            start=(kc == 0), stop=(kc == 7)), reads=[y3T.r, w_o.r], writes=[pY.r], inc=(kc == 7 and half == 1))
                ln_tail(k, xi.t[:, j, :], xi.r, pY, zs, bsts, mvs, lng, lnb, yo.t[:, j, :], yo.r, j)
            t.dma(STQ, x_out[r0:r0 + 512, :].rearrange("(j p) d -> p j d", p=128), yo.t[:], reads=[yo.r])
    t.barrier()


def s5_layer(k, cst, x_in, x_out, W, seqs, S5S):
    nc, t = k.nc, k.trk
    T_total = sum(L for _, L in seqs)
    with ExitStack() as ls:
        lam = {"A": k.sb(ls, "lamA", [128, 2, 2, 32], F32), "I": k.sb(ls, "lamI", [128, 2, 32], F32), "NI": k.sb(ls, "lamNI", [128, 2, 32], F32),
               "B": k.sb(ls, "lamB", [128, 2, 2, 32], F32)}
        stage = None
        s5_setup(k, cst, W, S5S, lam)
        if DBG_STOP in (21, 31, 32, 33, 34):
            return
        s5_scan_sweep(k, cst, x_in, W, seqs, S5S, lam, S5S["YS"], S5S["YG"], 0, stage)
        if DBG_STOP == 22:
            return
        s5_scan_sweep(k, cst, x_in, W, seqs, S5S, lam, S5S["YS"], S5S["YG"], 1, stage)
        if DBG_STOP == 23:
            return
        s5_tail_sweep(k, cst, x_in, x_out, W, S5S["YG"], T_total, stage)


def build_program(seqs, layers, ntile_rope):
    T_total = sum(L for _, L in seqs)
    nc = bass.Bass("TRN2", target_bir_lowering=False)
    dt = nc.dram_tensor
    x = dt("x", [T_total, D], F32, kind="ExternalInput").ap()
    y = dt("y", [T_total, D], F32, kind="ExternalOutput").ap()
    n_mla = sum(1 for l in layers if l == "mla")
    n_s5 = sum(1 for l in layers if l == "s5")
    Wd = {}
    if n_mla:
        Wd["mla_w_in"] = dt("mla_w_in", [n_mla, D, MLA_IN], F32, kind="ExternalInput").ap()
        Wd["mla_g_q"] = dt("mla_g_q", [n_mla, QLORA], F32, kind="ExternalInput").ap()
        Wd["mla_w_q_up"] = dt("mla_w_q_up", [n_mla, QLORA, 1536], F32, kind="ExternalInput").ap()
        Wd["mla_g_kv"] = dt("mla_g_kv", [n_mla, KVLORA], F32, kind="ExternalInput").ap()
        Wd["mla_w_kv_up"] = dt("mla_w_kv_up", [n_mla, KVLORA, 2048], F32, kind="ExternalInput").ap()
        Wd["mla_w_out"] = dt("mla_w_out", [n_mla, D, D], F32, kind="ExternalInput").ap()
    if n_s5:
        for nm, shp in (("s5_w_in", [D, 2 * D]), ("s5_a_re", [2, 64, 64]), ("s5_a_im", [2, 64, 64]), ("s5_log_step", [2, 64]),
                        ("s5_b_re", [2, 64, 64, 16]), ("s5_b_im", [2, 64, 64, 16]), ("s5_c_re", [2, 64, 16, 64]), ("s5_c_im", [2, 64, 16, 64]),
                        ("s5_d", [D]), ("s5_w_glu", [D, D]), ("s5_b_glu", [D]), ("s5_w_out", [D, D])):
            Wd[nm] = dt(nm, [n_s5] + shp, F32, kind="ExternalInput").ap()
    Wd["ln_g"] = dt("ln_g", [len(layers), D], F32, kind="ExternalInput").ap()
    Wd["ln_b"] = dt("ln_b", [len(layers), D], F32, kind="ExternalInput").ap()
    c_ident = dt("c_ident", [128, 128], BF16, kind="ExternalInput").ap()
    c_cos = dt("c_cos", [128, ntile_rope, 64], F32, kind="ExternalInput").ap()
    c_sin = dt("c_sin", [128, ntile_rope, 64], F32, kind="ExternalInput").ap()
    c_id32 = dt("c_id32", [128, 128], F32, kind="ExternalInput").ap()
    c_maskf = dt("c_maskf", [128, 128], F32, kind="ExternalInput").ap()
    c_maskb = dt("c_maskb", [128, 128], F32, kind="ExternalInput").ap()
    S5S = {
        "W1T": dt("s5s_w1t", [128, 2, 2, 32, 128], BF16, kind="Internal").ap(),
        "W3": dt("s5s_w3", [128, 2, 2, 32, 128], BF16, kind="Internal").ap(),
        "TOEP": dt("s5s_toep", [128, 64, 128], BF16, kind="Internal").ap(),
        "YS": dt("s5s_ys", [T_total, D], F32, kind="Internal").ap(),
        "YG": dt("s5s_yg", [T_total, D], BF16, kind="Internal").ap(),
    }
    xa = dt("xa", [T_total, D], F32, kind="Internal").ap()
    xb_ = dt("xb", [T_total, D], F32, kind="Internal").ap()
    scr = {
        "SG": dt("scr_sg", [D, T_total], BF16, kind="Internal").ap(),
        "OG": dt("scr_og", [D, T_total], BF16, kind="Internal").ap(),
        "CQ": dt("scr_cq", [QLORA, T_total], BF16, kind="Internal").ap(),
    }

    with ExitStack() as st:
        k = K(nc, st)
        t = k.trk
        cst = {}
        cst["ident"] = k.sb(st, "ident", [128, 128], BF16)
        cst["ones"] = k.sb(st, "ones", [128, 128], BF16)
        cst["c_cos"], cst["c_sin"], cst["ntile_rope"] = c_cos, c_sin, ntile_rope
        for nm, src in (("id32", c_id32), ("maskf", c_maskf), ("maskb", c_maskb)):
            cst[nm] = k.sb(st, nm, [128, 128], F32)
            t.dma("sp", cst[nm].t[:], src, writes=[cst[nm].r])
        k.eps_ln = k.sb(st, "eps_ln", [128, 1], F32)
        t.dma("sp", cst["ident"].t[:], c_ident, writes=[cst["ident"].r])
        t.op("dve", lambda e: e.memset(cst["ones"].t[:], 1.0), writes=[cst["ones"].r])
        cst["ones32"] = k.sb(st, "ones32", [128, 128], F32)
        t.op("dve", lambda e: e.memset(cst["ones32"].t[:], 1.0), writes=[cst["ones32"].r])
        t.op("dve", lambda e: e.memset(k.eps_ln.t[:], LN_EPS), writes=[k.eps_ln.r])

        cur = x
        i_mla = i_s5 = 0
        for li, lt in enumerate(layers):
            dst = y if li == len(layers) - 1 else (xa if li % 2 == 0 else xb_)
            if lt == "mla":
                W = {"w_in": Wd["mla_w_in"][i_mla], "g_q": Wd["mla_g_q"][i_mla], "w_q_up": Wd["mla_w_q_up"][i_mla],
                     "g_kv": Wd["mla_g_kv"][i_mla], "w_kv_up": Wd["mla_w_kv_up"][i_mla], "w_out": Wd["mla_w_out"][i_mla],
                     "ln_g": Wd["ln_g"][li], "ln_b": Wd["ln_b"][li]}
                mla_layer(k, cst, cur, dst, W, seqs, scr)
                i_mla += 1
            else:
                W = {n[3:]: Wd[n][i_s5] for n in Wd if n.startswith("s5_")}
                W["ln_g"], W["ln_b"] = Wd["ln_g"][li], Wd["ln_b"][li]
                s5_layer(k, cst, cur, dst, W, seqs, S5S)
                i_s5 += 1
            cur = dst
        t.barrier(("sp",))
    return nc


def s5_consts():
    sp = np.arange(128) // 16
    maskf = (sp[None, :] >= sp[:, None]).astype(np.float32)
    maskb = (sp[:, None] >= sp[None, :]).astype(np.float32)
    return np.eye(128, dtype=np.float32), maskf, maskb


def rope_consts(ntile):
    inv = (10000.0 ** (-np.arange(0, 64, 2, dtype=np.float32) / np.float32(64))).astype(np.float32)
    pos = np.arange(ntile * 128, dtype=np.float32)
    ang = (pos[:, None] * inv[None, :]).astype(np.float32)
    cos, sin = np.cos(ang).astype(np.float32), np.sin(ang).astype(np.float32)
    cos2 = np.concatenate([cos, cos], axis=1)
    sinpm = np.concatenate([-sin, sin], axis=1)
    f = lambda a: np.ascontiguousarray(a.reshape(ntile, 128, 64).transpose(1, 0, 2))
    return f(cos2), f(sinpm)


SEQ_P, SEQ_S = 8192, 4096
LAYERS = ["mla", "s5", "mla", "s5"]
_PROG = {}


def kernel(x_prompt, x_sample, mla_w_in, mla_g_q, mla_w_q_up, mla_g_kv, mla_w_kv_up, mla_w_out,
           s5_w_in, s5_a_re, s5_a_im, s5_log_step, s5_b_re, s5_b_im, s5_c_re, s5_c_im, s5_d,
           s5_w_glu, s5_b_glu, s5_w_out, ln_g, ln_b):
    import ml_dtypes
    f32 = lambda a: np.ascontiguousarray(np.asarray(a, dtype=np.float32))
    x_prompt, x_sample = f32(x_prompt), f32(x_sample)
    seqs = [(0, SEQ_P), (SEQ_P, SEQ_S), (SEQ_P + SEQ_S, SEQ_S)]
    ntile = SEQ_P // 128
    if "nc" not in _PROG:
        _PROG["nc"] = build_program(seqs, LAYERS, ntile)
    nc = _PROG["nc"]
    cos2, sinpm = rope_consts(ntile)
    i32, mf, mb = s5_consts()
    shared = {
        "mla_w_in": f32(mla_w_in), "mla_g_q": f32(mla_g_q), "mla_w_q_up": f32(mla_w_q_up), "mla_g_kv": f32(mla_g_kv),
        "mla_w_kv_up": f32(mla_w_kv_up), "mla_w_out": f32(mla_w_out),
        "s5_w_in": f32(s5_w_in), "s5_a_re": f32(s5_a_re), "s5_a_im": f32(s5_a_im), "s5_log_step": f32(s5_log_step),
        "s5_b_re": f32(s5_b_re), "s5_b_im": f32(s5_b_im), "s5_c_re": f32(s5_c_re), "s5_c_im": f32(s5_c_im), "s5_d": f32(s5_d),
        "s5_w_glu": f32(s5_w_glu), "s5_b_glu": f32(s5_b_glu), "s5_w_out": f32(s5_w_out),
        "ln_g": f32(ln_g), "ln_b": f32(ln_b),
        "c_ident": np.eye(128, dtype=ml_dtypes.bfloat16), "c_cos": cos2, "c_sin": sinpm,
        "c_id32": i32, "c_maskf": mf, "c_maskb": mb,
    }
    in_maps = []
    for c in range(NCORES):
        xc = np.concatenate([x_prompt[c], x_sample[2 * c], x_sample[2 * c + 1]], axis=0)
        m = dict(shared)
        m["x"] = np.ascontiguousarray(xc)
        in_maps.append(m)
    res = run_bass_kernel_spmd(nc, in_maps, core_ids=list(range(NCORES)))
    y_p = np.empty_like(x_prompt)
    y_s = np.empty_like(x_sample)
    for c in range(NCORES):
        yc = np.asarray(res.results[c]["y"], dtype=np.float32)
        y_p[c] = yc[0:SEQ_P]
        y_s[2 * c] = yc[SEQ_P:SEQ_P + SEQ_S]
        y_s[2 * c + 1] = yc[SEQ_P + SEQ_S:]
    return (y_p, y_s)
```

```python
import math
from contextlib import ExitStack

import numpy as np
import concourse.bass as bass
import concourse.mybir as mybir
from concourse.bass_utils import run_bass_kernel_spmd

F32 = mybir.dt.float32
BF16 = mybir.dt.bfloat16
AF = mybir.ActivationFunctionType
ALU = mybir.AluOpType

D = 1024
NH = 8
QLORA, KVLORA, ROPE = 384, 256, 64
MLA_IN = 1728
ATTN_SCALE = 1.0 / math.sqrt(192.0)
DEPTH = 4
ALPHA = (2 * DEPTH) ** 0.25
LN_EPS = 1e-5
RMS_EPS = 1e-6
NCORES = 8
import os as _os
DBG_STOP = int(_os.environ.get('DBG_STOP', '0'))
STQ = _os.environ.get('STQ', 'pool')
ATT_ONES = int(_os.environ.get('ATT_ONES', '0'))
POOL_EVERY = int(_os.environ.get('POOL_EVERY', '3'))
DBG_MASK = int(_os.environ.get('DBG_MASK', '15'))


class Res:
    __slots__ = ("name", "lw", "rd", "excl")

    def __init__(self, name="", excl=False):
        self.name = name
        self.lw = None
        self.rd = {}
        self.excl = excl


class Trk:
    def __init__(self, nc, stack):
        self.nc = nc
        self.stack = stack
        self.E = {"pe": nc.tensor, "act": nc.scalar, "dve": nc.vector, "pool": nc.gpsimd, "sp": nc.sync}
        self.sems, self.cnt, self.pend = {}, {}, {}
        for k in self.E:
            self.sems[k] = stack.enter_context(nc.semaphore("s_" + k))
            self.cnt[k] = 0
            self.pend[k] = False
        self.seen = {k: {} for k in self.E}

    def _need(self, eng, deps):
        for (sk, val) in deps:
            if self.seen[eng].get(sk, 0) >= val:
                continue
            if sk in self.E:
                assert self.cnt[sk] >= val, f"waiting on pending inc of {sk}"
            self.E[eng].wait_ge(self.sems[sk], val)
            self.seen[eng][sk] = val

    def _deps(self, eng, reads, writes):
        deps = []
        for r in reads:
            if r.lw is not None:
                deps.append(r.lw)
            if r.excl:
                for sk, v in r.rd.items():
                    if sk != eng:
                        deps.append((sk, v))
        for w in writes:
            if w.lw is not None and not (w.lw[0] == eng and eng == "pe"):
                deps.append(w.lw)
            for sk, v in w.rd.items():
                if not (sk == eng and eng == "pe"):
                    deps.append((sk, v))
        return deps

    def op(self, eng, fn, reads=(), writes=(), inc=True):
        self._need(eng, self._deps(eng, reads, writes))
        inst = fn(self.E[eng])
        if inc:
            inst.then_inc(self.sems[eng], 1)
            self.cnt[eng] += 1
            val = self.cnt[eng]
            self.pend[eng] = False
        else:
            val = self.cnt[eng] + 1
            self.pend[eng] = True
        for r in reads:
            r.rd[eng] = max(r.rd.get(eng, 0), val)
        for w in writes:
            w.lw = (eng, val)
            w.rd = {}
        return inst

    def dma(self, eng, out, in_, reads=(), writes=(), key=None, **kw):
        self._need(eng, self._deps("dma", reads, writes))
        if key is None:
            key = writes[0].name if writes else reads[0].name
        key = "d_" + key
        if key not in self.sems:
            self.sems[key] = self.stack.enter_context(self.nc.semaphore(key))
            self.cnt[key] = 0
        inst = self.E[eng].dma_start(out=out, in_=in_, **kw)
        inst.then_inc(self.sems[key], 16)
        self.cnt[key] += 16
        val = self.cnt[key]
        for r in reads:
            r.rd[key] = val
        for w in writes:
            w.lw = (key, val)
            w.rd = {}
        return inst

    def barrier(self, engines=("pe", "act", "dve", "pool", "sp")):
        for sk in self.E:
            assert not self.pend[sk], f"pending inc on {sk} at barrier"
        for e in engines:
            self._need(e, [(sk, self.cnt[sk]) for sk in self.sems if self.cnt[sk] > 0 and sk != e])


class Buf:
    def __init__(self, t, name, excl=False):
        self.t = t
        self.r = Res(name, excl)


class K:
    def __init__(self, nc, stack):
        self.nc = nc
        self.trk = Trk(nc, stack)
        self.uid = 0

    def sb(self, st, name, shape, dt):
        self.uid += 1
        nm = f"{name}_{self.uid}"
        if _os.environ.get("DBG_ALLOC"):
            n = 1
            for v in shape[1:]:
                n *= v
            print("ALLOC", nm, shape, n * (2 if dt == BF16 else 4))
        return Buf(st.enter_context(self.nc.sbuf_tensor(nm, list(shape), dt)), name)

    def ps(self, st, name, shape, dt):
        self.uid += 1
        nm = f"{name}_{self.uid}"
        return Buf(st.enter_context(self.nc.psum_tensor(nm, list(shape), dt)), name, excl=True)


def _R(bufs):
    return [b.r for b in bufs]


def load_w_bf16(k, st_bufs, w_dram, dst, nk, ncols, cast_eng=("act", "dve")):
    t = k.trk
    i = 0
    for kc in range(nk):
        for c0 in range(0, ncols, 2048):
            c1 = min(ncols, c0 + 2048)
            sbuf = st_bufs[i % len(st_bufs)]
            t.dma("sp", sbuf.t[:, 0:c1 - c0], w_dram[kc * 128:(kc + 1) * 128, c0:c1], writes=[sbuf.r])
            eng = cast_eng[i % len(cast_eng)]
            if eng == "act":
                t.op("act", lambda e, s=sbuf, kc=kc, c0=c0, c1=c1: e.copy(out=dst.t[:, kc, c0:c1], in_=s.t[:, 0:c1 - c0]),
                     reads=[sbuf.r], writes=[dst.r])
            else:
                t.op("dve", lambda e, s=sbuf, kc=kc, c0=c0, c1=c1: e.tensor_copy(out=dst.t[:, kc, c0:c1], in_=s.t[:, 0:c1 - c0]),
                     reads=[sbuf.r], writes=[dst.r])
            i += 1


def bcast_rows(ap_1d, n):
    return ap_1d.rearrange("(o n) -> o n", o=1).broadcast(0, 128) if hasattr(ap_1d, "broadcast") else None


def mla_layer(k, cst, x_in, x_out, W, seqs, scr):
    nc, t = k.nc, k.trk
    T_total = sum(L for _, L in seqs)
    Lmax = max(L for _, L in seqs)
    ident, ones = cst["ident"], cst["ones"]

    with ExitStack() as ls:
        cos2 = k.sb(ls, "cos2", [128, cst["ntile_rope"], 64], F32)
        sinpm = k.sb(ls, "sinpm", [128, cst["ntile_rope"], 64], F32)
        t.dma("sp", cos2.t[:], cst["c_cos"], writes=[cos2.r])
        t.dma("sp", sinpm.t[:], cst["c_sin"], writes=[sinpm.r])
        ckvT = k.sb(ls, "ckvT", [128, 2, Lmax], BF16)
        krT = k.sb(ls, "krT", [128, Lmax], BF16)
        t.op("pool", lambda e: e.memset(krT.t[64:128, :], 0.0), reads=[], writes=[krT.r])
        stage = [k.sb(ls, f"stage{i}", [128, 2048], F32) for i in range(2)]
        gq = k.sb(ls, "gq", [128, QLORA + KVLORA], F32)
        t.dma("sp", gq.t[:, 0:QLORA], W["g_q"].partition_broadcast(128), writes=[gq.r])
        t.dma("sp", gq.t[:, QLORA:QLORA + KVLORA], W["g_kv"].partition_broadcast(128), writes=[gq.r])

        for (row0, L) in seqs:
            nblk = L // 512
            with ExitStack() as p1:
                w_in = k.sb(p1, "w_in", [128, 8, MLA_IN], BF16)
                load_w_bf16(k, stage, W["w_in"], w_in, 8, MLA_IN)
                xt = k.sb(p1, "xt", [128, 4, D], F32)
                xb = k.sb(p1, "xb", [128, 4, D], BF16)
                xT = k.sb(p1, "xT", [128, 8, 512], BF16)
                lat = k.sb(p1, "lat", [128, 4, 704], BF16)
                sg = [k.sb(p1, f"sg{i}", [128, 8, 512], BF16) for i in range(2)]
                cqo = [k.sb(p1, f"cqo{i}", [128, 3, 512], BF16) for i in range(2)]
                junk = k.sb(p1, "junk", [128, 384], BF16)
                stat = k.sb(p1, "stat", [128, 8], F32)
                rtmp = k.sb(p1, "rtmp", [128, 4, 64], F32)
                rtmp2 = k.sb(p1, "rtmp2", [128, 4, 64], F32)
                psT = [k.ps(p1, f"psT{i}", [128, 2, 512], BF16) for i in range(2)]
                psL = k.ps(p1, "psL", [128, 1024], F32)
                psLT = k.ps(p1, "psLT", [128, 4, 512], BF16)
                psG = [k.ps(p1, f"psG{i}", [128, 512], F32) for i in range(2)]

                for b in range(nblk):
                    if DBG_STOP == 11:
                        break
                    r0 = row0 + b * 512
                    c0 = b * 512
                    g0 = r0
                    t.dma("sp", xt.t[:], x_in[r0:r0 + 512, :].rearrange("(j p) d -> p j d", p=128), writes=[xt.r])
                    t.op("act", lambda e: e.copy(out=xb.t[:, 0:2, :], in_=xt.t[:, 0:2, :]), reads=[xt.r], writes=[xb.r])
                    t.op("act", lambda e: e.copy(out=xb.t[:, 2:4, :], in_=xt.t[:, 2:4, :]), reads=[xt.r], writes=[xb.r])
                    for kp in range(4):
                        pT = psT[kp % 2]
                        for kk in range(2):
                            kc = kp * 2 + kk
                            for j in range(4):
                                last = (kk == 1 and j == 3)
                                t.op("pe", lambda e, kc=kc, kk=kk, j=j, pT=pT: e.transpose(
                                    out=pT.t[:, kk, j * 128:(j + 1) * 128], in_=xb.t[:, j, kc * 128:(kc + 1) * 128],
                                    identity=ident.t[:]), reads=[xb.r, ident.r], writes=[pT.r], inc=last)
                        t.op("dve", lambda e, kp=kp, pT=pT: e.tensor_copy(out=xT.t[:, kp * 2:kp * 2 + 2, :], in_=pT.t[:]),
                             reads=[pT.r], writes=[xT.r])
                    if DBG_STOP == 12:
                        continue
                    sgb = sg[b % 2]

                    def gate_head(h, sgb=sgb):
                        pG = psG[h % 2]
                        for kc in range(8):
                            t.op("pe", lambda e, kc=kc: e.matmul(
                                out=pG.t[:], lhsT=w_in.t[:, kc, 704 + h * 128:704 + (h + 1) * 128], rhs=xT.t[:, kc, :],
                                start=(kc == 0), stop=(kc == 7)), reads=[xT.r, w_in.r], writes=[pG.r], inc=(kc == 7))
                        t.op("act", lambda e: e.activation(out=sgb.t[:, h, :], in_=pG.t[:], func=AF.Silu),
                             reads=[pG.r], writes=[sgb.r])
                    for j in range(4):
                        for kc in range(8):
                            t.op("pe", lambda e, kc=kc, j=j: e.matmul(
                                out=psL.t[:, 0:512], lhsT=xT.t[:, kc, j * 128:(j + 1) * 128], rhs=w_in.t[:, kc, 0:512],
                                start=(kc == 0), stop=(kc == 7)), reads=[xT.r, w_in.r], writes=[psL.r], inc=False)
                            t.op("pe", lambda e, kc=kc, j=j: e.matmul(
                                out=psL.t[:, 512:704], lhsT=xT.t[:, kc, j * 128:(j + 1) * 128], rhs=w_in.t[:, kc, 512:704],
                                start=(kc == 0), stop=(kc == 7)), reads=[xT.r, w_in.r], writes=[psL.r], inc=(kc == 7))
                        t.op("act", lambda e: e.activation(out=junk.t[:, 0:QLORA], in_=psL.t[:, 0:QLORA], func=AF.Square,
                                                           accum_out=stat.t[:, 0:1]), reads=[psL.r], writes=[junk.r, stat.r])
                        t.op("act", lambda e: e.activation(out=junk.t[:, 0:KVLORA], in_=psL.t[:, QLORA:QLORA + KVLORA], func=AF.Square,
                                                           accum_out=stat.t[:, 1:2]), reads=[psL.r], writes=[junk.r, stat.r])
                        t.op("dve", lambda e: e.tensor_scalar(out=stat.t[:, 2:3], in0=stat.t[:, 0:1], scalar1=1.0 / QLORA, scalar2=RMS_EPS,
                                                              op0=ALU.mult, op1=ALU.add), reads=[stat.r], writes=[stat.r])
                        t.op("dve", lambda e: e.tensor_scalar(out=stat.t[:, 3:4], in0=stat.t[:, 1:2], scalar1=1.0 / KVLORA, scalar2=RMS_EPS,
                                                              op0=ALU.mult, op1=ALU.add), reads=[stat.r], writes=[stat.r])
                        t.op("act", lambda e: e.activation(out=stat.t[:, 4:6], in_=stat.t[:, 2:4], func=AF.Sqrt), reads=[stat.r], writes=[stat.r])
                        t.op("dve", lambda e: e.reciprocal(out=stat.t[:, 6:8], in_=stat.t[:, 4:6]), reads=[stat.r], writes=[stat.r])
                        t.op("dve", lambda e, j=j: e.scalar_tensor_tensor(
                            out=lat.t[:, j, 0:QLORA], in0=psL.t[:, 0:QLORA], scalar=stat.t[:, 6:7], in1=gq.t[:, 0:QLORA],
                            op0=ALU.mult, op1=ALU.mult), reads=[psL.r, stat.r, gq.r], writes=[lat.r])
                        t.op("dve", lambda e, j=j: e.scalar_tensor_tensor(
                            out=lat.t[:, j, QLORA:640], in0=psL.t[:, QLORA:640], scalar=stat.t[:, 7:8], in1=gq.t[:, QLORA:640],
                            op0=ALU.mult, op1=ALU.mult), reads=[psL.r, stat.r, gq.r], writes=[lat.r])
                        ti = b * 4 + j
                        t.op("dve", lambda e, j=j, ti=ti: e.tensor_tensor(out=rtmp.t[:, j, :], in0=psL.t[:, 640:704], in1=cos2.t[:, ti, :], op=ALU.mult),
                             reads=[psL.r, cos2.r], writes=[rtmp.r])
                        t.op("dve", lambda e, j=j, ti=ti: e.tensor_tensor(out=rtmp2.t[:, j, 0:32], in0=psL.t[:, 672:704], in1=sinpm.t[:, ti, 0:32], op=ALU.mult),
                             reads=[psL.r, sinpm.r], writes=[rtmp2.r])
                        t.op("dve", lambda e, j=j, ti=ti: e.tensor_tensor(out=rtmp2.t[:, j, 32:64], in0=psL.t[:, 640:672], in1=sinpm.t[:, ti, 32:64], op=ALU.mult),
                             reads=[psL.r, sinpm.r], writes=[rtmp2.r])
                        t.op("dve", lambda e, j=j: e.tensor_tensor(out=lat.t[:, j, 640:704], in0=rtmp.t[:, j, :], in1=rtmp2.t[:, j, :], op=ALU.add),
                             reads=[rtmp.r, rtmp2.r], writes=[lat.r])
                        if DBG_STOP not in (13, 14, 15):
                            gate_head(2 * j)
                            gate_head(2 * j + 1)
                    if DBG_STOP == 13:
                        continue
                    for j in range(4):
                        for c in range(6):
                            w = 128 if c < 5 else 64
                            last = (j == 3 and c == 5)
                            dstp = psLT if c < 4 else psT[0]
                            cc = c if c < 4 else c - 4
                            t.op("pe", lambda e, j=j, c=c, w=w, dstp=dstp, cc=cc: e.transpose(
                                out=dstp.t[0:w, cc, j * 128:(j + 1) * 128], in_=lat.t[:, j, c * 128:c * 128 + w], identity=ident.t[:]),
                                reads=[lat.r, ident.r], writes=[dstp.r], inc=(last or (j == 3 and c == 3)))
                    cq = cqo[b % 2]
                    if DBG_MASK & 1:
                        t.op("dve", lambda e, cq=cq: e.tensor_copy(out=cq.t[:], in_=psLT.t[:, 0:3, :]), reads=[psLT.r], writes=[cq.r])
                    if DBG_MASK & 2:
                        t.op("act", lambda e, c0=c0: e.copy(out=ckvT.t[:, 0, c0:c0 + 512], in_=psLT.t[:, 3, :]), reads=[psLT.r], writes=[])
                    if DBG_MASK & 4:
                        t.op("act", lambda e, c0=c0: e.copy(out=ckvT.t[:, 1, c0:c0 + 512], in_=psT[0].t[:, 0, :]), reads=[psT[0].r], writes=[])
                    if DBG_MASK & 8:
                        t.op("dve", lambda e, c0=c0: e.tensor_copy(out=krT.t[0:64, c0:c0 + 512], in_=psT[0].t[0:64, 1, :]), reads=[psT[0].r], writes=[])
                    if DBG_STOP != 15:
                        t.dma(STQ, scr["CQ"][:, g0:g0 + 512].rearrange("(c p) t -> p c t", p=128), cq.t[:], reads=[cq.r])
                    t.dma(STQ, scr["SG"][:, g0:g0 + 512].rearrange("(h p) t -> p h t", p=128), sgb.t[:], reads=[sgb.r])
            t.barrier()
            if DBG_STOP in (1, 11, 12, 13, 14, 15):
                continue
            with ExitStack() as p2:
                w_q = k.sb(p2, "w_q", [128, 3, 1536], BF16)
                w_kv = k.sb(p2, "w_kv", [128, 2, 2048], BF16)
                load_w_bf16(k, stage, W["w_q_up"], w_q, 3, 1536)
                load_w_bf16(k, stage, W["w_kv_up"], w_kv, 2, 2048)
                KT = k.sb(p2, "KT", [128, Lmax], BF16)
                V = k.sb(p2, "V", [128, Lmax // 128, 128], BF16)
                cqi = [k.sb(p2, f"cqi{i}", [128, 3, 512], BF16) for i in range(2)]
                sgi = [k.sb(p2, f"sgi{i}", [128, 512], BF16) for i in range(2)]
                ogo = [k.sb(p2, f"ogo{i}", [128, 512], BF16) for i in range(2)]
                qtok = k.sb(p2, "qtok", [128, 4, 192], BF16)
                qa = k.sb(p2, "qa", [128, 4, 64], F32)
                qb = k.sb(p2, "qb", [128, 4, 64], F32)
                QnT = [k.sb(p2, f"QnT{i}", [128, 512], BF16) for i in range(2)]
                QrT = [k.sb(p2, f"QrT{i}", [128, 512], BF16) for i in range(2)]
                for qq in QrT:
                    t.op("pool", lambda e, qq=qq: e.memset(qq.t[64:128, :], 0.0), reads=[], writes=[qq.r])
                NPT = 6
                pt = [k.sb(p2, f"pt{i}", [128, 512], BF16) for i in range(NPT)]
                rec = k.sb(p2, "rec", [128, 512], F32)
                otmp = k.sb(p2, "otmp", [128, 512], F32)
                psS = [k.ps(p2, f"psS{i}", [128, 512], F32) for i in range(3)]
                psO = [k.ps(p2, f"psO{i}", [128, 512], F32) for i in range(2)]
                psD = [k.ps(p2, f"psD{i}", [128, 512], F32) for i in range(1)]
                accsb = k.sb(p2, "accsb", [128, 512], F32)
                accP = k.sb(p2, "accP", [128, 512], F32)
                psQ = k.ps(p2, "psQ", [128, 2, 256], F32)
                psQT = k.ps(p2, "psQT", [128, 2, 512], BF16)

                nkb = L // 128
                qi0 = 0
                for _ in range(1):
                    pass

                items = [(h, qt) for h in range(NH) for qt in range(nblk)]

                def prologue(idx):
                    h, qt = items[idx]
                    g0 = row0 + qt * 512
                    bi = (qi0 + idx) % 2
                    cq, sgt, qn, qr = cqi[bi], sgi[bi], QnT[bi], QrT[bi]

                    def p0():
                        t.dma("sp", cq.t[:], scr["CQ"][:, g0:g0 + 512].rearrange("(c p) t -> p c t", p=128), writes=[cq.r])
                        t.dma("sp", sgt.t[:], scr["SG"][h * 128:(h + 1) * 128, g0:g0 + 512], writes=[sgt.r])

                    def phalf(half):
                        for jj in range(2):
                            j = half * 2 + jj
                            for kc in range(3):
                                t.op("pe", lambda e, kc=kc, j=j, jj=jj: e.matmul(
                                    out=psQ.t[:, jj, 0:192], lhsT=cq.t[:, kc, j * 128:(j + 1) * 128], rhs=w_q.t[:, kc, h * 192:(h + 1) * 192],
                                    start=(kc == 0), stop=(kc == 2)), reads=[cq.r, w_q.r], writes=[psQ.r], inc=(kc == 2 and jj == 1))
                        j0 = half * 2
                        ti0 = qt * 4 + j0
                        t.op("act", lambda e: e.copy(out=qtok.t[:, j0:j0 + 2, 0:128], in_=psQ.t[:, :, 0:128]), reads=[psQ.r], writes=[qtok.r])
                        t.op("dve", lambda e: e.tensor_tensor(out=qa.t[:, j0:j0 + 2, :], in0=psQ.t[:, :, 128:192], in1=cos2.t[:, ti0:ti0 + 2, :], op=ALU.mult),
                             reads=[psQ.r, cos2.r], writes=[qa.r])
                        t.op("dve", lambda e: e.tensor_tensor(out=qb.t[:, j0:j0 + 2, 0:32], in0=psQ.t[:, :, 160:192], in1=sinpm.t[:, ti0:ti0 + 2, 0:32], op=ALU.mult),
                             reads=[psQ.r, sinpm.r], writes=[qb.r])
                        t.op("dve", lambda e: e.tensor_tensor(out=qb.t[:, j0:j0 + 2, 32:64], in0=psQ.t[:, :, 128:160], in1=sinpm.t[:, ti0:ti0 + 2, 32:64], op=ALU.mult),
                             reads=[psQ.r, sinpm.r], writes=[qb.r])
                        t.op("dve", lambda e: e.tensor_tensor(out=qtok.t[:, j0:j0 + 2, 128:192], in0=qa.t[:, j0:j0 + 2, :], in1=qb.t[:, j0:j0 + 2, :], op=ALU.add),
                             reads=[qa.r, qb.r], writes=[qtok.r])

                    def p3():
                        for j in range(4):
                            t.op("pe", lambda e, j=j: e.transpose(out=psQT.t[:, 0, j * 128:(j + 1) * 128], in_=qtok.t[:, j, 0:128], identity=ident.t[:]),
                                 reads=[qtok.r, ident.r], writes=[psQT.r], inc=False)
                            t.op("pe", lambda e, j=j: e.transpose(out=psQT.t[0:64, 1, j * 128:(j + 1) * 128], in_=qtok.t[:, j, 128:192], identity=ident.t[:]),
                                 reads=[qtok.r, ident.r], writes=[psQT.r], inc=(j == 3))
                        t.op("act", lambda e: e.copy(out=qn.t[:], in_=psQT.t[:, 0, :]), reads=[psQT.r], writes=[qn.r])
                        t.op("act", lambda e: e.copy(out=qr.t[0:64, :], in_=psQT.t[0:64, 1, :]), reads=[psQT.r], writes=[qr.r])

                    return [p0, lambda: phalf(0), lambda: phalf(1), p3]

                def kv_head(h):
                    banks = [psQ.t[:].rearrange("p a b -> p (a b)"), psS[0].t[:], psS[1].t[:], psS[2].t[:]]
                    bres = [psQ.r, psS[0].r, psS[1].r, psS[2].r]
                    n = 0
                    for b in range(nblk):
                        bk, br = banks[n % 4], bres[n % 4]
                        n += 1
                        for kc in range(2):
                            t.op("pe", lambda e, kc=kc, b=b, bk=bk: e.matmul(
                                out=bk, lhsT=w_kv.t[:, kc, h * 256:h * 256 + 128],
                                rhs=ckvT.t[:, kc, b * 512:(b + 1) * 512], start=(kc == 0), stop=(kc == 1)),
                                reads=[w_kv.r], writes=[br], inc=(kc == 1))
                        t.op("dve", lambda e, b=b, bk=bk: e.tensor_copy(out=KT.t[:, b * 512:(b + 1) * 512], in_=bk),
                             reads=[br], writes=[KT.r])
                    for b in range(nblk):
                        bk, br = banks[n % 4], bres[n % 4]
                        n += 1
                        for j in range(4):
                            ti = b * 4 + j
                            for kc in range(2):
                                t.op("pe", lambda e, kc=kc, ti=ti, j=j, bk=bk: e.matmul(
                                    out=bk[:, j * 128:(j + 1) * 128], lhsT=ckvT.t[:, kc, ti * 128:(ti + 1) * 128],
                                    rhs=w_kv.t[:, kc, h * 256 + 128:h * 256 + 256], start=(kc == 0), stop=(kc == 1)),
                                    reads=[w_kv.r], writes=[br], inc=(kc == 1 and j == 3))
                        t.op("act", lambda e, b=b, bk=bk: e.copy(out=V.t[:, b * 4:(b + 1) * 4, :].rearrange("p a b -> p (a b)"), in_=bk),
                             reads=[br], writes=[V.r])

                for pc in prologue(0):
                    pc()
                for idx, (h, qt) in enumerate(items):
                    if qt == 0:
                        kv_head(h)
                    g0 = row0 + qt * 512
                    bi = (qi0 + idx) % 2
                    sgt, og, qn, qr = sgi[bi], ogo[bi], QnT[bi], QrT[bi]
                    pO, pD = psO[bi], psD[0]
                    nxt = prologue(idx + 1) if idx + 1 < len(items) else []
                    when = {0: 0, max(1, nkb // 4): 1, max(2, nkb // 2): 2, max(3, (3 * nkb) // 4): 3}

                    def issue_S(kb, qn=qn, qr=qr):
                        pS = psS[kb % 3]
                        t.op("pe", lambda e: e.matmul(out=pS.t[:], lhsT=KT.t[:, kb * 128:(kb + 1) * 128], rhs=qn.t[:], start=True, stop=False),
                             reads=[KT.r, qn.r], writes=[pS.r], inc=False)
                        t.op("pe", lambda e: e.matmul(out=pS.t[:], lhsT=krT.t[:, kb * 128:(kb + 1) * 128], rhs=qr.t[:], start=False, stop=True),
                             reads=[qr.r], writes=[pS.r], inc=True)
                        p = pt[kb % NPT]
                        t.op("act", lambda e: e.activation(out=p.t[:], in_=pS.t[:], func=AF.Exp, scale=ATTN_SCALE), reads=[pS.r], writes=[p.r])

                    def issue_O(kb, pO=pO, pD=pD):
                        p = pt[kb % NPT]
                        t.op("pe", lambda e: e.matmul(out=pO.t[:], lhsT=V.t[:, kb, :], rhs=p.t[:], start=(kb == 0), stop=(kb == nkb - 1)),
                             reads=[V.r, p.r], writes=[pO.r], inc=(not ATT_ONES))
                        if ATT_ONES:
                            t.op("pe", lambda e: e.matmul(out=pD.t[:], lhsT=ones.t[:], rhs=p.t[:], start=(kb == 0), stop=(kb == nkb - 1)),
                                 reads=[ones.r, p.r], writes=[pD.r], inc=True)
                        elif kb % POOL_EVERY == POOL_EVERY - 1:
                            if kb == POOL_EVERY - 1:
                                t.op("pool", lambda e: e.tensor_copy(out=accP.t[:], in_=p.t[:]), reads=[p.r], writes=[accP.r])
                            else:
                                t.op("pool", lambda e: e.tensor_tensor(out=accP.t[:], in0=accP.t[:], in1=p.t[:], op=ALU.add), reads=[p.r, accP.r], writes=[accP.r])
                        elif kb == 0:
                            t.op("dve", lambda e: e.tensor_copy(out=pD.t[:], in_=p.t[:]), reads=[p.r], writes=[pD.r])
                        else:
                            t.op("dve", lambda e: e.tensor_tensor(out=pD.t[:], in0=pD.t[:], in1=p.t[:], op=ALU.add), reads=[p.r, pD.r], writes=[pD.r])

                    issue_S(0)
                    if nkb > 1:
                        issue_S(1)
                    for kb in range(nkb):
                        if kb + 2 < nkb:
                            issue_S(kb + 2)
                        issue_O(kb)
                        if nxt and kb in when:
                            nxt[when[kb]]()
                    if not ATT_ONES:
                        t.op("dve", lambda e, pD=pD: e.tensor_copy(out=accsb.t[:], in_=pD.t[:]), reads=[pD.r], writes=[accsb.r])
                        t.op("pe", lambda e, pD=pD: e.matmul(out=pD.t[:], lhsT=cst["ones32"].t[:], rhs=accsb.t[:], start=True, stop=False),
                             reads=[cst["ones32"].r, accsb.r], writes=[pD.r], inc=False)
                        t.op("pe", lambda e, pD=pD: e.matmul(out=pD.t[:], lhsT=cst["ones32"].t[:], rhs=accP.t[:], start=False, stop=True),
                             reads=[cst["ones32"].r, accP.r], writes=[pD.r], inc=True)
                    t.op("dve", lambda e, pD=pD: e.reciprocal(out=rec.t[:], in_=pD.t[:]), reads=[pD.r], writes=[rec.r])
                    t.op("dve", lambda e, pO=pO: e.tensor_tensor(out=otmp.t[:], in0=pO.t[:], in1=rec.t[:], op=ALU.mult), reads=[pO.r, rec.r], writes=[otmp.r])
                    t.op("dve", lambda e, og=og, sgt=sgt: e.tensor_tensor(out=og.t[:], in0=otmp.t[:], in1=sgt.t[:], op=ALU.mult), reads=[otmp.r, sgt.r], writes=[og.r])
                    t.dma(STQ, scr["OG"][h * 128:(h + 1) * 128, g0:g0 + 512], og.t[:], reads=[og.r])
                qi0 += len(items)
            t.barrier()

    if DBG_STOP in (1, 2, 11, 12, 13, 14, 15):
        return
    t.barrier()
    with ExitStack() as p3:
        w_o = k.sb(p3, "w_o", [128, 8, D], BF16)
        with ExitStack() as sst:
            stage3 = [k.sb(sst, f"stage{i}", [128, 2048], F32) for i in range(2)]
            load_w_bf16(k, stage3, W["w_out"], w_o, 8, D)
            t.barrier()
        out_proj_ln(k, p3, x_in, x_out, W, w_o, scr["OG"], T_total, tok_perm=None)
    t.barrier()


def out_proj_ln(k, st, x_in, x_out, W, w_o, OGT, T_total, tok_perm=None):
    nc, t = k.nc, k.trk
    lng = k.sb(st, "lng", [128, D], F32)
    lnb = k.sb(st, "lnb", [128, D], F32)
    t.dma("sp", lng.t[:], W["ln_g"].partition_broadcast(128), writes=[lng.r])
    t.dma("sp", lnb.t[:], W["ln_b"].partition_broadcast(128), writes=[lnb.r])
    ogi = [k.sb(st, f"ogi{i}", [128, 8, 512], BF16) for i in range(2)]
    xi = [k.sb(st, f"xi{i}", [128, 4, D], F32) for i in range(2)]
    yo = [k.sb(st, f"yo{i}", [128, 4, D], F32) for i in range(2)]
    zs, bsts, mvs = ln_scratch(k, st)
    psY = [k.ps(st, f"psY{i}", [128, 1024], F32) for i in range(2)]
    for b in range(T_total // 512):
        r0 = b * 512
        og, x, y = ogi[b % 2], xi[b % 2], yo[b % 2]
        t.dma("sp", og.t[:], OGT[:, r0:r0 + 512].rearrange("(h p) t -> p h t", p=128), writes=[og.r])
        t.dma("sp", x.t[:], x_in[r0:r0 + 512, :].rearrange("(j p) d -> p j d", p=128), writes=[x.r])
        for j in range(4):
            pY = psY[j % 2]
            for half in range(2):
                for h in range(8):
                    t.op("pe", lambda e, h=h, j=j, half=half, pY=pY, og=og: e.matmul(
                        out=pY.t[:, half * 512:(half + 1) * 512], lhsT=og.t[:, h, j * 128:(j + 1) * 128], rhs=w_o.t[:, h, half * 512:(half + 1) * 512],
                        start=(h == 0), stop=(h == 7)), reads=[og.r, w_o.r], writes=[pY.r], inc=(h == 7 and half == 1))
            ln_tail(k, x.t[:, j, :], x.r, pY, zs, bsts, mvs, lng, lnb, y.t[:, j, :], y.r, j)
        t.dma(STQ, x_out[r0:r0 + 512, :].rearrange("(j p) d -> p j d", p=128), y.t[:], reads=[y.r])


def ln_tail(k, x_ap, x_r, pY, zs, bsts, mvs, lng, lnb, y_ap, y_r, i):
    t = k.trk
    z, z2, bst, mv = zs[0][i % 2], zs[1][i % 2], bsts[i % 2], mvs[i % 2]
    t.op("dve", lambda e: e.scalar_tensor_tensor(out=z.t[:], in0=x_ap, scalar=ALPHA, in1=pY.t[:], op0=ALU.mult, op1=ALU.add),
         reads=[x_r, pY.r], writes=[z.r])
    t.op("dve", lambda e: e.bn_stats(out=bst.t[:, 0, :], in_=z.t[:, 0:512]), reads=[z.r], writes=[bst.r])
    t.op("dve", lambda e: e.bn_stats(out=bst.t[:, 1, :], in_=z.t[:, 512:1024]), reads=[z.r], writes=[bst.r])
    t.op("dve", lambda e: e.bn_aggr(out=mv.t[:, 0:2], in_=bst.t[:].rearrange("p a b -> p (a b)")), reads=[bst.r], writes=[mv.r])
    t.op("act", lambda e: e.activation(out=mv.t[:, 2:3], in_=mv.t[:, 1:2], func=AF.Sqrt, bias=k.eps_ln.t[:, 0:1]), reads=[mv.r], writes=[mv.r])
    t.op("dve", lambda e: e.reciprocal(out=mv.t[:, 3:4], in_=mv.t[:, 2:3]), reads=[mv.r], writes=[mv.r])
    t.op("dve", lambda e: e.tensor_scalar(out=z.t[:], in0=z.t[:], scalar1=mv.t[:, 0:1], scalar2=mv.t[:, 3:4], op0=ALU.subtract, op1=ALU.mult),
         reads=[z.r, mv.r], writes=[z.r])
    t.op("dve", lambda e: e.tensor_tensor(out=z2.t[:], in0=z.t[:], in1=lng.t[:], op=ALU.mult), reads=[z.r, lng.r], writes=[z2.r])
    t.op("pool", lambda e: e.tensor_tensor(out=y_ap, in0=z2.t[:], in1=lnb.t[:], op=ALU.add), reads=[z2.r, lnb.r], writes=[y_r])


def ln_scratch(k, st):
    zs = ([k.sb(st, f"z{i}", [128, D], F32) for i in range(2)], [k.sb(st, f"zz{i}", [128, D], F32) for i in range(2)])
    bsts = [k.sb(st, f"bst{i}", [128, 2, 6], F32) for i in range(2)]
    mvs = [k.sb(st, f"mv{i}", [128, 4], F32) for i in range(2)]
    return zs, bsts, mvs


TWO_PI = 2.0 * math.pi
GELU_C = 2.0 * math.sqrt(2.0 / math.pi)


def s5_setup(k, cst, W, SW, lam):
    nc, t = k.nc, k.trk
    id32, maskf, maskb = cst["id32"], cst["maskf"], cst["maskb"]
    dve = lambda fn, rd, wr: t.op("dve", fn, reads=rd, writes=wr)
    with ExitStack() as su:
        rs = Res("s5small")
        SM = k.sb(su, "SM", [128, 40, 64], F32)
        PW = k.sb(su, "PW", [128, 16, 2, 64], F32)
        BT = k.sb(su, "BT", [128, 2, 2, 16, 32], F32)
        CT = k.sb(su, "CT", [128, 2, 2, 16, 32], F32)
        BB = k.sb(su, "BB", [128, 2, 2, 16, 32], F32)
        W3 = k.sb(su, "W3", [128, 2, 32, 128], BF16)
        W1T = k.sb(su, "W1T", [128, 2, 32, 128], BF16)
        TOE = k.sb(su, "TOE", [128, 64, 128], BF16)
        TAC = k.sb(su, "TAC", [128, 64, 128], F32)
        dcol = k.sb(su, "dcol", [128, 64], F32)
        psA = k.ps(su, "psA", [128, 512], F32)
        psB = k.ps(su, "psB", [128, 512], F32)
        sm = lambda i: SM.t[:, i, :]
        AR, AI, LS, STEP, LR, LI, M_, R_, TMP, TH, T2, SN, CS, T1, T2b, T3, LBr, LBi, DEN, MUr, MUi, NR, Qr, Qi = range(24)

        with ExitStack() as s1:
            praw = k.sb(s1, "praw", [32, 3, 2, 128], F32)
            ls = k.sb(s1, "ls", [32, 2, 2], F32)
            zer = k.sb(s1, "zer", [32, 64], F32)
            for d in range(2):
                t.dma("sp", praw.t[:, 0, d, :], W["a_re"][d].rearrange("(gb g2) n -> gb (g2 n)", g2=2), writes=[praw.r])
                t.dma("sp", praw.t[:, 1, d, :], W["a_im"][d].rearrange("(gb g2) n -> gb (g2 n)", g2=2), writes=[praw.r])
                t.dma("sp", ls.t[:, d, :], W["log_step"][d].rearrange("(gb g2) -> gb g2", g2=2), writes=[ls.r])
            dve(lambda e: e.memset(zer.t[:], 0.0), [], [zer.r])
            for d in range(2):
                for g2 in range(2):
                    dve(lambda e, d=d, g2=g2: e.tensor_scalar(out=praw.t[:, 2, d, g2 * 64:(g2 + 1) * 64], in0=zer.t[:], scalar1=ls.t[:, d, g2:g2 + 1],
                                                              scalar2=None, op0=ALU.add), [zer.r, ls.r, praw.r], [praw.r])
            for j in range(3):
                for d in range(2):
                    i = j * 2 + d
                    t.op("pe", lambda e, j=j, d=d, i=i: e.transpose(out=psA.t[:, i * 32:(i + 1) * 32], in_=praw.t[:, j, d, :], identity=id32.t[0:32, 0:32]),
                         reads=[praw.r, id32.r], writes=[psA.r], inc=(i == 5))
            dve(lambda e: e.tensor_copy(out=SM.t[:, 0:3, :].rearrange("q a b -> q (a b)"), in_=psA.t[:, 0:192]), [psA.r], [rs])

        t.barrier()

        if DBG_STOP == 31:
            return
        def tt(o, a, b, op):
            dve(lambda e: e.tensor_tensor(out=sm(o), in0=sm(a), in1=sm(b), op=op), [rs], [rs])

        def ts(o, a, s1_, s2_, op0, op1=None):
            if op1 is None:
                dve(lambda e: e.tensor_scalar(out=sm(o), in0=sm(a), scalar1=s1_, scalar2=None, op0=op0), [rs], [rs])
            else:
                dve(lambda e: e.tensor_scalar(out=sm(o), in0=sm(a), scalar1=s1_, scalar2=s2_, op0=op0, op1=op1), [rs], [rs])

        t.op("act", lambda e: e.activation(out=sm(STEP), in_=sm(LS), func=AF.Exp), reads=[rs], writes=[rs])
        tt(LR, AR, STEP, ALU.mult)
        tt(LI, AI, STEP, ALU.mult)
        ts(M_, LR, 1.0 / 120, 1.0 / 24, ALU.mult, ALU.add)
        for cco in (1.0 / 6, 0.5, 1.0, 1.0):
            tt(M_, M_, LR, ALU.mult)
            ts(M_, M_, cco, None, ALU.add)
        ts(R_, LI, 1.0, None, ALU.mult)
        for m in range(1, 6):
            ts(TMP, LI, TWO_PI * m, -TWO_PI, ALU.is_ge, ALU.mult)
            tt(R_, R_, TMP, ALU.add)
        ts(TH, R_, -math.pi, 0.125, ALU.add, ALU.mult)
        tt(T2, TH, TH, ALU.mult)
        ts(SN, T2, -1.0 / 5040, 1.0 / 120, ALU.mult, ALU.add)
        for cco in (-1.0 / 6, 1.0):
            tt(SN, SN, T2, ALU.mult)
            ts(SN, SN, cco, None, ALU.add)
        tt(SN, SN, TH, ALU.mult)
        ts(CS, T2, 1.0 / 40320, -1.0 / 720, ALU.mult, ALU.add)
        for cco in (1.0 / 24, -0.5, 1.0):
            tt(CS, CS, T2, ALU.mult)
            ts(CS, CS, cco, None, ALU.add)
        for _ in range(3):
            tt(T3, CS, SN, ALU.mult)
            tt(T1, CS, CS, ALU.mult)
            tt(T2b, SN, SN, ALU.mult)
            tt(CS, T1, T2b, ALU.subtract)
            ts(SN, T3, 2.0, None, ALU.mult)
        tt(LBr, M_, CS, ALU.mult)
        ts(LBr, LBr, -1.0, None, ALU.mult)
        tt(LBi, M_, SN, ALU.mult)
        ts(LBi, LBi, -1.0, None, ALU.mult)
        tt(T1, LBr, LBr, ALU.mult)
        tt(T2b, LBi, LBi, ALU.mult)
        tt(DEN, T1, T2b, ALU.add)
        dve(lambda e: e.reciprocal(out=sm(DEN), in_=sm(DEN)), [rs], [rs])
        tt(MUr, LBr, DEN, ALU.mult)
        tt(MUi, LBi, DEN, ALU.mult)
        ts(MUi, MUi, -1.0, None, ALU.mult)
        ts(NR, LBr, -1.0, None, ALU.add)
        tt(T1, AR, AR, ALU.mult)
        tt(T2b, AI, AI, ALU.mult)
        tt(DEN, T1, T2b, ALU.add)
        dve(lambda e: e.reciprocal(out=sm(DEN), in_=sm(DEN)), [rs], [rs])
        tt(T1, NR, AR, ALU.mult)
        tt(T2b, LBi, AI, ALU.mult)
        tt(Qr, T1, T2b, ALU.add)
        tt(Qr, Qr, DEN, ALU.mult)
        tt(T1, LBi, AR, ALU.mult)
        tt(T2b, NR, AI, ALU.mult)
        tt(Qi, T1, T2b, ALU.subtract)
        tt(Qi, Qi, DEN, ALU.mult)
        pw = lambda kk, ri: PW.t[:, kk, ri, :]
        dve(lambda e: e.memset(pw(7, 0), 1.0), [rs], [rs])
        dve(lambda e: e.memset(pw(7, 1), 0.0), [rs], [rs])

        def cmul_pw(ko, ki, br, bi):
            dve(lambda e: e.tensor_tensor(out=sm(T1), in0=pw(ki, 0), in1=sm(br), op=ALU.mult), [rs], [rs])
            dve(lambda e: e.tensor_tensor(out=sm(T2b), in0=pw(ki, 1), in1=sm(bi), op=ALU.mult), [rs], [rs])
            dve(lambda e: e.tensor_tensor(out=pw(ko, 0), in0=sm(T1), in1=sm(T2b), op=ALU.subtract), [rs], [rs])
            dve(lambda e: e.tensor_tensor(out=sm(T1), in0=pw(ki, 0), in1=sm(bi), op=ALU.mult), [rs], [rs])
            dve(lambda e: e.tensor_tensor(out=sm(T2b), in0=pw(ki, 1), in1=sm(br), op=ALU.mult), [rs], [rs])
            dve(lambda e: e.tensor_tensor(out=pw(ko, 1), in0=sm(T1), in1=sm(T2b), op=ALU.add), [rs], [rs])

        for kk in range(7, 15):
            cmul_pw(kk + 1, kk, LBr, LBi)
        for kk in range(7, 0, -1):
            cmul_pw(kk - 1, kk, MUr, MUi)
        for d in range(2):
            for r in range(2):
                dve(lambda e, d=d, r=r: e.tensor_copy(out=lam["A"].t[:, d, r, :], in_=PW.t[:, 15, 0, d * 32:(d + 1) * 32]), [rs], [lam["A"].r])
            dve(lambda e, d=d: e.tensor_copy(out=lam["I"].t[:, d, :], in_=PW.t[:, 15, 1, d * 32:(d + 1) * 32]), [rs], [lam["I"].r])
            for a_ in range(2):
                dve(lambda e, d=d, a_=a_: e.tensor_copy(out=lam["L4"].t[:, d, a_, 0, :], in_=PW.t[:, 15, 0, d * 32:(d + 1) * 32]), [rs, lam["L4"].r], [lam["L4"].r])
            dve(lambda e, d=d: e.tensor_copy(out=lam["L4"].t[:, d, 0, 1, :], in_=PW.t[:, 15, 1, d * 32:(d + 1) * 32]), [rs, lam["L4"].r], [lam["L4"].r])
            dve(lambda e, d=d: e.tensor_scalar(out=lam["L4"].t[:, d, 1, 1, :], in0=PW.t[:, 15, 1, d * 32:(d + 1) * 32], scalar1=-1.0, scalar2=None, op0=ALU.mult),
                [rs, lam["L4"].r], [lam["L4"].r])
            dve(lambda e, d=d: e.tensor_copy(out=lam["B"].t[:, d, 1, :], in_=PW.t[:, 15, 1, d * 32:(d + 1) * 32]), [rs], [lam["B"].r])
            dve(lambda e, d=d: e.tensor_scalar(out=lam["B"].t[:, d, 0, :], in0=PW.t[:, 15, 1, d * 32:(d + 1) * 32], scalar1=-1.0, scalar2=None, op0=ALU.mult),
                [rs, lam["B"].r], [lam["B"].r])
            dve(lambda e, d=d: e.tensor_scalar(out=lam["NI"].t[:, d, :], in0=PW.t[:, 15, 1, d * 32:(d + 1) * 32], scalar1=-1.0, scalar2=None, op0=ALU.mult),
                [rs], [lam["NI"].r])

        if DBG_STOP == 32:
            return
        with ExitStack() as s2:
            raw = k.sb(s2, "raw", [32, 2, 2048], F32)
            raw2 = k.sb(s2, "raw2", [32, 2, 2048], F32)
            for (src_r, src_i, dst, isC) in ((W["b_re"], W["b_im"], BT, False), (W["c_re"], W["c_im"], CT, True)):
                for d in range(2):
                    pat = "(gb g2) p n -> gb (g2 p n)" if isC else "(gb g2) n p -> gb (g2 n p)"
                    t.dma("sp", raw.t[:, 0, :], src_r[d].rearrange(pat, g2=2), writes=[raw.r])
                    t.dma("sp", raw.t[:, 1, :], src_i[d].rearrange(pat, g2=2), writes=[raw.r])
                    for ri in range(2):
                        ps = psA if ri == 0 else psB
                        if isC:
                            dve(lambda e, ri=ri: e.tensor_copy(out=raw2.t[:, ri, :].rearrange("q (p g n) -> q p g n", p=16, g=2),
                                                               in_=raw.t[:, ri, :].rearrange("q (g p n) -> q p g n", g=2, p=16)), [raw.r], [raw2.r])
                        for pp in range(16):
                            if isC:
                                src = raw2.t[:, ri, pp * 128:(pp + 1) * 128]
                            else:
                                src = raw.t[:, ri, :].rearrange("q (m p) -> q m p", p=16)[:, :, pp]
                            t.op("pe", lambda e, src=src, pp=pp, ps=ps: e.transpose(out=ps.t[:, pp * 32:(pp + 1) * 32], in_=src, identity=id32.t[0:32, 0:32]),
                                 reads=[raw.r, raw2.r, id32.r], writes=[ps.r], inc=(pp == 15))
                        dve(lambda e, d=d, ri=ri, ps=ps, dst=dst: e.tensor_copy(out=dst.t[:, d, ri, :, :].rearrange("q a b -> q (a b)"), in_=ps.t[:]), [ps.r], [dst.r])
        t.barrier()
        if DBG_STOP == 33:
            return
        with ExitStack() as s3:
            u1 = k.sb(s3, "u1", [128, 16, 32], F32)
            u2 = k.sb(s3, "u2", [128, 16, 32], F32)
            for d in range(2):
                qr = SM.t[:, Qr, d * 32:(d + 1) * 32].unsqueeze(1).broadcast_to([128, 16, 32])
                qi = SM.t[:, Qi, d * 32:(d + 1) * 32].unsqueeze(1).broadcast_to([128, 16, 32])
                br, bi = BT.t[:, d, 0, :, :], BT.t[:, d, 1, :, :]
                dve(lambda e: e.tensor_tensor(out=u1.t[:], in0=br, in1=qr, op=ALU.mult), [rs, BT.r], [u1.r])
                dve(lambda e: e.tensor_tensor(out=u2.t[:], in0=bi, in1=qi, op=ALU.mult), [rs, BT.r], [u2.r])
                dve(lambda e, d=d: e.tensor_tensor(out=BB.t[:, d, 0, :, :], in0=u1.t[:], in1=u2.t[:], op=ALU.subtract), [u1.r, u2.r], [BB.r])
                dve(lambda e: e.tensor_tensor(out=u1.t[:], in0=bi, in1=qr, op=ALU.mult), [rs, BT.r, BB.r], [u1.r])
                dve(lambda e: e.tensor_tensor(out=u2.t[:], in0=br, in1=qi, op=ALU.mult), [rs, BT.r, BB.r], [u2.r])
                dve(lambda e, d=d: e.tensor_tensor(out=BB.t[:, d, 1, :, :], in0=u1.t[:], in1=u2.t[:], op=ALU.add), [u1.r, u2.r], [BB.r])
        t.barrier()
        with nc.allow_non_contiguous_dma(reason="tiny param relayout"):
            for s in range(8):
                t.dma("sp", dcol.t[16 * s:16 * s + 16, :], W["d"].rearrange("(g p) -> p g", p=16), writes=[dcol.r])
        if DBG_STOP == 34:
            return
        with ExitStack() as s4:
            V1 = k.sb(s4, "V1", [128, 2, 16, 8, 16], F32)
            Z = k.sb(s4, "Z", [128, 2, 16, 8, 16], F32)
            v1 = k.sb(s4, "v1", [128, 16, 16], F32)
            v2 = k.sb(s4, "v2", [128, 16, 16], F32)
            tq = k.sb(s4, "tq", [128, 4, 128], F32)
            for d in range(2):
                for h2 in range(2):
                    g0 = h2 * 16
                    pwv = lambda kk, ri: PW.t[:, kk, ri, d * 32 + g0:d * 32 + g0 + 16].unsqueeze(2).broadcast_to([128, 16, 16])
                    bbv = lambda ri: BB.t[:, d, ri, :, g0:g0 + 16].rearrange("q p g -> q g p")
                    ctv = lambda ri: CT.t[:, d, ri, :, g0:g0 + 16].rearrange("q p g -> q g p")

                    def cprod(out_r, out_i, a_r, a_i, kk, neg_i, rd):
                        wr = rd[-1:]
                        dve(lambda e: e.tensor_tensor(out=v1.t[:], in0=a_r, in1=pwv(kk, 0), op=ALU.mult), [rs] + rd[:-1], [v1.r])
                        dve(lambda e: e.tensor_tensor(out=v2.t[:], in0=a_i, in1=pwv(kk, 1), op=ALU.mult), [rs] + rd[:-1], [v2.r])
                        dve(lambda e: e.tensor_tensor(out=out_r, in0=v1.t[:], in1=v2.t[:], op=ALU.subtract), [v1.r, v2.r], wr)
                        dve(lambda e: e.tensor_tensor(out=v1.t[:], in0=a_r, in1=pwv(kk, 1), op=ALU.mult), [rs] + rd[:-1] + wr, [v1.r])
                        dve(lambda e: e.tensor_tensor(out=v2.t[:], in0=a_i, in1=pwv(kk, 0), op=ALU.mult), [rs] + rd[:-1] + wr, [v2.r])
                        if neg_i:
                            dve(lambda e: e.scalar_tensor_tensor(out=out_i, in0=v1.t[:], scalar=-1.0, in1=v2.t[:], op0=ALU.mult, op1=ALU.subtract), [v1.r, v2.r], wr)
                        else:
                            dve(lambda e: e.tensor_tensor(out=out_i, in0=v1.t[:], in1=v2.t[:], op=ALU.add), [v1.r, v2.r], wr)

                    for s in range(8):
                        kk = (14 - s) if d == 0 else (s + 7)
                        cprod(V1.t[:, 0, :, s, :], V1.t[:, 1, :, s, :], bbv(0), bbv(1), kk, False, [BB.r, V1.r])
                        kz = s if d == 0 else (7 - s)
                        cprod(Z.t[:, 0, :, s, :], Z.t[:, 1, :, s, :], ctv(0), ctv(1), kz, True, [CT.r, Z.r])
                        kw = (s + 8) if d == 0 else (15 - s)
                        w3v = lambda ri: W3.t[:, ri, g0:g0 + 16, s * 16:(s + 1) * 16]
                        cprod(w3v(0), w3v(1), ctv(0), ctv(1), kw, True, [CT.r, W3.r])
                    for ri in range(2):
                        for gq in range(4):
                            ps = psA if (gq % 2 == 0) else psB
                            for gl in range(4):
                                gbl = gq * 4 + gl
                                t.op("pe", lambda e, ri=ri, gbl=gbl, gl=gl, ps=ps: e.transpose(
                                    out=ps.t[:, gl * 128:(gl + 1) * 128], in_=V1.t[:, ri, gbl, :, :].rearrange("q s p -> q (s p)"), identity=id32.t[:]),
                                    reads=[V1.r, id32.r], writes=[ps.r], inc=(gl == 3))
                            gb0 = g0 + gq * 4
                            dve(lambda e, ri=ri, gb0=gb0, ps=ps: e.tensor_copy(out=W1T.t[:, ri, gb0:gb0 + 4, :].rearrange("q a b -> q (a b)"), in_=ps.t[:]),
                                [ps.r], [W1T.r])
                    for gq in range(4):
                        for g2 in range(2):
                            ps = psA if g2 == 0 else psB
                            pr = slice(g2 * 64, (g2 + 1) * 64)
                            for gl in range(4):
                                gbl = gq * 4 + gl
                                for ri in range(2):
                                    t.op("pe", lambda e, ri=ri, gbl=gbl, pr=pr, gl=gl, ps=ps: e.matmul(
                                        out=ps.t[:, gl * 128:(gl + 1) * 128], lhsT=V1.t[pr, ri, gbl, :, :].rearrange("q s p -> q (s p)"),
                                        rhs=Z.t[pr, ri, gbl, :, :].rearrange("q s p -> q (s p)"), start=(ri == 0), stop=(ri == 1)),
                                        reads=[V1.r, Z.r], writes=[ps.r], inc=(ri == 1 and gl == 3))
                        for g2 in range(2):
                            ps = psA if g2 == 0 else psB
                            gg0 = 2 * (g0 + gq * 4) + g2
                            msk = (maskf if d == 0 else maskb).t[:].unsqueeze(1).broadcast_to([128, 4, 128])
                            psv = ps.t[:].rearrange("q (a b) -> q a b", a=4)
                            tav = TAC.t[:, 2 * (g0 + gq * 4):2 * (g0 + gq * 4) + 8, :].rearrange("q (a two) b -> q a two b", two=2)[:, :, g2, :]
                            if d == 0:
                                dve(lambda e, tav=tav, psv=psv, msk=msk: e.tensor_tensor(out=tav, in0=psv, in1=msk, op=ALU.mult),
                                    [ps.r, maskf.r], [TAC.r])
                            else:
                                dve(lambda e, psv=psv, msk=msk: e.tensor_tensor(out=tq.t[:], in0=psv, in1=msk, op=ALU.mult), [ps.r, maskb.r], [tq.r])
                                dve(lambda e, tav=tav: e.tensor_tensor(out=tav, in0=tav, in1=tq.t[:], op=ALU.add),
                                    [tq.r, TAC.r], [TAC.r])
                t.dma(STQ, SW["W1T"][:, d], W1T.t[:], reads=[W1T.r])
                t.dma(STQ, SW["W3"][:, d], W3.t[:], reads=[W3.r])
        t.barrier()
        for g in range(64):
            dve(lambda e, g=g: e.scalar_tensor_tensor(out=TOE.t[:, g, :], in0=id32.t[:], scalar=dcol.t[:, g:g + 1], in1=TAC.t[:, g, :], op0=ALU.mult, op1=ALU.add),
                [id32.r, dcol.r, TAC.r], [TOE.r])
        t.dma(STQ, SW["TOEP"], TOE.t[:], reads=[TOE.r])
    t.barrier()


def s5_scan_sweep(k, cst, x_in, W, seqs, SW, lam, YS, YG, d, stage):
    nc, t = k.nc, k.trk
    ident = cst["ident"]
    with ExitStack() as sw:
        w_u = k.sb(sw, "w_u", [128, 8, D], BF16)
        with ExitStack() as sst:
            stage = [k.sb(sst, f"stage{i}", [128, 2048], F32) for i in range(2)]
            load_w_bf16(k, stage, W["w_in"][:, 0:D], w_u, 8, D)
            t.barrier()
        W1T = k.sb(sw, "W1Ts", [128, 2, 32, 128], BF16)
        W3 = k.sb(sw, "W3s", [128, 2, 32, 128], BF16)
        t.dma("sp", W1T.t[:], SW["W1T"][:, d], writes=[W1T.r])
        t.dma("sp", W3.t[:], SW["W3"][:, d], writes=[W3.r])
        if d == 0:
            TOE = k.sb(sw, "TOEs", [128, 64, 128], BF16)
            t.dma("sp", TOE.t[:], SW["TOEP"], writes=[TOE.r])
        xq = k.sb(sw, "xq", [128, D], F32)
        xcb = k.sb(sw, "xcb", [128, D], BF16)
        xTp = k.sb(sw, "xTp", [128, 8, 128], BF16)
        ucq = k.sb(sw, "ucq", [128, 64, 4, 16], BF16)
        Us = [k.sb(sw, f"U{i}", [128, 64, 128], BF16) for i in range(2)]
        XHs = [k.sb(sw, f"XH{i}", [128, 130, 2, 32], F32) for i in range(2)]
        Hb = k.sb(sw, "Hb", [128, 32, 2, 130], BF16)
        P4s = [k.sb(sw, f"P4{i}", [128, 2, 2, 32], F32) for i in range(2)]
        t2ra = [Res(f"t2ra{i}") for i in range(2)]
        t2rb = [Res(f"t2rb{i}") for i in range(2)]
        carry = k.sb(sw, "carry", [128, 2, 32], F32)
        ycm = k.sb(sw, "ycm", [128, 8, 256], F32)
        if d == 1:
            yfh = k.sb(sw, "yfh", [128, 8, 256], F32)
            gt = yfh
            ygb = k.sb(sw, "ygb", [128, 8, 256], BF16)
        psT = [k.ps(sw, "psT0", [128, 8, 128], BF16)] * 2
        psU = [k.ps(sw, f"psU{i}", [128, 512], F32) for i in range(2)]
        psUTs = [k.ps(sw, f"psUT{i}", [128, 8, 128], BF16) for i in range(2)]
        psX = k.ps(sw, "psX", [128, 4, 128], F32)
        psY = [k.ps(sw, f"psY{i}", [128, 4, 128], F32) for i in range(2)]
        LA, LI_, LNI = lam["A"], lam["I"], lam["NI"]
        for XH in XHs:
            t.op("dve", lambda e, XH=XH: e.memset(XH.t[:], 0.0), reads=[], writes=[XH.r])
        xoff = 1 if d == 0 else 0
        cin = 0 if d == 0 else 128
        cout = 128 if d == 0 else 0
        hoff = 0 if d == 0 else 1

        segs = []
        for (row0, L) in seqs:
            nseg = L // 1024
            order = range(nseg) if d == 0 else range(nseg - 1, -1, -1)
            for si, s in enumerate(order):
                segs.append((row0 + s * 1024, si == 0))

        def stage_a(i):
            r0, _first = segs[i]
            U, XH = Us[i % 2], XHs[i % 2]
            xv = x_in[r0:r0 + 1024, :].rearrange("(c s) d -> c s d", s=8)
            for q in range(8):
                t.dma("sp", xq.t[:], xv[:, q, :], writes=[xq.r])
                t.op("act", lambda e: e.copy(out=xcb.t[:], in_=xq.t[:]), reads=[xq.r], writes=[xcb.r])
                pT = psT[q % 2]
                for kc in range(8):
                    t.op("pe", lambda e, kc=kc, pT=pT: e.transpose(out=pT.t[:, kc, :], in_=xcb.t[:, kc * 128:(kc + 1) * 128], identity=ident.t[:]),
                         reads=[xcb.r, ident.r], writes=[pT.r], inc=(kc == 7))
                t.op("act", lambda e, pT=pT: e.copy(out=xTp.t[:], in_=pT.t[:]), reads=[pT.r], writes=[xTp.r])
                for half in range(2):
                    pU = psU[half]
                    for kc in range(8):
                        t.op("pe", lambda e, kc=kc, half=half, pU=pU: e.matmul(out=pU.t[:], lhsT=xTp.t[:, kc, :], rhs=w_u.t[:, kc, half * 512:(half + 1) * 512],
                                                                               start=(kc == 0), stop=(kc == 7)), reads=[xTp.r, w_u.r], writes=[pU.r], inc=(kc == 7))
                    t.op("act", lambda e, half=half, pU=pU, q=q: e.copy(out=ucq.t[:, half * 32:(half + 1) * 32, q % 4, :],
                                                                         in_=pU.t[:].rearrange("q (g p) -> q g p", p=16)), reads=[pU.r], writes=[ucq.r])
                if q % 4 != 3:
                    continue
                hq = q // 4
                for gq in range(8):
                    psUT = psUTs[gq % 2]
                    for gl in range(8):
                        g = gq * 8 + gl
                        t.op("pe", lambda e, g=g, gl=gl, hq=hq, psUT=psUT: e.transpose(out=psUT.t[64 * hq:64 * hq + 64, gl, :], in_=ucq.t[:, g, :, :].rearrange("q s p -> q (s p)"), identity=ident.t[:]),
                             reads=[ucq.r, ident.r], writes=[psUT.r], inc=(gl == 7))
                    t.op("act", lambda e, gq=gq, hq=hq, U=U, psUT=psUT: e.copy(out=U.t[64 * hq:64 * hq + 64, gq * 8:gq * 8 + 8, :], in_=psUT.t[64 * hq:64 * hq + 64, :, :]),
                         reads=[psUT.r], writes=[U.r])
            for gp in range(16):
                for gl in range(2):
                    gb = gp * 2 + gl
                    for ri in range(2):
                        for g2 in range(2):
                            t.op("pe", lambda e, gb=gb, gl=gl, ri=ri, g2=g2, U=U: e.matmul(
                                out=psX.t[g2 * 64:(g2 + 1) * 64, gl * 2 + ri, :], lhsT=W1T.t[:, ri, gb, g2 * 64:(g2 + 1) * 64], rhs=U.t[:, 2 * gb + g2, :],
                                start=True, stop=True), reads=[W1T.r, U.r], writes=[psX.r], inc=(gl == 1 and ri == 1 and g2 == 1))
                t.op("act", lambda e, gp=gp, XH=XH: e.copy(
                    out=XH.t[:, xoff:xoff + 128, :, gp * 2:gp * 2 + 2].rearrange("q c r g -> q g r c"),
                    in_=psX.t[:].rearrange("q (g r) c -> q g r c", g=2)), reads=[psX.r], writes=[XH.r])

        def stage_b(i):
            r0, first = segs[i]
            U, XH = Us[i % 2], XHs[i % 2]
            XHp = XHs[(i + 1) % 2]
            if first:
                t.op("dve", lambda e: e.memset(XH.t[:, cin, :, :], 0.0), reads=[], writes=[XH.r])
            else:
                t.op("dve", lambda e: e.tensor_copy(out=XH.t[:, cin, :, :], in_=carry.t[:]), reads=[carry.r, XH.r], writes=[XH.r])
            steps = range(128) if d == 0 else range(127, -1, -1)
            for c in steps:
                cur = c + xoff
                prv = cur - 1 if d == 0 else cur + 1
                P = P4s[c % 2]
                t.op("dve", lambda e, prv=prv, P=P: e.tensor_tensor(out=P.t[:], in0=XH.t[:, prv, :, :].unsqueeze(2).broadcast_to([128, 2, 2, 32]),
                                                                     in1=lam["L4"].t[:, d, :, :, :], op=ALU.mult), reads=[XH.r, lam["L4"].r], writes=[P.r])
                t.op("dve", lambda e, cur=cur, P=P: e.tensor_tensor(out=XH.t[:, cur, :, :], in0=XH.t[:, cur, :, :], in1=P.t[:, 0, :, :], op=ALU.add), reads=[XH.r, P.r], writes=[XH.r])
                t.op("dve", lambda e, cur=cur, P=P: e.tensor_tensor(out=XH.t[:, cur, :, :], in0=XH.t[:, cur, :, :], in1=P.t[:, 1, ::-1, :], op=ALU.add), reads=[XH.r, P.r], writes=[XH.r])
            t.op("dve", lambda e: e.tensor_copy(out=carry.t[:], in_=XH.t[:, cout, :, :]), reads=[XH.r], writes=[carry.r])
            t.op("act", lambda e: e.copy(out=Hb.t[:], in_=XH.t[:].rearrange("q c r g -> q g r c")), reads=[XH.r], writes=[Hb.r])
            for qtr in range(4):
                ysl = YS[r0:r0 + 1024, :].rearrange("(c s) d -> c s d", s=8)[:, :, qtr * 256:(qtr + 1) * 256]
                if d == 1:
                    t.dma("sp", yfh.t[:], ysl, writes=[yfh.r])
                for gq in range(2):
                    for g2 in range(2):
                        pY = psY[g2]
                        pr = slice(g2 * 64, (g2 + 1) * 64)
                        for gl in range(4):
                            g = qtr * 16 + gq * 8 + 2 * gl + g2
                            gb = g // 2
                            if d == 0:
                                t.op("pe", lambda e, g=g, gl=gl, pY=pY: e.matmul(out=pY.t[:, gl, :], lhsT=U.t[:, g, :], rhs=TOE.t[:, g, :], start=True, stop=False),
                                     reads=[U.r, TOE.r], writes=[pY.r], inc=False)
                            for ri in range(2):
                                t.op("pe", lambda e, gb=gb, pr=pr, ri=ri, gl=gl, pY=pY: e.matmul(
                                    out=pY.t[:, gl, :], lhsT=Hb.t[pr, gb, ri, hoff:hoff + 128], rhs=W3.t[pr, ri, gb, :],
                                    start=(d == 1 and ri == 0), stop=(ri == 1)), reads=[Hb.r, W3.r], writes=[pY.r], inc=(ri == 1 and gl == 3))
                    for g2 in range(2):
                        pY = psY[g2]
                        oview = ycm.t[:, :, gq * 128:(gq + 1) * 128].rearrange("q t (g two p) -> q g two t p", g=4, two=2)[:, :, g2, :, :]
                        iview = pY.t[:].rearrange("q g (t p) -> q g t p", t=8)
                        if d == 0:
                            t.op("act", lambda e, oview=oview, iview=iview: e.copy(out=oview, in_=iview), reads=[pY.r], writes=[ycm.r])
                        else:
                            fview = yfh.t[:, :, gq * 128:(gq + 1) * 128].rearrange("q t (g two p) -> q g two t p", g=4, two=2)[:, :, g2, :, :]
                            t.op("dve", lambda e, oview=oview, iview=iview, fview=fview: e.tensor_tensor(out=oview, in0=iview, in1=fview, op=ALU.add),
                                 reads=[pY.r, yfh.r], writes=[ycm.r])
                if d == 0:
                    t.dma(STQ, ysl, ycm.t[:], reads=[ycm.r])
                else:
                    t.op("act", lambda e: e.activation(out=gt.t[:], in_=ycm.t[:], func=AF.Square), reads=[ycm.r], writes=[gt.r])
                    t.op("dve", lambda e: e.tensor_scalar(out=gt.t[:], in0=gt.t[:], scalar1=0.044715, scalar2=1.0, op0=ALU.mult, op1=ALU.add), reads=[gt.r], writes=[gt.r])
                    t.op("dve", lambda e: e.tensor_tensor(out=gt.t[:], in0=gt.t[:], in1=ycm.t[:], op=ALU.mult), reads=[gt.r, ycm.r], writes=[gt.r])
                    t.op("act", lambda e: e.activation(out=gt.t[:], in_=gt.t[:], func=AF.Sigmoid, scale=GELU_C), reads=[gt.r], writes=[gt.r])
                    t.op("dve", lambda e: e.tensor_tensor(out=ygb.t[:], in0=gt.t[:], in1=ycm.t[:], op=ALU.mult), reads=[gt.r, ycm.r], writes=[ygb.r])
                    t.dma(STQ, YG[r0:r0 + 1024, :].rearrange("(c s) d -> c s d", s=8)[:, :, qtr * 256:(qtr + 1) * 256], ygb.t[:], reads=[ygb.r])

        stage_a(0)
        for i in range(len(segs)):
            if i + 1 < len(segs):
                stage_a(i + 1)
            stage_b(i)
    t.barrier()


def s5_tail_sweep(k, cst, x_in, x_out, W, YG, T_total, stage):
    nc, t = k.nc, k.trk
    ident = cst["ident"]
    with ExitStack() as sw:
        w_g = k.sb(sw, "w_g", [128, 8, D], BF16)
        w_glu = k.sb(sw, "w_glu", [128, 8, D], BF16)
        w_o = k.sb(sw, "w_o", [128, 8, D], BF16)
        with ExitStack() as sst:
            stage = [k.sb(sst, f"stage{i}", [128, 2048], F32) for i in range(2)]
            load_w_bf16(k, stage, W["w_in"][:, D:2 * D], w_g, 8, D)
            load_w_bf16(k, stage, W["w_glu"], w_glu, 8, D)
            load_w_bf16(k, stage, W["w_out"], w_o, 8, D)
            t.barrier()
        bglu = k.sb(sw, "bglu", [128, 8], F32)
        with nc.allow_non_contiguous_dma(reason="tiny bias relayout"):
            t.dma("sp", bglu.t[:], W["b_glu"].rearrange("(c p) -> p c", p=128), writes=[bglu.r])
        lng = k.sb(sw, "lng", [128, D], F32)
        lnb = k.sb(sw, "lnb", [128, D], F32)
        t.dma("sp", lng.t[:], W["ln_g"].partition_broadcast(128), writes=[lng.r])
        t.dma("sp", lnb.t[:], W["ln_b"].partition_broadcast(128), writes=[lnb.r])
        xi = k.sb(sw, "xi", [128, 4, D], F32)
        xb = k.sb(sw, "xb", [128, 4, D], BF16)
        yi = k.sb(sw, "yi", [128, 4, D], BF16)
        xT = k.sb(sw, "xT", [128, 8, 512], BF16)
        yT = k.sb(sw, "yT", [128, 8, 512], BF16)
        sgms = [k.sb(sw, f"sgm{i}", [128, 512], F32) for i in range(2)]
        sils = [k.sb(sw, f"sil{i}", [128, 512], F32) for i in range(2)]
        y3T = k.sb(sw, "y3T", [128, 8, 512], BF16)
        yos = [k.sb(sw, f"yo{i}", [128, 4, D], F32) for i in range(2)]
        zs, bsts, mvs = ln_scratch(k, sw)
        psT = [k.ps(sw, f"psT{i}", [128, 2, 512], BF16) for i in range(2)]
        psZs = [k.ps(sw, f"psZ{i}", [128, 512], F32) for i in range(2)]
        psGs = [k.ps(sw, f"psG{i}", [128, 512], F32) for i in range(2)]
        psY = [k.ps(sw, "psY0", [128, 1024], F32)] * 2
        for b in range(T_total // 512):
            r0 = b * 512
            yo = yos[b % 2]
            t.dma("sp", xi.t[:], x_in[r0:r0 + 512, :].rearrange("(j p) d -> p j d", p=128), writes=[xi.r])
            t.dma("sp", yi.t[:], YG[r0:r0 + 512, :].rearrange("(j p) d -> p j d", p=128), writes=[yi.r])
            t.op("act", lambda e: e.copy(out=xb.t[:], in_=xi.t[:]), reads=[xi.r], writes=[xb.r])
            for (src, dstT) in ((xb, xT), (yi, yT)):
                for kp in range(4):
                    pT = psT[kp % 2]
                    for kk in range(2):
                        kc = kp * 2 + kk
                        for j in range(4):
                            t.op("pe", lambda e, kc=kc, kk=kk, j=j, pT=pT, src=src: e.transpose(
                                out=pT.t[:, kk, j * 128:(j + 1) * 128], in_=src.t[:, j, kc * 128:(kc + 1) * 128], identity=ident.t[:]),
                                reads=[src.r, ident.r], writes=[pT.r], inc=(kk == 1 and j == 3))
                    t.op("dve", lambda e, kp=kp, pT=pT, dstT=dstT: e.tensor_copy(out=dstT.t[:, kp * 2:kp * 2 + 2, :], in_=pT.t[:]), reads=[pT.r], writes=[dstT.r])
            for co in range(8):
                psZ, psG, sgm, sil = psZs[co % 2], psGs[co % 2], sgms[co % 2], sils[co % 2]
                for kc in range(8):
                    t.op("pe", lambda e, kc=kc, co=co: e.matmul(out=psZ.t[:], lhsT=w_glu.t[:, kc, co * 128:(co + 1) * 128], rhs=yT.t[:, kc, :], start=(kc == 0), stop=(kc == 7)),
                         reads=[w_glu.r, yT.r], writes=[psZ.r], inc=(kc == 7))
                for kc in range(8):
                    t.op("pe", lambda e, kc=kc, co=co: e.matmul(out=psG.t[:], lhsT=w_g.t[:, kc, co * 128:(co + 1) * 128], rhs=xT.t[:, kc, :], start=(kc == 0), stop=(kc == 7)),
                         reads=[w_g.r, xT.r], writes=[psG.r], inc=(kc == 7))
                t.op("act", lambda e, co=co: e.activation(out=sgm.t[:], in_=psZ.t[:], func=AF.Sigmoid, bias=bglu.t[:, co:co + 1]), reads=[psZ.r, bglu.r], writes=[sgm.r])
                t.op("act", lambda e: e.activation(out=sil.t[:], in_=psG.t[:], func=AF.Silu), reads=[psG.r], writes=[sil.r])
                t.op("dve", lambda e: e.tensor_tensor(out=sgm.t[:], in0=sgm.t[:], in1=sil.t[:], op=ALU.mult), reads=[sgm.r, sil.r], writes=[sgm.r])
                t.op("dve", lambda e, co=co: e.tensor_tensor(out=y3T.t[:, co, :], in0=sgm.t[:], in1=yT.t[:, co, :], op=ALU.mult), reads=[sgm.r, yT.r], writes=[y3T.r])
            for j in range(4):
                pY = psY[j % 2]
                for half in range(2):
                    for kc in range(8):
                        t.op("pe", lambda e, kc=kc, j=j, half=half, pY=pY: e.matmul(
                            out=pY.t[:, half * 512:(half + 1) * 512], lhsT=y3T.t[:, kc, j * 128:(j + 1) * 128], rhs=w_o.t[:, kc, half * 512:(half + 1) * 512],
                            start=(kc == 0), stop=(kc == 7)), reads=[y3T.r, w_o.r], writes=[pY.r], inc=(kc == 7 and half == 1))
                ln_tail(k, xi.t[:, j, :], xi.r, pY, zs, bsts, mvs, lng, lnb, yo.t[:, j, :], yo.r, j)
            t.dma(STQ, x_out[r0:r0 + 512, :].rearrange("(j p) d -> p j d", p=128), yo.t[:], reads=[yo.r])
    t.barrier()


def s5_layer(k, cst, x_in, x_out, W, seqs, S5S):
    nc, t = k.nc, k.trk
    T_total = sum(L for _, L in seqs)
    with ExitStack() as ls:
        lam = {"A": k.sb(ls, "lamA", [128, 2, 2, 32], F32), "I": k.sb(ls, "lamI", [128, 2, 32], F32), "NI": k.sb(ls, "lamNI", [128, 2, 32], F32),
               "B": k.sb(ls, "lamB", [128, 2, 2, 32], F32), "L4": k.sb(ls, "lamL4", [128, 2, 2, 2, 32], F32)}
        stage = None
        s5_setup(k, cst, W, S5S, lam)
        if DBG_STOP in (21, 31, 32, 33, 34):
            return
        s5_scan_sweep(k, cst, x_in, W, seqs, S5S, lam, S5S["YS"], S5S["YG"], 0, stage)
        if DBG_STOP == 22:
            return
        s5_scan_sweep(k, cst, x_in, W, seqs, S5S, lam, S5S["YS"], S5S["YG"], 1, stage)
        if DBG_STOP == 23:
            return
        s5_tail_sweep(k, cst, x_in, x_out, W, S5S["YG"], T_total, stage)


def build_program(seqs, layers, ntile_rope):
    T_total = sum(L for _, L in seqs)
    nc = bass.Bass("TRN2", target_bir_lowering=False)
    dt = nc.dram_tensor
    x = dt("x", [T_total, D], F32, kind="ExternalInput").ap()
    y = dt("y", [T_total, D], F32, kind="ExternalOutput").ap()
    n_mla = sum(1 for l in layers if l == "mla")
    n_s5 = sum(1 for l in layers if l == "s5")
    Wd = {}
    if n_mla:
        Wd["mla_w_in"] = dt("mla_w_in", [n_mla, D, MLA_IN], F32, kind="ExternalInput").ap()
        Wd["mla_g_q"] = dt("mla_g_q", [n_mla, QLORA], F32, kind="ExternalInput").ap()
        Wd["mla_w_q_up"] = dt("mla_w_q_up", [n_mla, QLORA, 1536], F32, kind="ExternalInput").ap()
        Wd["mla_g_kv"] = dt("mla_g_kv", [n_mla, KVLORA], F32, kind="ExternalInput").ap()
        Wd["mla_w_kv_up"] = dt("mla_w_kv_up", [n_mla, KVLORA, 2048], F32, kind="ExternalInput").ap()
        Wd["mla_w_out"] = dt("mla_w_out", [n_mla, D, D], F32, kind="ExternalInput").ap()
    if n_s5:
        for nm, shp in (("s5_w_in", [D, 2 * D]), ("s5_a_re", [2, 64, 64]), ("s5_a_im", [2, 64, 64]), ("s5_log_step", [2, 64]),
                        ("s5_b_re", [2, 64, 64, 16]), ("s5_b_im", [2, 64, 64, 16]), ("s5_c_re", [2, 64, 16, 64]), ("s5_c_im", [2, 64, 16, 64]),
                        ("s5_d", [D]), ("s5_w_glu", [D, D]), ("s5_b_glu", [D]), ("s5_w_out", [D, D])):
            Wd[nm] = dt(nm, [n_s5] + shp, F32, kind="ExternalInput").ap()
    Wd["ln_g"] = dt("ln_g", [len(layers), D], F32, kind="ExternalInput").ap()
    Wd["ln_b"] = dt("ln_b", [len(layers), D], F32, kind="ExternalInput").ap()
    c_ident = dt("c_ident", [128, 128], BF16, kind="ExternalInput").ap()
    c_cos = dt("c_cos", [128, ntile_rope, 64], F32, kind="ExternalInput").ap()
    c_sin = dt("c_sin", [128, ntile_rope, 64], F32, kind="ExternalInput").ap()
    c_id32 = dt("c_id32", [128, 128], F32, kind="ExternalInput").ap()
    c_maskf = dt("c_maskf", [128, 128], F32, kind="ExternalInput").ap()
    c_maskb = dt("c_maskb", [128, 128], F32, kind="ExternalInput").ap()
    S5S = {
        "W1T": dt("s5s_w1t", [128, 2, 2, 32, 128], BF16, kind="Internal").ap(),
        "W3": dt("s5s_w3", [128, 2, 2, 32, 128], BF16, kind="Internal").ap(),
        "TOEP": dt("s5s_toep", [128, 64, 128], BF16, kind="Internal").ap(),
        "YS": dt("s5s_ys", [T_total, D], F32, kind="Internal").ap(),
        "YG": dt("s5s_yg", [T_total, D], BF16, kind="Internal").ap(),
    }
    xa = dt("xa", [T_total, D], F32, kind="Internal").ap()
    xb_ = dt("xb", [T_total, D], F32, kind="Internal").ap()
    scr = {
        "SG": dt("scr_sg", [D, T_total], BF16, kind="Internal").ap(),
        "OG": dt("scr_og", [D, T_total], BF16, kind="Internal").ap(),
        "CQ": dt("scr_cq", [QLORA, T_total], BF16, kind="Internal").ap(),
    }

    with ExitStack() as st:
        k = K(nc, st)
        t = k.trk
        cst = {}
        cst["ident"] = k.sb(st, "ident", [128, 128], BF16)
        cst["ones"] = k.sb(st, "ones", [128, 128], BF16)
        cst["c_cos"], cst["c_sin"], cst["ntile_rope"] = c_cos, c_sin, ntile_rope
        for nm, src in (("id32", c_id32), ("maskf", c_maskf), ("maskb", c_maskb)):
            cst[nm] = k.sb(st, nm, [128, 128], F32)
            t.dma("sp", cst[nm].t[:], src, writes=[cst[nm].r])
        k.eps_ln = k.sb(st, "eps_ln", [128, 1], F32)
        t.dma("sp", cst["ident"].t[:], c_ident, writes=[cst["ident"].r])
        t.op("dve", lambda e: e.memset(cst["ones"].t[:], 1.0), writes=[cst["ones"].r])
        cst["ones32"] = k.sb(st, "ones32", [128, 128], F32)
        t.op("dve", lambda e: e.memset(cst["ones32"].t[:], 1.0), writes=[cst["ones32"].r])
        t.op("dve", lambda e: e.memset(k.eps_ln.t[:], LN_EPS), writes=[k.eps_ln.r])

        cur = x
        i_mla = i_s5 = 0
        for li, lt in enumerate(layers):
            dst = y if li == len(layers) - 1 else (xa if li % 2 == 0 else xb_)
            if lt == "mla":
                W = {"w_in": Wd["mla_w_in"][i_mla], "g_q": Wd["mla_g_q"][i_mla], "w_q_up": Wd["mla_w_q_up"][i_mla],
                     "g_kv": Wd["mla_g_kv"][i_mla], "w_kv_up": Wd["mla_w_kv_up"][i_mla], "w_out": Wd["mla_w_out"][i_mla],
                     "ln_g": Wd["ln_g"][li], "ln_b": Wd["ln_b"][li]}
                mla_layer(k, cst, cur, dst, W, seqs, scr)
                i_mla += 1
            else:
                W = {n[3:]: Wd[n][i_s5] for n in Wd if n.startswith("s5_")}
                W["ln_g"], W["ln_b"] = Wd["ln_g"][li], Wd["ln_b"][li]
                s5_layer(k, cst, cur, dst, W, seqs, S5S)
                i_s5 += 1
            cur = dst
        t.barrier(("sp",))
    return nc


def s5_consts():
    sp = np.arange(128) // 16
    maskf = (sp[None, :] >= sp[:, None]).astype(np.float32)
    maskb = (sp[:, None] >= sp[None, :]).astype(np.float32)
    return np.eye(128, dtype=np.float32), maskf, maskb


def rope_consts(ntile):
    inv = (10000.0 ** (-np.arange(0, 64, 2, dtype=np.float32) / np.float32(64))).astype(np.float32)
    pos = np.arange(ntile * 128, dtype=np.float32)
    ang = (pos[:, None] * inv[None, :]).astype(np.float32)
    cos, sin = np.cos(ang).astype(np.float32), np.sin(ang).astype(np.float32)
    cos2 = np.concatenate([cos, cos], axis=1)
    sinpm = np.concatenate([-sin, sin], axis=1)
    f = lambda a: np.ascontiguousarray(a.reshape(ntile, 128, 64).transpose(1, 0, 2))
    return f(cos2), f(sinpm)


SEQ_P, SEQ_S = 8192, 4096
LAYERS = ["mla", "s5", "mla", "s5"]
_PROG = {}


def kernel(x_prompt, x_sample, mla_w_in, mla_g_q, mla_w_q_up, mla_g_kv, mla_w_kv_up, mla_w_out,
           s5_w_in, s5_a_re, s5_a_im, s5_log_step, s5_b_re, s5_b_im, s5_c_re, s5_c_im, s5_d,
           s5_w_glu, s5_b_glu, s5_w_out, ln_g, ln_b):
    import ml_dtypes
    f32 = lambda a: np.ascontiguousarray(np.asarray(a, dtype=np.float32))
    x_prompt, x_sample = f32(x_prompt), f32(x_sample)
    seqs = [(0, SEQ_P), (SEQ_P, SEQ_S), (SEQ_P + SEQ_S, SEQ_S)]
    ntile = SEQ_P // 128
    if "nc" not in _PROG:
        _PROG["nc"] = build_program(seqs, LAYERS, ntile)
    nc = _PROG["nc"]
    cos2, sinpm = rope_consts(ntile)
    i32, mf, mb = s5_consts()
    shared = {
        "mla_w_in": f32(mla_w_in), "mla_g_q": f32(mla_g_q), "mla_w_q_up": f32(mla_w_q_up), "mla_g_kv": f32(mla_g_kv),
        "mla_w_kv_up": f32(mla_w_kv_up), "mla_w_out": f32(mla_w_out),
        "s5_w_in": f32(s5_w_in), "s5_a_re": f32(s5_a_re), "s5_a_im": f32(s5_a_im), "s5_log_step": f32(s5_log_step),
        "s5_b_re": f32(s5_b_re), "s5_b_im": f32(s5_b_im), "s5_c_re": f32(s5_c_re), "s5_c_im": f32(s5_c_im), "s5_d": f32(s5_d),
        "s5_w_glu": f32(s5_w_glu), "s5_b_glu": f32(s5_b_glu), "s5_w_out": f32(s5_w_out),
        "ln_g": f32(ln_g), "ln_b": f32(ln_b),
        "c_ident": np.eye(128, dtype=ml_dtypes.bfloat16), "c_cos": cos2, "c_sin": sinpm,
        "c_id32": i32, "c_maskf": mf, "c_maskb": mb,
    }
    in_maps = []
    for c in range(NCORES):
        xc = np.concatenate([x_prompt[c], x_sample[2 * c], x_sample[2 * c + 1]], axis=0)
        m = dict(shared)
        m["x"] = np.ascontiguousarray(xc)
        in_maps.append(m)
    res = run_bass_kernel_spmd(nc, in_maps, core_ids=list(range(NCORES)))
    y_p = np.empty_like(x_prompt)
    y_s = np.empty_like(x_sample)
    for c in range(NCORES):
        yc = np.asarray(res.results[c]["y"], dtype=np.float32)
        y_p[c] = yc[0:SEQ_P]
        y_s[2 * c] = yc[SEQ_P:SEQ_P + SEQ_S]
        y_s[2 * c + 1] = yc[SEQ_P + SEQ_S:]
    return (y_p, y_s)
```

```python
import math
from contextlib import ExitStack

import numpy as np
import concourse.bass as bass
import concourse.mybir as mybir
from concourse.bass_utils import run_bass_kernel_spmd

F32 = mybir.dt.float32
BF16 = mybir.dt.bfloat16
AF = mybir.ActivationFunctionType
ALU = mybir.AluOpType

D = 1024
NH = 8
QLORA, KVLORA, ROPE = 384, 256, 64
MLA_IN = 1728
ATTN_SCALE = 1.0 / math.sqrt(192.0)
DEPTH = 4
ALPHA = (2 * DEPTH) ** 0.25
LN_EPS = 1e-5
RMS_EPS = 1e-6
NCORES = 8
import os as _os
DBG_STOP = int(_os.environ.get('DBG_STOP', '0'))
STQ = _os.environ.get('STQ', 'pool')
ATT_ONES = int(_os.environ.get('ATT_ONES', '0'))
POOL_EVERY = int(_os.environ.get('POOL_EVERY', '3'))
DBG_MASK = int(_os.environ.get('DBG_MASK', '15'))


class Res:
    __slots__ = ("name", "lw", "rd", "excl")

    def __init__(self, name="", excl=False):
        self.name = name
        self.lw = None
        self.rd = {}
        self.excl = excl


class Trk:
    def __init__(self, nc, stack):
        self.nc = nc
        self.stack = stack
        self.E = {"pe": nc.tensor, "act": nc.scalar, "dve": nc.vector, "pool": nc.gpsimd, "sp": nc.sync}
        self.sems, self.cnt, self.pend = {}, {}, {}
        for k in self.E:
            self.sems[k] = stack.enter_context(nc.semaphore("s_" + k))
            self.cnt[k] = 0
            self.pend[k] = False
        self.seen = {k: {} for k in self.E}

    def _need(self, eng, deps):
        for (sk, val) in deps:
            if self.seen[eng].get(sk, 0) >= val:
                continue
            if sk in self.E:
                assert self.cnt[sk] >= val, f"waiting on pending inc of {sk}"
            self.E[eng].wait_ge(self.sems[sk], val)
            self.seen[eng][sk] = val

    def _deps(self, eng, reads, writes):
        deps = []
        for r in reads:
            if r.lw is not None:
                deps.append(r.lw)
            if r.excl:
                for sk, v in r.rd.items():
                    if sk != eng:
                        deps.append((sk, v))
        for w in writes:
            if w.lw is not None and not (w.lw[0] == eng and eng == "pe"):
                deps.append(w.lw)
            for sk, v in w.rd.items():
                if not (sk == eng and eng == "pe"):
                    deps.append((sk, v))
        return deps

    def op(self, eng, fn, reads=(), writes=(), inc=True):
        self._need(eng, self._deps(eng, reads, writes))
        inst = fn(self.E[eng])
        if inc:
            inst.then_inc(self.sems[eng], 1)
            self.cnt[eng] += 1
            val = self.cnt[eng]
            self.pend[eng] = False
        else:
            val = self.cnt[eng] + 1
            self.pend[eng] = True
        for r in reads:
            r.rd[eng] = max(r.rd.get(eng, 0), val)
        for w in writes:
            w.lw = (eng, val)
            w.rd = {}
        return inst

    def dma(self, eng, out, in_, reads=(), writes=(), key=None, **kw):
        self._need(eng, self._deps("dma", reads, writes))
        if key is None:
            key = writes[0].name if writes else reads[0].name
        key = "d_" + key
        if key not in self.sems:
            self.sems[key] = self.stack.enter_context(self.nc.semaphore(key))
            self.cnt[key] = 0
        inst = self.E[eng].dma_start(out=out, in_=in_, **kw)
        inst.then_inc(self.sems[key], 16)
        self.cnt[key] += 16
        val = self.cnt[key]
        for r in reads:
            r.rd[key] = val
        for w in writes:
            w.lw = (key, val)
            w.rd = {}
        return inst

    def barrier(self, engines=("pe", "act", "dve", "pool", "sp")):
        for sk in self.E:
            assert not self.pend[sk], f"pending inc on {sk} at barrier"
        for e in engines:
            self._need(e, [(sk, self.cnt[sk]) for sk in self.sems if self.cnt[sk] > 0 and sk != e])


class Buf:
    def __init__(self, t, name, excl=False):
        self.t = t
        self.r = Res(name, excl)


class K:
    def __init__(self, nc, stack):
        self.nc = nc
        self.trk = Trk(nc, stack)
        self.uid = 0

    def sb(self, st, name, shape, dt):
        self.uid += 1
        nm = f"{name}_{self.uid}"
        if _os.environ.get("DBG_ALLOC"):
            n = 1
            for v in shape[1:]:
                n *= v
            print("ALLOC", nm, shape, n * (2 if dt == BF16 else 4))
        return Buf(st.enter_context(self.nc.sbuf_tensor(nm, list(shape), dt)), name)

    def ps(self, st, name, shape, dt):
        self.uid += 1
        nm = f"{name}_{self.uid}"
        return Buf(st.enter_context(self.nc.psum_tensor(nm, list(shape), dt)), name, excl=True)


def _R(bufs):
    return [b.r for b in bufs]


def load_w_bf16(k, st_bufs, w_dram, dst, nk, ncols, cast_eng=("act", "dve")):
    t = k.trk
    i = 0
    for kc in range(nk):
        for c0 in range(0, ncols, 2048):
            c1 = min(ncols, c0 + 2048)
            sbuf = st_bufs[i % len(st_bufs)]
            t.dma("sp", sbuf.t[:, 0:c1 - c0], w_dram[kc * 128:(kc + 1) * 128, c0:c1], writes=[sbuf.r])
            eng = cast_eng[i % len(cast_eng)]
            if eng == "act":
                t.op("act", lambda e, s=sbuf, kc=kc, c0=c0, c1=c1: e.copy(out=dst.t[:, kc, c0:c1], in_=s.t[:, 0:c1 - c0]),
                     reads=[sbuf.r], writes=[dst.r])
            else:
                t.op("dve", lambda e, s=sbuf, kc=kc, c0=c0, c1=c1: e.tensor_copy(out=dst.t[:, kc, c0:c1], in_=s.t[:, 0:c1 - c0]),
                     reads=[sbuf.r], writes=[dst.r])
            i += 1


def bcast_rows(ap_1d, n):
    return ap_1d.rearrange("(o n) -> o n", o=1).broadcast(0, 128) if hasattr(ap_1d, "broadcast") else None


def mla_layer(k, cst, x_in, x_out, W, seqs, scr):
    nc, t = k.nc, k.trk
    T_total = sum(L for _, L in seqs)
    Lmax = max(L for _, L in seqs)
    ident, ones = cst["ident"], cst["ones"]

    with ExitStack() as ls:
        cos2 = k.sb(ls, "cos2", [128, cst["ntile_rope"], 64], F32)
        sinpm = k.sb(ls, "sinpm", [128, cst["ntile_rope"], 64], F32)
        t.dma("sp", cos2.t[:], cst["c_cos"], writes=[cos2.r])
        t.dma("sp", sinpm.t[:], cst["c_sin"], writes=[sinpm.r])
        ckvT = k.sb(ls, "ckvT", [128, 2, Lmax], BF16)
        krT = k.sb(ls, "krT", [128, Lmax], BF16)
        t.op("pool", lambda e: e.memset(krT.t[64:128, :], 0.0), reads=[], writes=[krT.r])
        stage = [k.sb(ls, f"stage{i}", [128, 2048], F32) for i in range(2)]
        gq = k.sb(ls, "gq", [128, QLORA + KVLORA], F32)
        t.dma("sp", gq.t[:, 0:QLORA], W["g_q"].partition_broadcast(128), writes=[gq.r])
        t.dma("sp", gq.t[:, QLORA:QLORA + KVLORA], W["g_kv"].partition_broadcast(128), writes=[gq.r])

        for (row0, L) in seqs:
            nblk = L // 512
            with ExitStack() as p1:
                w_in = k.sb(p1, "w_in", [128, 8, MLA_IN], BF16)
                load_w_bf16(k, stage, W["w_in"], w_in, 8, MLA_IN)
                xt = k.sb(p1, "xt", [128, 4, D], F32)
                xb = k.sb(p1, "xb", [128, 4, D], BF16)
                xT = k.sb(p1, "xT", [128, 8, 512], BF16)
                lat = k.sb(p1, "lat", [128, 4, 704], BF16)
                sg = [k.sb(p1, f"sg{i}", [128, 8, 512], BF16) for i in range(2)]
                cqo = [k.sb(p1, f"cqo{i}", [128, 3, 512], BF16) for i in range(2)]
                junk = k.sb(p1, "junk", [128, 384], BF16)
                stat = k.sb(p1, "stat", [128, 8], F32)
                rtmp = k.sb(p1, "rtmp", [128, 4, 64], F32)
                rtmp2 = k.sb(p1, "rtmp2", [128, 4, 64], F32)
                psT = [k.ps(p1, f"psT{i}", [128, 2, 512], BF16) for i in range(2)]
                psL = k.ps(p1, "psL", [128, 1024], F32)
                psLT = k.ps(p1, "psLT", [128, 4, 512], BF16)
                psG = [k.ps(p1, f"psG{i}", [128, 512], F32) for i in range(2)]

                for b in range(nblk):
                    if DBG_STOP == 11:
                        break
                    r0 = row0 + b * 512
                    c0 = b * 512
                    g0 = r0
                    t.dma("sp", xt.t[:], x_in[r0:r0 + 512, :].rearrange("(j p) d -> p j d", p=128), writes=[xt.r])
                    t.op("act", lambda e: e.copy(out=xb.t[:, 0:2, :], in_=xt.t[:, 0:2, :]), reads=[xt.r], writes=[xb.r])
                    t.op("act", lambda e: e.copy(out=xb.t[:, 2:4, :], in_=xt.t[:, 2:4, :]), reads=[xt.r], writes=[xb.r])
                    for kp in range(4):
                        pT = psT[kp % 2]
                        for kk in range(2):
                            kc = kp * 2 + kk
                            for j in range(4):
                                last = (kk == 1 and j == 3)
                                t.op("pe", lambda e, kc=kc, kk=kk, j=j, pT=pT: e.transpose(
                                    out=pT.t[:, kk, j * 128:(j + 1) * 128], in_=xb.t[:, j, kc * 128:(kc + 1) * 128],
                                    identity=ident.t[:]), reads=[xb.r, ident.r], writes=[pT.r], inc=last)
                        t.op("dve", lambda e, kp=kp, pT=pT: e.tensor_copy(out=xT.t[:, kp * 2:kp * 2 + 2, :], in_=pT.t[:]),
                             reads=[pT.r], writes=[xT.r])
                    if DBG_STOP == 12:
                        continue
                    sgb = sg[b % 2]

                    def gate_head(h, sgb=sgb):
                        pG = psG[h % 2]
                        for kc in range(8):
                            t.op("pe", lambda e, kc=kc: e.matmul(
                                out=pG.t[:], lhsT=w_in.t[:, kc, 704 + h * 128:704 + (h + 1) * 128], rhs=xT.t[:, kc, :],
                                start=(kc == 0), stop=(kc == 7)), reads=[xT.r, w_in.r], writes=[pG.r], inc=(kc == 7))
                        t.op("act", lambda e: e.activation(out=sgb.t[:, h, :], in_=pG.t[:], func=AF.Silu),
                             reads=[pG.r], writes=[sgb.r])
                    for j in range(4):
                        for kc in range(8):
                            t.op("pe", lambda e, kc=kc, j=j: e.matmul(
                                out=psL.t[:, 0:512], lhsT=xT.t[:, kc, j * 128:(j + 1) * 128], rhs=w_in.t[:, kc, 0:512],
                                start=(kc == 0), stop=(kc == 7)), reads=[xT.r, w_in.r], writes=[psL.r], inc=False)
                            t.op("pe", lambda e, kc=kc, j=j: e.matmul(
                                out=psL.t[:, 512:704], lhsT=xT.t[:, kc, j * 128:(j + 1) * 128], rhs=w_in.t[:, kc, 512:704],
                                start=(kc == 0), stop=(kc == 7)), reads=[xT.r, w_in.r], writes=[psL.r], inc=(kc == 7))
                        t.op("act", lambda e: e.activation(out=junk.t[:, 0:QLORA], in_=psL.t[:, 0:QLORA], func=AF.Square,
                                                           accum_out=stat.t[:, 0:1]), reads=[psL.r], writes=[junk.r, stat.r])
                        t.op("act", lambda e: e.activation(out=junk.t[:, 0:KVLORA], in_=psL.t[:, QLORA:QLORA + KVLORA], func=AF.Square,
                                                           accum_out=stat.t[:, 1:2]), reads=[psL.r], writes=[junk.r, stat.r])
                        t.op("dve", lambda e: e.tensor_scalar(out=stat.t[:, 2:3], in0=stat.t[:, 0:1], scalar1=1.0 / QLORA, scalar2=RMS_EPS,
                                                              op0=ALU.mult, op1=ALU.add), reads=[stat.r], writes=[stat.r])
                        t.op("dve", lambda e: e.tensor_scalar(out=stat.t[:, 3:4], in0=stat.t[:, 1:2], scalar1=1.0 / KVLORA, scalar2=RMS_EPS,
                                                              op0=ALU.mult, op1=ALU.add), reads=[stat.r], writes=[stat.r])
                        t.op("act", lambda e: e.activation(out=stat.t[:, 4:6], in_=stat.t[:, 2:4], func=AF.Sqrt), reads=[stat.r], writes=[stat.r])
                        t.op("dve", lambda e: e.reciprocal(out=stat.t[:, 6:8], in_=stat.t[:, 4:6]), reads=[stat.r], writes=[stat.r])
                        t.op("dve", lambda e, j=j: e.scalar_tensor_tensor(
                            out=lat.t[:, j, 0:QLORA], in0=psL.t[:, 0:QLORA], scalar=stat.t[:, 6:7], in1=gq.t[:, 0:QLORA],
                            op0=ALU.mult, op1=ALU.mult), reads=[psL.r, stat.r, gq.r], writes=[lat.r])
                        t.op("dve", lambda e, j=j: e.scalar_tensor_tensor(
                            out=lat.t[:, j, QLORA:640], in0=psL.t[:, QLORA:640], scalar=stat.t[:, 7:8], in1=gq.t[:, QLORA:640],
                            op0=ALU.mult, op1=ALU.mult), reads=[psL.r, stat.r, gq.r], writes=[lat.r])
                        ti = b * 4 + j
                        t.op("dve", lambda e, j=j, ti=ti: e.tensor_tensor(out=rtmp.t[:, j, :], in0=psL.t[:, 640:704], in1=cos2.t[:, ti, :], op=ALU.mult),
                             reads=[psL.r, cos2.r], writes=[rtmp.r])
                        t.op("dve", lambda e, j=j, ti=ti: e.tensor_tensor(out=rtmp2.t[:, j, 0:32], in0=psL.t[:, 672:704], in1=sinpm.t[:, ti, 0:32], op=ALU.mult),
                             reads=[psL.r, sinpm.r], writes=[rtmp2.r])
                        t.op("dve", lambda e, j=j, ti=ti: e.tensor_tensor(out=rtmp2.t[:, j, 32:64], in0=psL.t[:, 640:672], in1=sinpm.t[:, ti, 32:64], op=ALU.mult),
                             reads=[psL.r, sinpm.r], writes=[rtmp2.r])
                        t.op("dve", lambda e, j=j: e.tensor_tensor(out=lat.t[:, j, 640:704], in0=rtmp.t[:, j, :], in1=rtmp2.t[:, j, :], op=ALU.add),
                             reads=[rtmp.r, rtmp2.r], writes=[lat.r])
                        if DBG_STOP not in (13, 14, 15):
                            gate_head(2 * j)
                            gate_head(2 * j + 1)
                    if DBG_STOP == 13:
                        continue
                    for j in range(4):
                        for c in range(6):
                            w = 128 if c < 5 else 64
                            last = (j == 3 and c == 5)
                            dstp = psLT if c < 4 else psT[0]
                            cc = c if c < 4 else c - 4
                            t.op("pe", lambda e, j=j, c=c, w=w, dstp=dstp, cc=cc: e.transpose(
                                out=dstp.t[0:w, cc, j * 128:(j + 1) * 128], in_=lat.t[:, j, c * 128:c * 128 + w], identity=ident.t[:]),
                                reads=[lat.r, ident.r], writes=[dstp.r], inc=(last or (j == 3 and c == 3)))
                    cq = cqo[b % 2]
                    if DBG_MASK & 1:
                        t.op("dve", lambda e, cq=cq: e.tensor_copy(out=cq.t[:], in_=psLT.t[:, 0:3, :]), reads=[psLT.r], writes=[cq.r])
                    if DBG_MASK & 2:
                        t.op("act", lambda e, c0=c0: e.copy(out=ckvT.t[:, 0, c0:c0 + 512], in_=psLT.t[:, 3, :]), reads=[psLT.r], writes=[])
                    if DBG_MASK & 4:
                        t.op("act", lambda e, c0=c0: e.copy(out=ckvT.t[:, 1, c0:c0 + 512], in_=psT[0].t[:, 0, :]), reads=[psT[0].r], writes=[])
                    if DBG_MASK & 8:
                        t.op("dve", lambda e, c0=c0: e.tensor_copy(out=krT.t[0:64, c0:c0 + 512], in_=psT[0].t[0:64, 1, :]), reads=[psT[0].r], writes=[])
                    if DBG_STOP != 15:
                        t.dma(STQ, scr["CQ"][:, g0:g0 + 512].rearrange("(c p) t -> p c t", p=128), cq.t[:], reads=[cq.r])
                    t.dma(STQ, scr["SG"][:, g0:g0 + 512].rearrange("(h p) t -> p h t", p=128), sgb.t[:], reads=[sgb.r])
            t.barrier()
            if DBG_STOP in (1, 11, 12, 13, 14, 15):
                continue
            with ExitStack() as p2:
                w_q = k.sb(p2, "w_q", [128, 3, 1536], BF16)
                w_kv = k.sb(p2, "w_kv", [128, 2, 2048], BF16)
                load_w_bf16(k, stage, W["w_q_up"], w_q, 3, 1536)
                load_w_bf16(k, stage, W["w_kv_up"], w_kv, 2, 2048)
                KT = k.sb(p2, "KT", [128, Lmax], BF16)
                V = k.sb(p2, "V", [128, Lmax // 128, 128], BF16)
                cqi = [k.sb(p2, f"cqi{i}", [128, 3, 512], BF16) for i in range(2)]
                sgi = [k.sb(p2, f"sgi{i}", [128, 512], BF16) for i in range(2)]
                ogo = [k.sb(p2, f"ogo{i}", [128, 512], BF16) for i in range(2)]
                qtok = k.sb(p2, "qtok", [128, 4, 192], BF16)
                qa = k.sb(p2, "qa", [128, 4, 64], F32)
                qb = k.sb(p2, "qb", [128, 4, 64], F32)
                QnT = [k.sb(p2, f"QnT{i}", [128, 512], BF16) for i in range(2)]
                QrT = [k.sb(p2, f"QrT{i}", [128, 512], BF16) for i in range(2)]
                for qq in QrT:
                    t.op("pool", lambda e, qq=qq: e.memset(qq.t[64:128, :], 0.0), reads=[], writes=[qq.r])
                NPT = 12
                pt = [k.sb(p2, f"pt{i}", [128, 512], BF16) for i in range(NPT)]
                rec = k.sb(p2, "rec", [128, 512], F32)
                otmp = k.sb(p2, "otmp", [128, 512], F32)
                psS = [k.ps(p2, f"psS{i}", [128, 512], F32) for i in range(3)]
                psO = [k.ps(p2, f"psO{i}", [128, 512], F32) for i in range(2)]
                psD = [k.ps(p2, f"psD{i}", [128, 512], F32) for i in range(1)]
                accsb = k.sb(p2, "accsb", [128, 512], F32)
                accP = k.sb(p2, "accP", [128, 512], F32)
                psQ = k.ps(p2, "psQ", [128, 2, 256], F32)
                psQT = k.ps(p2, "psQT", [128, 2, 512], BF16)

                nkb = L // 128
                qi0 = 0
                for _ in range(1):
                    pass

                items = [(h, qt) for h in range(NH) for qt in range(nblk)]

                def prologue(idx):
                    h, qt = items[idx]
                    g0 = row0 + qt * 512
                    bi = (qi0 + idx) % 2
                    cq, sgt, qn, qr = cqi[bi], sgi[bi], QnT[bi], QrT[bi]

                    def p0():
                        t.dma("sp", cq.t[:], scr["CQ"][:, g0:g0 + 512].rearrange("(c p) t -> p c t", p=128), writes=[cq.r])
                        t.dma("sp", sgt.t[:], scr["SG"][h * 128:(h + 1) * 128, g0:g0 + 512], writes=[sgt.r])

                    def phalf(half):
                        for jj in range(2):
                            j = half * 2 + jj
                            for kc in range(3):
                                t.op("pe", lambda e, kc=kc, j=j, jj=jj: e.matmul(
                                    out=psQ.t[:, jj, 0:192], lhsT=cq.t[:, kc, j * 128:(j + 1) * 128], rhs=w_q.t[:, kc, h * 192:(h + 1) * 192],
                                    start=(kc == 0), stop=(kc == 2)), reads=[cq.r, w_q.r], writes=[psQ.r], inc=(kc == 2 and jj == 1))
                        j0 = half * 2
                        ti0 = qt * 4 + j0
                        t.op("act", lambda e: e.copy(out=qtok.t[:, j0:j0 + 2, 0:128], in_=psQ.t[:, :, 0:128]), reads=[psQ.r], writes=[qtok.r])
                        t.op("dve", lambda e: e.tensor_tensor(out=qa.t[:, j0:j0 + 2, :], in0=psQ.t[:, :, 128:192], in1=cos2.t[:, ti0:ti0 + 2, :], op=ALU.mult),
                             reads=[psQ.r, cos2.r], writes=[qa.r])
                        t.op("dve", lambda e: e.tensor_tensor(out=qb.t[:, j0:j0 + 2, 0:32], in0=psQ.t[:, :, 160:192], in1=sinpm.t[:, ti0:ti0 + 2, 0:32], op=ALU.mult),
                             reads=[psQ.r, sinpm.r], writes=[qb.r])
                        t.op("dve", lambda e: e.tensor_tensor(out=qb.t[:, j0:j0 + 2, 32:64], in0=psQ.t[:, :, 128:160], in1=sinpm.t[:, ti0:ti0 + 2, 32:64], op=ALU.mult),
                             reads=[psQ.r, sinpm.r], writes=[qb.r])
                        t.op("dve", lambda e: e.tensor_tensor(out=qtok.t[:, j0:j0 + 2, 128:192], in0=qa.t[:, j0:j0 + 2, :], in1=qb.t[:, j0:j0 + 2, :], op=ALU.add),
                             reads=[qa.r, qb.r], writes=[qtok.r])

                    def p3():
                        for j in range(4):
                            t.op("pe", lambda e, j=j: e.transpose(out=psQT.t[:, 0, j * 128:(j + 1) * 128], in_=qtok.t[:, j, 0:128], identity=ident.t[:]),
                                 reads=[qtok.r, ident.r], writes=[psQT.r], inc=False)
                            t.op("pe", lambda e, j=j: e.transpose(out=psQT.t[0:64, 1, j * 128:(j + 1) * 128], in_=qtok.t[:, j, 128:192], identity=ident.t[:]),
                                 reads=[qtok.r, ident.r], writes=[psQT.r], inc=(j == 3))
                        t.op("act", lambda e: e.copy(out=qn.t[:], in_=psQT.t[:, 0, :]), reads=[psQT.r], writes=[qn.r])
                        t.op("act", lambda e: e.copy(out=qr.t[0:64, :], in_=psQT.t[0:64, 1, :]), reads=[psQT.r], writes=[qr.r])

                    return [p0, lambda: phalf(0), lambda: phalf(1), p3]

                def kv_head(h):
                    banks = [psQ.t[:].rearrange("p a b -> p (a b)"), psS[0].t[:], psS[1].t[:], psS[2].t[:]]
                    bres = [psQ.r, psS[0].r, psS[1].r, psS[2].r]
                    n = 0
                    for b in range(nblk):
                        bk, br = banks[n % 4], bres[n % 4]
                        n += 1
                        for kc in range(2):
                            t.op("pe", lambda e, kc=kc, b=b, bk=bk: e.matmul(
                                out=bk, lhsT=w_kv.t[:, kc, h * 256:h * 256 + 128],
                                rhs=ckvT.t[:, kc, b * 512:(b + 1) * 512], start=(kc == 0), stop=(kc == 1)),
                                reads=[w_kv.r], writes=[br], inc=(kc == 1))
                        t.op("dve", lambda e, b=b, bk=bk: e.tensor_copy(out=KT.t[:, b * 512:(b + 1) * 512], in_=bk),
                             reads=[br], writes=[KT.r])
                    for b in range(nblk):
                        bk, br = banks[n % 4], bres[n % 4]
                        n += 1
                        for j in range(4):
                            ti = b * 4 + j
                            for kc in range(2):
                                t.op("pe", lambda e, kc=kc, ti=ti, j=j, bk=bk: e.matmul(
                                    out=bk[:, j * 128:(j + 1) * 128], lhsT=ckvT.t[:, kc, ti * 128:(ti + 1) * 128],
                                    rhs=w_kv.t[:, kc, h * 256 + 128:h * 256 + 256], start=(kc == 0), stop=(kc == 1)),
                                    reads=[w_kv.r], writes=[br], inc=(kc == 1 and j == 3))
                        t.op("act", lambda e, b=b, bk=bk: e.copy(out=V.t[:, b * 4:(b + 1) * 4, :].rearrange("p a b -> p (a b)"), in_=bk),
                             reads=[br], writes=[V.r])

                for pc in prologue(0):
                    pc()
                for idx, (h, qt) in enumerate(items):
                    if qt == 0:
                        kv_head(h)
                    g0 = row0 + qt * 512
                    bi = (qi0 + idx) % 2
                    sgt, og, qn, qr = sgi[bi], ogo[bi], QnT[bi], QrT[bi]
                    pO, pD = psO[bi], psD[0]
                    nxt = prologue(idx + 1) if idx + 1 < len(items) else []
                    when = {0: 0, max(1, nkb // 4): 1, max(2, nkb // 2): 2, max(3, (3 * nkb) // 4): 3}

                    def issue_S(kb, qn=qn, qr=qr):
                        pS = psS[kb % 3]
                        t.op("pe", lambda e: e.matmul(out=pS.t[:], lhsT=KT.t[:, kb * 128:(kb + 1) * 128], rhs=qn.t[:], start=True, stop=False),
                             reads=[KT.r, qn.r], writes=[pS.r], inc=False)
                        t.op("pe", lambda e: e.matmul(out=pS.t[:], lhsT=krT.t[:, kb * 128:(kb + 1) * 128], rhs=qr.t[:], start=False, stop=True),
                             reads=[qr.r], writes=[pS.r], inc=True)
                        p = pt[kb % NPT]
                        t.op("act", lambda e: e.activation(out=p.t[:], in_=pS.t[:], func=AF.Exp, scale=ATTN_SCALE), reads=[pS.r], writes=[p.r])

                    def issue_O(kb, pO=pO, pD=pD):
                        p = pt[kb % NPT]
                        t.op("pe", lambda e: e.matmul(out=pO.t[:], lhsT=V.t[:, kb, :], rhs=p.t[:], start=(kb == 0), stop=(kb == nkb - 1)),
                             reads=[V.r, p.r], writes=[pO.r], inc=(not ATT_ONES))
                        if ATT_ONES:
                            t.op("pe", lambda e: e.matmul(out=pD.t[:], lhsT=ones.t[:], rhs=p.t[:], start=(kb == 0), stop=(kb == nkb - 1)),
                                 reads=[ones.r, p.r], writes=[pD.r], inc=True)
                        elif kb % POOL_EVERY == POOL_EVERY - 1:
                            if kb == POOL_EVERY - 1:
                                t.op("pool", lambda e: e.tensor_copy(out=accP.t[:], in_=p.t[:]), reads=[p.r], writes=[accP.r])
                            else:
                                t.op("pool", lambda e: e.tensor_tensor(out=accP.t[:], in0=accP.t[:], in1=p.t[:], op=ALU.add), reads=[p.r, accP.r], writes=[accP.r])
                        elif kb == 0:
                            t.op("dve", lambda e: e.tensor_copy(out=pD.t[:], in_=p.t[:]), reads=[p.r], writes=[pD.r])
                        else:
                            t.op("dve", lambda e: e.tensor_tensor(out=pD.t[:], in0=pD.t[:], in1=p.t[:], op=ALU.add), reads=[p.r, pD.r], writes=[pD.r])

                    issue_S(0)
                    if nkb > 1:
                        issue_S(1)
                    for kb in range(nkb):
                        if kb + 2 < nkb:
                            issue_S(kb + 2)
                        issue_O(kb)
                        if nxt and kb in when:
                            nxt[when[kb]]()
                    if not ATT_ONES:
                        t.op("dve", lambda e, pD=pD: e.tensor_copy(out=accsb.t[:], in_=pD.t[:]), reads=[pD.r], writes=[accsb.r])
                        t.op("pe", lambda e, pD=pD: e.matmul(out=pD.t[:], lhsT=cst["ones32"].t[:], rhs=accsb.t[:], start=True, stop=False),
                             reads=[cst["ones32"].r, accsb.r], writes=[pD.r], inc=False)
                        t.op("pe", lambda e, pD=pD: e.matmul(out=pD.t[:], lhsT=cst["ones32"].t[:], rhs=accP.t[:], start=False, stop=True),
                             reads=[cst["ones32"].r, accP.r], writes=[pD.r], inc=True)
                    t.op("dve", lambda e, pD=pD: e.reciprocal(out=rec.t[:], in_=pD.t[:]), reads=[pD.r], writes=[rec.r])
                    t.op("dve", lambda e, pO=pO: e.tensor_tensor(out=otmp.t[:], in0=pO.t[:], in1=rec.t[:], op=ALU.mult), reads=[pO.r, rec.r], writes=[otmp.r])
                    t.op("dve", lambda e, og=og, sgt=sgt: e.tensor_tensor(out=og.t[:], in0=otmp.t[:], in1=sgt.t[:], op=ALU.mult), reads=[otmp.r, sgt.r], writes=[og.r])
                    t.dma(STQ, scr["OG"][h * 128:(h + 1) * 128, g0:g0 + 512], og.t[:], reads=[og.r])
                qi0 += len(items)
            t.barrier()

    if DBG_STOP in (1, 2, 11, 12, 13, 14, 15):
        return
    t.barrier()
    with ExitStack() as p3:
        w_o = k.sb(p3, "w_o", [128, 8, D], BF16)
        with ExitStack() as sst:
            stage3 = [k.sb(sst, f"stage{i}", [128, 2048], F32) for i in range(2)]
            load_w_bf16(k, stage3, W["w_out"], w_o, 8, D)
            t.barrier()
        out_proj_ln(k, p3, x_in, x_out, W, w_o, scr["OG"], T_total, tok_perm=None)
    t.barrier()


def out_proj_ln(k, st, x_in, x_out, W, w_o, OGT, T_total, tok_perm=None):
    nc, t = k.nc, k.trk
    lng = k.sb(st, "lng", [128, D], F32)
    lnb = k.sb(st, "lnb", [128, D], F32)
    t.dma("sp", lng.t[:], W["ln_g"].partition_broadcast(128), writes=[lng.r])
    t.dma("sp", lnb.t[:], W["ln_b"].partition_broadcast(128), writes=[lnb.r])
    ogi = [k.sb(st, f"ogi{i}", [128, 8, 512], BF16) for i in range(2)]
    xi = [k.sb(st, f"xi{i}", [128, 4, D], F32) for i in range(2)]
    yo = [k.sb(st, f"yo{i}", [128, 4, D], F32) for i in range(2)]
    zs, bsts, mvs = ln_scratch(k, st)
    psY = [k.ps(st, f"psY{i}", [128, 1024], F32) for i in range(2)]
    for b in range(T_total // 512):
        r0 = b * 512
        og, x, y = ogi[b % 2], xi[b % 2], yo[b % 2]
        t.dma("sp", og.t[:], OGT[:, r0:r0 + 512].rearrange("(h p) t -> p h t", p=128), writes=[og.r])
        t.dma("sp", x.t[:], x_in[r0:r0 + 512, :].rearrange("(j p) d -> p j d", p=128), writes=[x.r])
        for j in range(4):
            pY = psY[j % 2]
            for half in range(2):
                for h in range(8):
                    t.op("pe", lambda e, h=h, j=j, half=half, pY=pY, og=og: e.matmul(
                        out=pY.t[:, half * 512:(half + 1) * 512], lhsT=og.t[:, h, j * 128:(j + 1) * 128], rhs=w_o.t[:, h, half * 512:(half + 1) * 512],
                        start=(h == 0), stop=(h == 7)), reads=[og.r, w_o.r], writes=[pY.r], inc=(h == 7 and half == 1))
            ln_tail(k, x.t[:, j, :], x.r, pY, zs, bsts, mvs, lng, lnb, y.t[:, j, :], y.r, j)
        t.dma(STQ, x_out[r0:r0 + 512, :].rearrange("(j p) d -> p j d", p=128), y.t[:], reads=[y.r])


def ln_tail(k, x_ap, x_r, pY, zs, bsts, mvs, lng, lnb, y_ap, y_r, i):
    t = k.trk
    z, z2, bst, mv = zs[0][i % 2], zs[1][i % 2], bsts[i % 2], mvs[i % 2]
    t.op("dve", lambda e: e.scalar_tensor_tensor(out=z.t[:], in0=x_ap, scalar=ALPHA, in1=pY.t[:], op0=ALU.mult, op1=ALU.add),
         reads=[x_r, pY.r], writes=[z.r])
    t.op("dve", lambda e: e.bn_stats(out=bst.t[:, 0, :], in_=z.t[:, 0:512]), reads=[z.r], writes=[bst.r])
    t.op("dve", lambda e: e.bn_stats(out=bst.t[:, 1, :], in_=z.t[:, 512:1024]), reads=[z.r], writes=[bst.r])
    t.op("dve", lambda e: e.bn_aggr(out=mv.t[:, 0:2], in_=bst.t[:].rearrange("p a b -> p (a b)")), reads=[bst.r], writes=[mv.r])
    t.op("act", lambda e: e.activation(out=mv.t[:, 2:3], in_=mv.t[:, 1:2], func=AF.Sqrt, bias=k.eps_ln.t[:, 0:1]), reads=[mv.r], writes=[mv.r])
    t.op("dve", lambda e: e.reciprocal(out=mv.t[:, 3:4], in_=mv.t[:, 2:3]), reads=[mv.r], writes=[mv.r])
    t.op("dve", lambda e: e.tensor_scalar(out=z.t[:], in0=z.t[:], scalar1=mv.t[:, 0:1], scalar2=mv.t[:, 3:4], op0=ALU.subtract, op1=ALU.mult),
         reads=[z.r, mv.r], writes=[z.r])
    t.op("dve", lambda e: e.tensor_tensor(out=z2.t[:], in0=z.t[:], in1=lng.t[:], op=ALU.mult), reads=[z.r, lng.r], writes=[z2.r])
    t.op("pool", lambda e: e.tensor_tensor(out=y_ap, in0=z2.t[:], in1=lnb.t[:], op=ALU.add), reads=[z2.r, lnb.r], writes=[y_r])


def ln_scratch(k, st):
    zs = ([k.sb(st, f"z{i}", [128, D], F32) for i in range(2)], [k.sb(st, f"zz{i}", [128, D], F32) for i in range(2)])
    bsts = [k.sb(st, f"bst{i}", [128, 2, 6], F32) for i in range(2)]
    mvs = [k.sb(st, f"mv{i}", [128, 4], F32) for i in range(2)]
    return zs, bsts, mvs


TWO_PI = 2.0 * math.pi
GELU_C = 2.0 * math.sqrt(2.0 / math.pi)


def s5_setup(k, cst, W, SW, lam):
    nc, t = k.nc, k.trk
    id32, maskf, maskb = cst["id32"], cst["maskf"], cst["maskb"]
    dve = lambda fn, rd, wr: t.op("dve", fn, reads=rd, writes=wr)
    with ExitStack() as su:
        rs = Res("s5small")
        SM = k.sb(su, "SM", [128, 40, 64], F32)
        PW = k.sb(su, "PW", [128, 16, 2, 64], F32)
        BT = k.sb(su, "BT", [128, 2, 2, 16, 32], F32)
        CT = k.sb(su, "CT", [128, 2, 2, 16, 32], F32)
        BB = k.sb(su, "BB", [128, 2, 2, 16, 32], F32)
        W3 = k.sb(su, "W3", [128, 2, 32, 128], BF16)
        W1T = k.sb(su, "W1T", [128, 2, 32, 128], BF16)
        TOE = k.sb(su, "TOE", [128, 64, 128], BF16)
        TAC = k.sb(su, "TAC", [128, 64, 128], F32)
        dcol = k.sb(su, "dcol", [128, 64], F32)
        psA = k.ps(su, "psA", [128, 512], F32)
        psB = k.ps(su, "psB", [128, 512], F32)
        sm = lambda i: SM.t[:, i, :]
        AR, AI, LS, STEP, LR, LI, M_, R_, TMP, TH, T2, SN, CS, T1, T2b, T3, LBr, LBi, DEN, MUr, MUi, NR, Qr, Qi = range(24)

        with ExitStack() as s1:
            praw = k.sb(s1, "praw", [32, 3, 2, 128], F32)
            ls = k.sb(s1, "ls", [32, 2, 2], F32)
            zer = k.sb(s1, "zer", [32, 64], F32)
            for d in range(2):
                t.dma("sp", praw.t[:, 0, d, :], W["a_re"][d].rearrange("(gb g2) n -> gb (g2 n)", g2=2), writes=[praw.r])
                t.dma("sp", praw.t[:, 1, d, :], W["a_im"][d].rearrange("(gb g2) n -> gb (g2 n)", g2=2), writes=[praw.r])
                t.dma("sp", ls.t[:, d, :], W["log_step"][d].rearrange("(gb g2) -> gb g2", g2=2), writes=[ls.r])
            dve(lambda e: e.memset(zer.t[:], 0.0), [], [zer.r])
            for d in range(2):
                for g2 in range(2):
                    dve(lambda e, d=d, g2=g2: e.tensor_scalar(out=praw.t[:, 2, d, g2 * 64:(g2 + 1) * 64], in0=zer.t[:], scalar1=ls.t[:, d, g2:g2 + 1],
                                                              scalar2=None, op0=ALU.add), [zer.r, ls.r, praw.r], [praw.r])
            for j in range(3):
                for d in range(2):
                    i = j * 2 + d
                    t.op("pe", lambda e, j=j, d=d, i=i: e.transpose(out=psA.t[:, i * 32:(i + 1) * 32], in_=praw.t[:, j, d, :], identity=id32.t[0:32, 0:32]),
                         reads=[praw.r, id32.r], writes=[psA.r], inc=(i == 5))
            dve(lambda e: e.tensor_copy(out=SM.t[:, 0:3, :].rearrange("q a b -> q (a b)"), in_=psA.t[:, 0:192]), [psA.r], [rs])

        t.barrier()

        if DBG_STOP == 31:
            return
        def tt(o, a, b, op):
            dve(lambda e: e.tensor_tensor(out=sm(o), in0=sm(a), in1=sm(b), op=op), [rs], [rs])

        def ts(o, a, s1_, s2_, op0, op1=None):
            if op1 is None:
                dve(lambda e: e.tensor_scalar(out=sm(o), in0=sm(a), scalar1=s1_, scalar2=None, op0=op0), [rs], [rs])
            else:
                dve(lambda e: e.tensor_scalar(out=sm(o), in0=sm(a), scalar1=s1_, scalar2=s2_, op0=op0, op1=op1), [rs], [rs])

        t.op("act", lambda e: e.activation(out=sm(STEP), in_=sm(LS), func=AF.Exp), reads=[rs], writes=[rs])
        tt(LR, AR, STEP, ALU.mult)
        tt(LI, AI, STEP, ALU.mult)
        ts(M_, LR, 1.0 / 120, 1.0 / 24, ALU.mult, ALU.add)
        for cco in (1.0 / 6, 0.5, 1.0, 1.0):
            tt(M_, M_, LR, ALU.mult)
            ts(M_, M_, cco, None, ALU.add)
        ts(R_, LI, 1.0, None, ALU.mult)
        for m in range(1, 6):
            ts(TMP, LI, TWO_PI * m, -TWO_PI, ALU.is_ge, ALU.mult)
            tt(R_, R_, TMP, ALU.add)
        ts(TH, R_, -math.pi, 0.125, ALU.add, ALU.mult)
        tt(T2, TH, TH, ALU.mult)
        ts(SN, T2, -1.0 / 5040, 1.0 / 120, ALU.mult, ALU.add)
        for cco in (-1.0 / 6, 1.0):
            tt(SN, SN, T2, ALU.mult)
            ts(SN, SN, cco, None, ALU.add)
        tt(SN, SN, TH, ALU.mult)
        ts(CS, T2, 1.0 / 40320, -1.0 / 720, ALU.mult, ALU.add)
        for cco in (1.0 / 24, -0.5, 1.0):
            tt(CS, CS, T2, ALU.mult)
            ts(CS, CS, cco, None, ALU.add)
        for _ in range(3):
            tt(T3, CS, SN, ALU.mult)
            tt(T1, CS, CS, ALU.mult)
            tt(T2b, SN, SN, ALU.mult)
            tt(CS, T1, T2b, ALU.subtract)
            ts(SN, T3, 2.0, None, ALU.mult)
        tt(LBr, M_, CS, ALU.mult)
        ts(LBr, LBr, -1.0, None, ALU.mult)
        tt(LBi, M_, SN, ALU.mult)
        ts(LBi, LBi, -1.0, None, ALU.mult)
        tt(T1, LBr, LBr, ALU.mult)
        tt(T2b, LBi, LBi, ALU.mult)
        tt(DEN, T1, T2b, ALU.add)
        dve(lambda e: e.reciprocal(out=sm(DEN), in_=sm(DEN)), [rs], [rs])
        tt(MUr, LBr, DEN, ALU.mult)
        tt(MUi, LBi, DEN, ALU.mult)
        ts(MUi, MUi, -1.0, None, ALU.mult)
        ts(NR, LBr, -1.0, None, ALU.add)
        tt(T1, AR, AR, ALU.mult)
        tt(T2b, AI, AI, ALU.mult)
        tt(DEN, T1, T2b, ALU.add)
        dve(lambda e: e.reciprocal(out=sm(DEN), in_=sm(DEN)), [rs], [rs])
        tt(T1, NR, AR, ALU.mult)
        tt(T2b, LBi, AI, ALU.mult)
        tt(Qr, T1, T2b, ALU.add)
        tt(Qr, Qr, DEN, ALU.mult)
        tt(T1, LBi, AR, ALU.mult)
        tt(T2b, NR, AI, ALU.mult)
        tt(Qi, T1, T2b, ALU.subtract)
        tt(Qi, Qi, DEN, ALU.mult)
        pw = lambda kk, ri: PW.t[:, kk, ri, :]
        dve(lambda e: e.memset(pw(7, 0), 1.0), [rs], [rs])
        dve(lambda e: e.memset(pw(7, 1), 0.0), [rs], [rs])

        def cmul_pw(ko, ki, br, bi):
            dve(lambda e: e.tensor_tensor(out=sm(T1), in0=pw(ki, 0), in1=sm(br), op=ALU.mult), [rs], [rs])
            dve(lambda e: e.tensor_tensor(out=sm(T2b), in0=pw(ki, 1), in1=sm(bi), op=ALU.mult), [rs], [rs])
            dve(lambda e: e.tensor_tensor(out=pw(ko, 0), in0=sm(T1), in1=sm(T2b), op=ALU.subtract), [rs], [rs])
            dve(lambda e: e.tensor_tensor(out=sm(T1), in0=pw(ki, 0), in1=sm(bi), op=ALU.mult), [rs], [rs])
            dve(lambda e: e.tensor_tensor(out=sm(T2b), in0=pw(ki, 1), in1=sm(br), op=ALU.mult), [rs], [rs])
            dve(lambda e: e.tensor_tensor(out=pw(ko, 1), in0=sm(T1), in1=sm(T2b), op=ALU.add), [rs], [rs])

        for kk in range(7, 15):
            cmul_pw(kk + 1, kk, LBr, LBi)
        for kk in range(7, 0, -1):
            cmul_pw(kk - 1, kk, MUr, MUi)
        for d in range(2):
            for r in range(2):
                dve(lambda e, d=d, r=r: e.tensor_copy(out=lam["A"].t[:, d, r, :], in_=PW.t[:, 15, 0, d * 32:(d + 1) * 32]), [rs], [lam["A"].r])
            dve(lambda e, d=d: e.tensor_copy(out=lam["I"].t[:, d, :], in_=PW.t[:, 15, 1, d * 32:(d + 1) * 32]), [rs], [lam["I"].r])
            for a_ in range(2):
                dve(lambda e, d=d, a_=a_: e.tensor_copy(out=lam["L4"].t[:, d, a_, 0, :], in_=PW.t[:, 15, 0, d * 32:(d + 1) * 32]), [rs, lam["L4"].r], [lam["L4"].r])
            dve(lambda e, d=d: e.tensor_copy(out=lam["L4"].t[:, d, 0, 1, :], in_=PW.t[:, 15, 1, d * 32:(d + 1) * 32]), [rs, lam["L4"].r], [lam["L4"].r])
            dve(lambda e, d=d: e.tensor_scalar(out=lam["L4"].t[:, d, 1, 1, :], in0=PW.t[:, 15, 1, d * 32:(d + 1) * 32], scalar1=-1.0, scalar2=None, op0=ALU.mult),
                [rs, lam["L4"].r], [lam["L4"].r])
            dve(lambda e, d=d: e.tensor_copy(out=lam["B"].t[:, d, 1, :], in_=PW.t[:, 15, 1, d * 32:(d + 1) * 32]), [rs], [lam["B"].r])
            dve(lambda e, d=d: e.tensor_scalar(out=lam["B"].t[:, d, 0, :], in0=PW.t[:, 15, 1, d * 32:(d + 1) * 32], scalar1=-1.0, scalar2=None, op0=ALU.mult),
                [rs, lam["B"].r], [lam["B"].r])
            dve(lambda e, d=d: e.tensor_scalar(out=lam["NI"].t[:, d, :], in0=PW.t[:, 15, 1, d * 32:(d + 1) * 32], scalar1=-1.0, scalar2=None, op0=ALU.mult),
                [rs], [lam["NI"].r])

        if DBG_STOP == 32:
            return
        with ExitStack() as s2:
            raw = k.sb(s2, "raw", [32, 2, 2048], F32)
            raw2 = k.sb(s2, "raw2", [32, 2, 2048], F32)
            for (src_r, src_i, dst, isC) in ((W["b_re"], W["b_im"], BT, False), (W["c_re"], W["c_im"], CT, True)):
                for d in range(2):
                    pat = "(gb g2) p n -> gb (g2 p n)" if isC else "(gb g2) n p -> gb (g2 n p)"
                    t.dma("sp", raw.t[:, 0, :], src_r[d].rearrange(pat, g2=2), writes=[raw.r])
                    t.dma("sp", raw.t[:, 1, :], src_i[d].rearrange(pat, g2=2), writes=[raw.r])
                    for ri in range(2):
                        ps = psA if ri == 0 else psB
                        if isC:
                            dve(lambda e, ri=ri: e.tensor_copy(out=raw2.t[:, ri, :].rearrange("q (p g n) -> q p g n", p=16, g=2),
                                                               in_=raw.t[:, ri, :].rearrange("q (g p n) -> q p g n", g=2, p=16)), [raw.r], [raw2.r])
                        for pp in range(16):
                            if isC:
                                src = raw2.t[:, ri, pp * 128:(pp + 1) * 128]
                            else:
                                src = raw.t[:, ri, :].rearrange("q (m p) -> q m p", p=16)[:, :, pp]
                            t.op("pe", lambda e, src=src, pp=pp, ps=ps: e.transpose(out=ps.t[:, pp * 32:(pp + 1) * 32], in_=src, identity=id32.t[0:32, 0:32]),
                                 reads=[raw.r, raw2.r, id32.r], writes=[ps.r], inc=(pp == 15))
                        dve(lambda e, d=d, ri=ri, ps=ps, dst=dst: e.tensor_copy(out=dst.t[:, d, ri, :, :].rearrange("q a b -> q (a b)"), in_=ps.t[:]), [ps.r], [dst.r])
        t.barrier()
        if DBG_STOP == 33:
            return
        with ExitStack() as s3:
            u1 = k.sb(s3, "u1", [128, 16, 32], F32)
            u2 = k.sb(s3, "u2", [128, 16, 32], F32)
            for d in range(2):
                qr = SM.t[:, Qr, d * 32:(d + 1) * 32].unsqueeze(1).broadcast_to([128, 16, 32])
                qi = SM.t[:, Qi, d * 32:(d + 1) * 32].unsqueeze(1).broadcast_to([128, 16, 32])
                br, bi = BT.t[:, d, 0, :, :], BT.t[:, d, 1, :, :]
                dve(lambda e: e.tensor_tensor(out=u1.t[:], in0=br, in1=qr, op=ALU.mult), [rs, BT.r], [u1.r])
                dve(lambda e: e.tensor_tensor(out=u2.t[:], in0=bi, in1=qi, op=ALU.mult), [rs, BT.r], [u2.r])
                dve(lambda e, d=d: e.tensor_tensor(out=BB.t[:, d, 0, :, :], in0=u1.t[:], in1=u2.t[:], op=ALU.subtract), [u1.r, u2.r], [BB.r])
                dve(lambda e: e.tensor_tensor(out=u1.t[:], in0=bi, in1=qr, op=ALU.mult), [rs, BT.r, BB.r], [u1.r])
                dve(lambda e: e.tensor_tensor(out=u2.t[:], in0=br, in1=qi, op=ALU.mult), [rs, BT.r, BB.r], [u2.r])
                dve(lambda e, d=d: e.tensor_tensor(out=BB.t[:, d, 1, :, :], in0=u1.t[:], in1=u2.t[:], op=ALU.add), [u1.r, u2.r], [BB.r])
        t.barrier()
        with nc.allow_non_contiguous_dma(reason="tiny param relayout"):
            for s in range(8):
                t.dma("sp", dcol.t[16 * s:16 * s + 16, :], W["d"].rearrange("(g p) -> p g", p=16), writes=[dcol.r])
        if DBG_STOP == 34:
            return
        with ExitStack() as s4:
            V1 = k.sb(s4, "V1", [128, 2, 16, 8, 16], F32)
            Z = k.sb(s4, "Z", [128, 2, 16, 8, 16], F32)
            v1 = k.sb(s4, "v1", [128, 16, 16], F32)
            v2 = k.sb(s4, "v2", [128, 16, 16], F32)
            tq = k.sb(s4, "tq", [128, 4, 128], F32)
            for d in range(2):
                for h2 in range(2):
                    g0 = h2 * 16
                    pwv = lambda kk, ri: PW.t[:, kk, ri, d * 32 + g0:d * 32 + g0 + 16].unsqueeze(2).broadcast_to([128, 16, 16])
                    bbv = lambda ri: BB.t[:, d, ri, :, g0:g0 + 16].rearrange("q p g -> q g p")
                    ctv = lambda ri: CT.t[:, d, ri, :, g0:g0 + 16].rearrange("q p g -> q g p")

                    def cprod(out_r, out_i, a_r, a_i, kk, neg_i, rd):
                        wr = rd[-1:]
                        dve(lambda e: e.tensor_tensor(out=v1.t[:], in0=a_r, in1=pwv(kk, 0), op=ALU.mult), [rs] + rd[:-1], [v1.r])
                        dve(lambda e: e.tensor_tensor(out=v2.t[:], in0=a_i, in1=pwv(kk, 1), op=ALU.mult), [rs] + rd[:-1], [v2.r])
                        dve(lambda e: e.tensor_tensor(out=out_r, in0=v1.t[:], in1=v2.t[:], op=ALU.subtract), [v1.r, v2.r], wr)
                        dve(lambda e: e.tensor_tensor(out=v1.t[:], in0=a_r, in1=pwv(kk, 1), op=ALU.mult), [rs] + rd[:-1] + wr, [v1.r])
                        dve(lambda e: e.tensor_tensor(out=v2.t[:], in0=a_i, in1=pwv(kk, 0), op=ALU.mult), [rs] + rd[:-1] + wr, [v2.r])
                        if neg_i:
                            dve(lambda e: e.scalar_tensor_tensor(out=out_i, in0=v1.t[:], scalar=-1.0, in1=v2.t[:], op0=ALU.mult, op1=ALU.subtract), [v1.r, v2.r], wr)
                        else:
                            dve(lambda e: e.tensor_tensor(out=out_i, in0=v1.t[:], in1=v2.t[:], op=ALU.add), [v1.r, v2.r], wr)

                    for s in range(8):
                        kk = (14 - s) if d == 0 else (s + 7)
                        cprod(V1.t[:, 0, :, s, :], V1.t[:, 1, :, s, :], bbv(0), bbv(1), kk, False, [BB.r, V1.r])
                        kz = s if d == 0 else (7 - s)
                        cprod(Z.t[:, 0, :, s, :], Z.t[:, 1, :, s, :], ctv(0), ctv(1), kz, True, [CT.r, Z.r])
                        kw = (s + 8) if d == 0 else (15 - s)
                        w3v = lambda ri: W3.t[:, ri, g0:g0 + 16, s * 16:(s + 1) * 16]
                        cprod(w3v(0), w3v(1), ctv(0), ctv(1), kw, True, [CT.r, W3.r])
                    for ri in range(2):
                        for gq in range(4):
                            ps = psA if (gq % 2 == 0) else psB
                            for gl in range(4):
                                gbl = gq * 4 + gl
                                t.op("pe", lambda e, ri=ri, gbl=gbl, gl=gl, ps=ps: e.transpose(
                                    out=ps.t[:, gl * 128:(gl + 1) * 128], in_=V1.t[:, ri, gbl, :, :].rearrange("q s p -> q (s p)"), identity=id32.t[:]),
                                    reads=[V1.r, id32.r], writes=[ps.r], inc=(gl == 3))
                            gb0 = g0 + gq * 4
                            dve(lambda e, ri=ri, gb0=gb0, ps=ps: e.tensor_copy(out=W1T.t[:, ri, gb0:gb0 + 4, :].rearrange("q a b -> q (a b)"), in_=ps.t[:]),
                                [ps.r], [W1T.r])
                    for gq in range(4):
                        for g2 in range(2):
                            ps = psA if g2 == 0 else psB
                            pr = slice(g2 * 64, (g2 + 1) * 64)
                            for gl in range(4):
                                gbl = gq * 4 + gl
                                for ri in range(2):
                                    t.op("pe", lambda e, ri=ri, gbl=gbl, pr=pr, gl=gl, ps=ps: e.matmul(
                                        out=ps.t[:, gl * 128:(gl + 1) * 128], lhsT=V1.t[pr, ri, gbl, :, :].rearrange("q s p -> q (s p)"),
                                        rhs=Z.t[pr, ri, gbl, :, :].rearrange("q s p -> q (s p)"), start=(ri == 0), stop=(ri == 1)),
                                        reads=[V1.r, Z.r], writes=[ps.r], inc=(ri == 1 and gl == 3))
                        for g2 in range(2):
                            ps = psA if g2 == 0 else psB
                            gg0 = 2 * (g0 + gq * 4) + g2
                            msk = (maskf if d == 0 else maskb).t[:].unsqueeze(1).broadcast_to([128, 4, 128])
                            psv = ps.t[:].rearrange("q (a b) -> q a b", a=4)
                            tav = TAC.t[:, 2 * (g0 + gq * 4):2 * (g0 + gq * 4) + 8, :].rearrange("q (a two) b -> q a two b", two=2)[:, :, g2, :]
                            if d == 0:
                                dve(lambda e, tav=tav, psv=psv, msk=msk: e.tensor_tensor(out=tav, in0=psv, in1=msk, op=ALU.mult),
                                    [ps.r, maskf.r], [TAC.r])
                            else:
                                dve(lambda e, psv=psv, msk=msk: e.tensor_tensor(out=tq.t[:], in0=psv, in1=msk, op=ALU.mult), [ps.r, maskb.r], [tq.r])
                                dve(lambda e, tav=tav: e.tensor_tensor(out=tav, in0=tav, in1=tq.t[:], op=ALU.add),
                                    [tq.r, TAC.r], [TAC.r])
                t.dma(STQ, SW["W1T"][:, d], W1T.t[:], reads=[W1T.r])
                t.dma(STQ, SW["W3"][:, d], W3.t[:], reads=[W3.r])
        t.barrier()
        for g in range(64):
            dve(lambda e, g=g: e.scalar_tensor_tensor(out=TOE.t[:, g, :], in0=id32.t[:], scalar=dcol.t[:, g:g + 1], in1=TAC.t[:, g, :], op0=ALU.mult, op1=ALU.add),
                [id32.r, dcol.r, TAC.r], [TOE.r])
        t.dma(STQ, SW["TOEP"], TOE.t[:], reads=[TOE.r])
    t.barrier()


def s5_scan_sweep(k, cst, x_in, W, seqs, SW, lam, YS, YG, d, stage):
    nc, t = k.nc, k.trk
    ident = cst["ident"]
    with ExitStack() as sw:
        w_u = k.sb(sw, "w_u", [128, 8, D], BF16)
        with ExitStack() as sst:
            stage = [k.sb(sst, f"stage{i}", [128, 2048], F32) for i in range(2)]
            load_w_bf16(k, stage, W["w_in"][:, 0:D], w_u, 8, D)
            t.barrier()
        W1T = k.sb(sw, "W1Ts", [128, 2, 32, 128], BF16)
        W3 = k.sb(sw, "W3s", [128, 2, 32, 128], BF16)
        t.dma("sp", W1T.t[:], SW["W1T"][:, d], writes=[W1T.r])
        t.dma("sp", W3.t[:], SW["W3"][:, d], writes=[W3.r])
        if d == 0:
            TOE = k.sb(sw, "TOEs", [128, 64, 128], BF16)
            t.dma("sp", TOE.t[:], SW["TOEP"], writes=[TOE.r])
        xq = k.sb(sw, "xq", [128, D], F32)
        xcb = k.sb(sw, "xcb", [128, D], BF16)
        xTp = k.sb(sw, "xTp", [128, 8, 128], BF16)
        ucq = k.sb(sw, "ucq", [128, 64, 4, 16], BF16)
        Us = [k.sb(sw, f"U{i}", [128, 64, 128], BF16) for i in range(2)]
        XHs = [k.sb(sw, f"XH{i}", [128, 130, 2, 32], F32) for i in range(2)]
        Hb = k.sb(sw, "Hb", [128, 32, 2, 130], BF16)
        P4s = [k.sb(sw, f"P4{i}", [128, 2, 2, 32], F32) for i in range(2)]
        t2ra = [Res(f"t2ra{i}") for i in range(2)]
        t2rb = [Res(f"t2rb{i}") for i in range(2)]
        carry = k.sb(sw, "carry", [128, 2, 32], F32)
        ycm = k.sb(sw, "ycm", [128, 8, 256], F32)
        if d == 1:
            yfh = k.sb(sw, "yfh", [128, 8, 256], F32)
            gt = yfh
            ygb = k.sb(sw, "ygb", [128, 8, 256], BF16)
        psT = [k.ps(sw, "psT0", [128, 8, 128], BF16)] * 2
        psU = [k.ps(sw, f"psU{i}", [128, 512], F32) for i in range(2)]
        psUTs = [k.ps(sw, f"psUT{i}", [128, 8, 128], BF16) for i in range(2)]
        psX = k.ps(sw, "psX", [128, 4, 128], F32)
        psY = [k.ps(sw, f"psY{i}", [128, 4, 128], F32) for i in range(2)]
        LA, LI_, LNI = lam["A"], lam["I"], lam["NI"]
        for XH in XHs:
            t.op("dve", lambda e, XH=XH: e.memset(XH.t[:], 0.0), reads=[], writes=[XH.r])
        xoff = 1 if d == 0 else 0
        cin = 0 if d == 0 else 128
        cout = 128 if d == 0 else 0
        hoff = 0 if d == 0 else 1

        segs = []
        for (row0, L) in seqs:
            nseg = L // 1024
            order = range(nseg) if d == 0 else range(nseg - 1, -1, -1)
            for si, s in enumerate(order):
                segs.append((row0 + s * 1024, si == 0))

        def stage_a(i):
            r0, _first = segs[i]
            U, XH = Us[i % 2], XHs[i % 2]
            xv = x_in[r0:r0 + 1024, :].rearrange("(c s) d -> c s d", s=8)
            for q in range(8):
                t.dma("sp", xq.t[:], xv[:, q, :], writes=[xq.r])
                t.op("act", lambda e: e.copy(out=xcb.t[:], in_=xq.t[:]), reads=[xq.r], writes=[xcb.r])
                pT = psT[q % 2]
                for kc in range(8):
                    t.op("pe", lambda e, kc=kc, pT=pT: e.transpose(out=pT.t[:, kc, :], in_=xcb.t[:, kc * 128:(kc + 1) * 128], identity=ident.t[:]),
                         reads=[xcb.r, ident.r], writes=[pT.r], inc=(kc == 7))
                t.op("act", lambda e, pT=pT: e.copy(out=xTp.t[:], in_=pT.t[:]), reads=[pT.r], writes=[xTp.r])
                for half in range(2):
                    pU = psU[half]
                    for kc in range(8):
                        t.op("pe", lambda e, kc=kc, half=half, pU=pU: e.matmul(out=pU.t[:], lhsT=xTp.t[:, kc, :], rhs=w_u.t[:, kc, half * 512:(half + 1) * 512],
                                                                               start=(kc == 0), stop=(kc == 7)), reads=[xTp.r, w_u.r], writes=[pU.r], inc=(kc == 7))
                    t.op("act", lambda e, half=half, pU=pU, q=q: e.copy(out=ucq.t[:, half * 32:(half + 1) * 32, q % 4, :],
                                                                         in_=pU.t[:].rearrange("q (g p) -> q g p", p=16)), reads=[pU.r], writes=[ucq.r])
                if q % 4 != 3:
                    continue
                hq = q // 4
                for gq in range(8):
                    psUT = psUTs[gq % 2]
                    for gl in range(8):
                        g = gq * 8 + gl
                        t.op("pe", lambda e, g=g, gl=gl, hq=hq, psUT=psUT: e.transpose(out=psUT.t[64 * hq:64 * hq + 64, gl, :], in_=ucq.t[:, g, :, :].rearrange("q s p -> q (s p)"), identity=ident.t[:]),
                             reads=[ucq.r, ident.r], writes=[psUT.r], inc=(gl == 7))
                    t.op("act", lambda e, gq=gq, hq=hq, U=U, psUT=psUT: e.copy(out=U.t[64 * hq:64 * hq + 64, gq * 8:gq * 8 + 8, :], in_=psUT.t[64 * hq:64 * hq + 64, :, :]),
                         reads=[psUT.r], writes=[U.r])
            for gp in range(16):
                for gl in range(2):
                    gb = gp * 2 + gl
                    for ri in range(2):
                        for g2 in range(2):
                            t.op("pe", lambda e, gb=gb, gl=gl, ri=ri, g2=g2, U=U: e.matmul(
                                out=psX.t[g2 * 64:(g2 + 1) * 64, gl * 2 + ri, :], lhsT=W1T.t[:, ri, gb, g2 * 64:(g2 + 1) * 64], rhs=U.t[:, 2 * gb + g2, :],
                                start=True, stop=True), reads=[W1T.r, U.r], writes=[psX.r], inc=(gl == 1 and ri == 1 and g2 == 1))
                t.op("act", lambda e, gp=gp, XH=XH: e.copy(
                    out=XH.t[:, xoff:xoff + 128, :, gp * 2:gp * 2 + 2].rearrange("q c r g -> q g r c"),
                    in_=psX.t[:].rearrange("q (g r) c -> q g r c", g=2)), reads=[psX.r], writes=[XH.r])

        def stage_b(i):
            r0, first = segs[i]
            U, XH = Us[i % 2], XHs[i % 2]
            XHp = XHs[(i + 1) % 2]
            if first:
                t.op("dve", lambda e: e.memset(XH.t[:, cin, :, :], 0.0), reads=[], writes=[XH.r])
            else:
                t.op("dve", lambda e: e.tensor_copy(out=XH.t[:, cin, :, :], in_=carry.t[:]), reads=[carry.r, XH.r], writes=[XH.r])
            steps = range(128) if d == 0 else range(127, -1, -1)
            for c in steps:
                cur = c + xoff
                prv = cur - 1 if d == 0 else cur + 1
                P = P4s[c % 2]
                t.op("dve", lambda e, prv=prv, P=P: e.tensor_tensor(out=P.t[:], in0=XH.t[:, prv, :, :].unsqueeze(2).broadcast_to([128, 2, 2, 32]),
                                                                     in1=lam["L4"].t[:, d, :, :, :], op=ALU.mult), reads=[XH.r, lam["L4"].r], writes=[P.r])
                t.op("dve", lambda e, cur=cur, P=P: e.tensor_tensor(out=XH.t[:, cur, :, :], in0=XH.t[:, cur, :, :], in1=P.t[:, 0, :, :], op=ALU.add), reads=[XH.r, P.r], writes=[XH.r])
                t.op("dve", lambda e, cur=cur, P=P: e.tensor_tensor(out=XH.t[:, cur, :, :], in0=XH.t[:, cur, :, :], in1=P.t[:, 1, ::-1, :], op=ALU.add), reads=[XH.r, P.r], writes=[XH.r])
            t.op("dve", lambda e: e.tensor_copy(out=carry.t[:], in_=XH.t[:, cout, :, :]), reads=[XH.r], writes=[carry.r])
            t.op("act", lambda e: e.copy(out=Hb.t[:], in_=XH.t[:].rearrange("q c r g -> q g r c")), reads=[XH.r], writes=[Hb.r])
            for qtr in range(4):
                ysl = YS[r0:r0 + 1024, :].rearrange("(c s) d -> c s d", s=8)[:, :, qtr * 256:(qtr + 1) * 256]
                if d == 1:
                    t.dma("sp", yfh.t[:], ysl, writes=[yfh.r])
                for gq in range(2):
                    for g2 in range(2):
                        pY = psY[g2]
                        pr = slice(g2 * 64, (g2 + 1) * 64)
                        for gl in range(4):
                            g = qtr * 16 + gq * 8 + 2 * gl + g2
                            gb = g // 2
                            if d == 0:
                                t.op("pe", lambda e, g=g, gl=gl, pY=pY: e.matmul(out=pY.t[:, gl, :], lhsT=U.t[:, g, :], rhs=TOE.t[:, g, :], start=True, stop=False),
                                     reads=[U.r, TOE.r], writes=[pY.r], inc=False)
                            for ri in range(2):
                                t.op("pe", lambda e, gb=gb, pr=pr, ri=ri, gl=gl, pY=pY: e.matmul(
                                    out=pY.t[:, gl, :], lhsT=Hb.t[pr, gb, ri, hoff:hoff + 128], rhs=W3.t[pr, ri, gb, :],
                                    start=(d == 1 and ri == 0), stop=(ri == 1)), reads=[Hb.r, W3.r], writes=[pY.r], inc=(ri == 1 and gl == 3))
                    for g2 in range(2):
                        pY = psY[g2]
                        oview = ycm.t[:, :, gq * 128:(gq + 1) * 128].rearrange("q t (g two p) -> q g two t p", g=4, two=2)[:, :, g2, :, :]
                        iview = pY.t[:].rearrange("q g (t p) -> q g t p", t=8)
                        if d == 0:
                            t.op("act", lambda e, oview=oview, iview=iview: e.copy(out=oview, in_=iview), reads=[pY.r], writes=[ycm.r])
                        else:
                            fview = yfh.t[:, :, gq * 128:(gq + 1) * 128].rearrange("q t (g two p) -> q g two t p", g=4, two=2)[:, :, g2, :, :]
                            t.op("dve", lambda e, oview=oview, iview=iview, fview=fview: e.tensor_tensor(out=oview, in0=iview, in1=fview, op=ALU.add),
                                 reads=[pY.r, yfh.r], writes=[ycm.r])
                if d == 0:
                    t.dma(STQ, ysl, ycm.t[:], reads=[ycm.r])
                else:
                    t.op("act", lambda e: e.activation(out=gt.t[:], in_=ycm.t[:], func=AF.Square), reads=[ycm.r], writes=[gt.r])
                    t.op("dve", lambda e: e.tensor_scalar(out=gt.t[:], in0=gt.t[:], scalar1=0.044715, scalar2=1.0, op0=ALU.mult, op1=ALU.add), reads=[gt.r], writes=[gt.r])
                    t.op("dve", lambda e: e.tensor_tensor(out=gt.t[:], in0=gt.t[:], in1=ycm.t[:], op=ALU.mult), reads=[gt.r, ycm.r], writes=[gt.r])
                    t.op("act", lambda e: e.activation(out=gt.t[:], in_=gt.t[:], func=AF.Sigmoid, scale=GELU_C), reads=[gt.r], writes=[gt.r])
                    t.op("dve", lambda e: e.tensor_tensor(out=ygb.t[:], in0=gt.t[:], in1=ycm.t[:], op=ALU.mult), reads=[gt.r, ycm.r], writes=[ygb.r])
                    t.dma(STQ, YG[r0:r0 + 1024, :].rearrange("(c s) d -> c s d", s=8)[:, :, qtr * 256:(qtr + 1) * 256], ygb.t[:], reads=[ygb.r])

        stage_a(0)
        for i in range(len(segs)):
            if i + 1 < len(segs):
                stage_a(i + 1)
            stage_b(i)
    t.barrier()


def s5_tail_sweep(k, cst, x_in, x_out, W, YG, T_total, stage):
    nc, t = k.nc, k.trk
    ident = cst["ident"]
    with ExitStack() as sw:
        w_g = k.sb(sw, "w_g", [128, 8, D], BF16)
        w_glu = k.sb(sw, "w_glu", [128, 8, D], BF16)
        w_o = k.sb(sw, "w_o", [128, 8, D], BF16)
        with ExitStack() as sst:
            stage = [k.sb(sst, f"stage{i}", [128, 2048], F32) for i in range(2)]
            load_w_bf16(k, stage, W["w_in"][:, D:2 * D], w_g, 8, D)
            load_w_bf16(k, stage, W["w_glu"], w_glu, 8, D)
            load_w_bf16(k, stage, W["w_out"], w_o, 8, D)
            t.barrier()
        bglu = k.sb(sw, "bglu", [128, 8], F32)
        with nc.allow_non_contiguous_dma(reason="tiny bias relayout"):
            t.dma("sp", bglu.t[:], W["b_glu"].rearrange("(c p) -> p c", p=128), writes=[bglu.r])
        lng = k.sb(sw, "lng", [128, D], F32)
        lnb = k.sb(sw, "lnb", [128, D], F32)
        t.dma("sp", lng.t[:], W["ln_g"].partition_broadcast(128), writes=[lng.r])
        t.dma("sp", lnb.t[:], W["ln_b"].partition_broadcast(128), writes=[lnb.r])
        xi = k.sb(sw, "xi", [128, 4, D], F32)
        xb = k.sb(sw, "xb", [128, 4, D], BF16)
        yi = k.sb(sw, "yi", [128, 4, D], BF16)
        xT = k.sb(sw, "xT", [128, 8, 512], BF16)
        yT = k.sb(sw, "yT", [128, 8, 512], BF16)
        sgms = [k.sb(sw, f"sgm{i}", [128, 512], F32) for i in range(2)]
        sils = [k.sb(sw, f"sil{i}", [128, 512], F32) for i in range(2)]
        y3T = k.sb(sw, "y3T", [128, 8, 512], BF16)
        yos = [k.sb(sw, f"yo{i}", [128, 4, D], F32) for i in range(2)]
        zs, bsts, mvs = ln_scratch(k, sw)
        psT = [k.ps(sw, f"psT{i}", [128, 2, 512], BF16) for i in range(2)]
        psZs = [k.ps(sw, f"psZ{i}", [128, 512], F32) for i in range(2)]
        psGs = [k.ps(sw, f"psG{i}", [128, 512], F32) for i in range(2)]
        psY = [k.ps(sw, "psY0", [128, 1024], F32)] * 2
        for b in range(T_total // 512):
            r0 = b * 512
            yo = yos[b % 2]
            t.dma("sp", xi.t[:], x_in[r0:r0 + 512, :].rearrange("(j p) d -> p j d", p=128), writes=[xi.r])
            t.dma("sp", yi.t[:], YG[r0:r0 + 512, :].rearrange("(j p) d -> p j d", p=128), writes=[yi.r])
            t.op("act", lambda e: e.copy(out=xb.t[:], in_=xi.t[:]), reads=[xi.r], writes=[xb.r])
            for (src, dstT) in ((xb, xT), (yi, yT)):
                for kp in range(4):
                    pT = psT[kp % 2]
                    for kk in range(2):
                        kc = kp * 2 + kk
                        for j in range(4):
                            t.op("pe", lambda e, kc=kc, kk=kk, j=j, pT=pT, src=src: e.transpose(
                                out=pT.t[:, kk, j * 128:(j + 1) * 128], in_=src.t[:, j, kc * 128:(kc + 1) * 128], identity=ident.t[:]),
                                reads=[src.r, ident.r], writes=[pT.r], inc=(kk == 1 and j == 3))
                    t.op("dve", lambda e, kp=kp, pT=pT, dstT=dstT: e.tensor_copy(out=dstT.t[:, kp * 2:kp * 2 + 2, :], in_=pT.t[:]), reads=[pT.r], writes=[dstT.r])
            for co in range(8):
                psZ, psG, sgm, sil = psZs[co % 2], psGs[co % 2], sgms[co % 2], sils[co % 2]
                for kc in range(8):
                    t.op("pe", lambda e, kc=kc, co=co: e.matmul(out=psZ.t[:], lhsT=w_glu.t[:, kc, co * 128:(co + 1) * 128], rhs=yT.t[:, kc, :], start=(kc == 0), stop=(kc == 7)),
                         reads=[w_glu.r, yT.r], writes=[psZ.r], inc=(kc == 7))
                for kc in range(8):
                    t.op("pe", lambda e, kc=kc, co=co: e.matmul(out=psG.t[:], lhsT=w_g.t[:, kc, co * 128:(co + 1) * 128], rhs=xT.t[:, kc, :], start=(kc == 0), stop=(kc == 7)),
                         reads=[w_g.r, xT.r], writes=[psG.r], inc=(kc == 7))
                t.op("act", lambda e, co=co: e.activation(out=sgm.t[:], in_=psZ.t[:], func=AF.Sigmoid, bias=bglu.t[:, co:co + 1]), reads=[psZ.r, bglu.r], writes=[sgm.r])
                t.op("act", lambda e: e.activation(out=sil.t[:], in_=psG.t[:], func=AF.Silu), reads=[psG.r], writes=[sil.r])
                t.op("dve", lambda e: e.tensor_tensor(out=sgm.t[:], in0=sgm.t[:], in1=sil.t[:], op=ALU.mult), reads=[sgm.r, sil.r], writes=[sgm.r])
                t.op("dve", lambda e, co=co: e.tensor_tensor(out=y3T.t[:, co, :], in0=sgm.t[:], in1=yT.t[:, co, :], op=ALU.mult), reads=[sgm.r, yT.r], writes=[y3T.r])
            for j in range(4):
                pY = psY[j % 2]
                for half in range(2):
                    for kc in range(8):
                        t.op("pe", lambda e, kc=kc, j=j, half=half, pY=pY: e.matmul(
                            out=pY.t[:, half * 512:(half + 1) * 512], lhsT=y3T.t[:, kc, j * 128:(j + 1) * 128], rhs=w_o.t[:, kc, half * 512:(half + 1) * 512],
                            start=(kc == 0), stop=(kc == 7)), reads=[y3T.r, w_o.r], writes=[pY.r], inc=(kc == 7 and half == 1))
                ln_tail(k, xi.t[:, j, :], xi.r, pY, zs, bsts, mvs, lng, lnb, yo.t[:, j, :], yo.r, j)
            t.dma(STQ, x_out[r0:r0 + 512, :].rearrange("(j p) d -> p j d", p=128), yo.t[:], reads=[yo.r])
    t.barrier()


def s5_layer(k, cst, x_in, x_out, W, seqs, S5S):
    nc, t = k.nc, k.trk
    T_total = sum(L for _, L in seqs)
    with ExitStack() as ls:
        lam = {"A": k.sb(ls, "lamA", [128, 2, 2, 32], F32), "I": k.sb(ls, "lamI", [128, 2, 32], F32), "NI": k.sb(ls, "lamNI", [128, 2, 32], F32),
               "B": k.sb(ls, "lamB", [128, 2, 2, 32], F32), "L4": k.sb(ls, "lamL4", [128, 2, 2, 2, 32], F32)}
        stage = None
        s5_setup(k, cst, W, S5S, lam)
        if DBG_STOP in (21, 31, 32, 33, 34):
            return
        s5_scan_sweep(k, cst, x_in, W, seqs, S5S, lam, S5S["YS"], S5S["YG"], 0, stage)
        if DBG_STOP == 22:
            return
        s5_scan_sweep(k, cst, x_in, W, seqs, S5S, lam, S5S["YS"], S5S["YG"], 1, stage)
        if DBG_STOP == 23:
            return
        s5_tail_sweep(k, cst, x_in, x_out, W, S5S["YG"], T_total, stage)


def build_program(seqs, layers, ntile_rope):
    T_total = sum(L for _, L in seqs)
    nc = bass.Bass("TRN2", target_bir_lowering=False)
    dt = nc.dram_tensor
    x = dt("x", [T_total, D], F32, kind="ExternalInput").ap()
    y = dt("y", [T_total, D], F32, kind="ExternalOutput").ap()
    n_mla = sum(1 for l in layers if l == "mla")
    n_s5 = sum(1 for l in layers if l == "s5")
    Wd = {}
    if n_mla:
        Wd["mla_w_in"] = dt("mla_w_in", [n_mla, D, MLA_IN], F32, kind="ExternalInput").ap()
        Wd["mla_g_q"] = dt("mla_g_q", [n_mla, QLORA], F32, kind="ExternalInput").ap()
        Wd["mla_w_q_up"] = dt("mla_w_q_up", [n_mla, QLORA, 1536], F32, kind="ExternalInput").ap()
        Wd["mla_g_kv"] = dt("mla_g_kv", [n_mla, KVLORA], F32, kind="ExternalInput").ap()
        Wd["mla_w_kv_up"] = dt("mla_w_kv_up", [n_mla, KVLORA, 2048], F32, kind="ExternalInput").ap()
        Wd["mla_w_out"] = dt("mla_w_out", [n_mla, D, D], F32, kind="ExternalInput").ap()
    if n_s5:
        for nm, shp in (("s5_w_in", [D, 2 * D]), ("s5_a_re", [2, 64, 64]), ("s5_a_im", [2, 64, 64]), ("s5_log_step", [2, 64]),
                        ("s5_b_re", [2, 64, 64, 16]), ("s5_b_im", [2, 64, 64, 16]), ("s5_c_re", [2, 64, 16, 64]), ("s5_c_im", [2, 64, 16, 64]),
                        ("s5_d", [D]), ("s5_w_glu", [D, D]), ("s5_b_glu", [D]), ("s5_w_out", [D, D])):
            Wd[nm] = dt(nm, [n_s5] + shp, F32, kind="ExternalInput").ap()
    Wd["ln_g"] = dt("ln_g", [len(layers), D], F32, kind="ExternalInput").ap()
    Wd["ln_b"] = dt("ln_b", [len(layers), D], F32, kind="ExternalInput").ap()
    c_ident = dt("c_ident", [128, 128], BF16, kind="ExternalInput").ap()
    c_cos = dt("c_cos", [128, ntile_rope, 64], F32, kind="ExternalInput").ap()
    c_sin = dt("c_sin", [128, ntile_rope, 64], F32, kind="ExternalInput").ap()
    c_id32 = dt("c_id32", [128, 128], F32, kind="ExternalInput").ap()
    c_maskf = dt("c_maskf", [128, 128], F32, kind="ExternalInput").ap()
    c_maskb = dt("c_maskb", [128, 128], F32, kind="ExternalInput").ap()
    S5S = {
        "W1T": dt("s5s_w1t", [128, 2, 2, 32, 128], BF16, kind="Internal").ap(),
        "W3": dt("s5s_w3", [128, 2, 2, 32, 128], BF16, kind="Internal").ap(),
        "TOEP": dt("s5s_toep", [128, 64, 128], BF16, kind="Internal").ap(),
        "YS": dt("s5s_ys", [T_total, D], F32, kind="Internal").ap(),
        "YG": dt("s5s_yg", [T_total, D], BF16, kind="Internal").ap(),
    }
    xa = dt("xa", [T_total, D], F32, kind="Internal").ap()
    xb_ = dt("xb", [T_total, D], F32, kind="Internal").ap()
    scr = {
        "SG": dt("scr_sg", [D, T_total], BF16, kind="Internal").ap(),
        "OG": dt("scr_og", [D, T_total], BF16, kind="Internal").ap(),
        "CQ": dt("scr_cq", [QLORA, T_total], BF16, kind="Internal").ap(),
    }

    with ExitStack() as st:
        k = K(nc, st)
        t = k.trk
        cst = {}
        cst["ident"] = k.sb(st, "ident", [128, 128], BF16)
        cst["ones"] = k.sb(st, "ones", [128, 128], BF16)
        cst["c_cos"], cst["c_sin"], cst["ntile_rope"] = c_cos, c_sin, ntile_rope
        for nm, src in (("id32", c_id32), ("maskf", c_maskf), ("maskb", c_maskb)):
            cst[nm] = k.sb(st, nm, [128, 128], F32)
            t.dma("sp", cst[nm].t[:], src, writes=[cst[nm].r])
        k.eps_ln = k.sb(st, "eps_ln", [128, 1], F32)
        t.dma("sp", cst["ident"].t[:], c_ident, writes=[cst["ident"].r])
        t.op("dve", lambda e: e.memset(cst["ones"].t[:], 1.0), writes=[cst["ones"].r])
        cst["ones32"] = k.sb(st, "ones32", [128, 128], F32)
        t.op("dve", lambda e: e.memset(cst["ones32"].t[:], 1.0), writes=[cst["ones32"].r])
        t.op("dve", lambda e: e.memset(k.eps_ln.t[:], LN_EPS), writes=[k.eps_ln.r])

        cur = x
        i_mla = i_s5 = 0
        for li, lt in enumerate(layers):
            dst = y if li == len(layers) - 1 else (xa if li % 2 == 0 else xb_)
            if lt == "mla":
                W = {"w_in": Wd["mla_w_in"][i_mla], "g_q": Wd["mla_g_q"][i_mla], "w_q_up": Wd["mla_w_q_up"][i_mla],
                     "g_kv": Wd["mla_g_kv"][i_mla], "w_kv_up": Wd["mla_w_kv_up"][i_mla], "w_out": Wd["mla_w_out"][i_mla],
                     "ln_g": Wd["ln_g"][li], "ln_b": Wd["ln_b"][li]}
                mla_layer(k, cst, cur, dst, W, seqs, scr)
                i_mla += 1
            else:
                W = {n[3:]: Wd[n][i_s5] for n in Wd if n.startswith("s5_")}
                W["ln_g"], W["ln_b"] = Wd["ln_g"][li], Wd["ln_b"][li]
                s5_layer(k, cst, cur, dst, W, seqs, S5S)
                i_s5 += 1
            cur = dst
        t.barrier(("sp",))
    return nc


def s5_consts():
    sp = np.arange(128) // 16
    maskf = (sp[None, :] >= sp[:, None]).astype(np.float32)
    maskb = (sp[:, None] >= sp[None, :]).astype(np.float32)
    return np.eye(128, dtype=np.float32), maskf, maskb


def rope_consts(ntile):
    inv = (10000.0 ** (-np.arange(0, 64, 2, dtype=np.float32) / np.float32(64))).astype(np.float32)
    pos = np.arange(ntile * 128, dtype=np.float32)
    ang = (pos[:, None] * inv[None, :]).astype(np.float32)
    cos, sin = np.cos(ang).astype(np.float32), np.sin(ang).astype(np.float32)
    cos2 = np.concatenate([cos, cos], axis=1)
    sinpm = np.concatenate([-sin, sin], axis=1)
    f = lambda a: np.ascontiguousarray(a.reshape(ntile, 128, 64).transpose(1, 0, 2))
    return f(cos2), f(sinpm)


SEQ_P, SEQ_S = 8192, 4096
LAYERS = ["mla", "s5", "mla", "s5"]
_PROG = {}


def kernel(x_prompt, x_sample, mla_w_in, mla_g_q, mla_w_q_up, mla_g_kv, mla_w_kv_up, mla_w_out,
           s5_w_in, s5_a_re, s5_a_im, s5_log_step, s5_b_re, s5_b_im, s5_c_re, s5_c_im, s5_d,
           s5_w_glu, s5_b_glu, s5_w_out, ln_g, ln_b):
    import ml_dtypes
    f32 = lambda a: np.ascontiguousarray(np.asarray(a, dtype=np.float32))
    x_prompt, x_sample = f32(x_prompt), f32(x_sample)
    seqs = [(0, SEQ_P), (SEQ_P, SEQ_S), (SEQ_P + SEQ_S, SEQ_S)]
    ntile = SEQ_P // 128
    if "nc" not in _PROG:
        _PROG["nc"] = build_program(seqs, LAYERS, ntile)
    nc = _PROG["nc"]
    cos2, sinpm = rope_consts(ntile)
    i32, mf, mb = s5_consts()
    shared = {
        "mla_w_in": f32(mla_w_in), "mla_g_q": f32(mla_g_q), "mla_w_q_up": f32(mla_w_q_up), "mla_g_kv": f32(mla_g_kv),
        "mla_w_kv_up": f32(mla_w_kv_up), "mla_w_out": f32(mla_w_out),
        "s5_w_in": f32(s5_w_in), "s5_a_re": f32(s5_a_re), "s5_a_im": f32(s5_a_im), "s5_log_step": f32(s5_log_step),
        "s5_b_re": f32(s5_b_re), "s5_b_im": f32(s5_b_im), "s5_c_re": f32(s5_c_re), "s5_c_im": f32(s5_c_im), "s5_d": f32(s5_d),
        "s5_w_glu": f32(s5_w_glu), "s5_b_glu": f32(s5_b_glu), "s5_w_out": f32(s5_w_out),
        "ln_g": f32(ln_g), "ln_b": f32(ln_b),
        "c_ident": np.eye(128, dtype=ml_dtypes.bfloat16), "c_cos": cos2, "c_sin": sinpm,
        "c_id32": i32, "c_maskf": mf, "c_maskb": mb,
    }
    in_maps = []
    for c in range(NCORES):
        xc = np.concatenate([x_prompt[c], x_sample[2 * c], x_sample[2 * c + 1]], axis=0)
        m = dict(shared)
        m["x"] = np.ascontiguousarray(xc)
        in_maps.append(m)
    res = run_bass_kernel_spmd(nc, in_maps, core_ids=list(range(NCORES)))
    y_p = np.empty_like(x_prompt)
    y_s = np.empty_like(x_sample)
    for c in range(NCORES):
        yc = np.asarray(res.results[c]["y"], dtype=np.float32)
        y_p[c] = yc[0:SEQ_P]
        y_s[2 * c] = yc[SEQ_P:SEQ_P + SEQ_S]
        y_s[2 * c + 1] = yc[SEQ_P + SEQ_S:]
    return (y_p, y_s)
```
